# Optimizing a Trainium2 kernel written in Bass

```python
import jax, jax.numpy as jnp
from jax import lax
import numpy as np

D_MODEL = 1024
BATCH = 16
SEQ = 2048
DEPTH = 4
DEC_BATCH = 8
DEC_SEQ = 16
PAST_LEN = 4096

CHUNK = 64
GMLP_CHUNK = 128
Q_BLOCK = 128
G_A = 4
DG_A = 64
W_A = G_A * DG_A
H_B = 4
NOPE_DIM = 128
ROPE_DIM = 64
V_DIM = 128
W_B = H_B * V_DIM
Q_LORA = 384
KV_LORA = 256
ROPE_THETA = 10000.0
MLA_SCALE = (NOPE_DIM + ROPE_DIM) ** -0.5
H_C = 4
D_C = 64
W_C = H_C * D_C
SB_SCALE = D_C ** -0.5
MIX_WIDTH = W_A + W_B + W_C
W_IN_COLS = 2 * W_A + Q_LORA + KV_LORA + ROPE_DIM + 3 * W_C
D_FF = 4 * D_MODEL
ALPHA = (2 * DEPTH) ** 0.25
BETA = (8 * DEPTH) ** -0.25
EPS = 1e-5

kernel_name = 'hybrid_gmlp_mla_stickbreak_stream_step'


def in_split_points():
    sizes = (W_A, W_A, Q_LORA, KV_LORA, ROPE_DIM, W_C, W_C)
    return [int(v) for v in np.cumsum(sizes)]


def layer_norm(x, g=None, b=None):
    xf = x.astype(jnp.float32)
    mu = xf.mean(-1, keepdims=True)
    var = jnp.square(xf - mu).mean(-1, keepdims=True)
    y = (xf - mu) * lax.rsqrt(var + EPS)
    if g is not None:
        y = y * g.astype(jnp.float32) + b.astype(jnp.float32)
    return y.astype(x.dtype)


def rms_norm(x, g):
    xf = x.astype(jnp.float32)
    y = xf * lax.rsqrt(jnp.mean(jnp.square(xf), -1, keepdims=True) + EPS) * g.astype(jnp.float32)
    return y.astype(x.dtype)


def rope(x, pos):
    half = ROPE_DIM // 2
    inv = ROPE_THETA ** (-jnp.arange(half, dtype=jnp.float32) / half)
    ang = pos.astype(jnp.float32)[:, None] * inv[None, :]
    shape = (1, pos.shape[0]) + (1,) * (x.ndim - 3) + (half,)
    cos = jnp.cos(ang).reshape(shape)
    sin = jnp.sin(ang).reshape(shape)
    xf = x.astype(jnp.float32)
    x1, x2 = xf[..., :half], xf[..., half:]
    return jnp.concatenate([x1 * cos - x2 * sin, x2 * cos + x1 * sin], -1).astype(x.dtype)


def chunk_visible(q_pos, k_pos):
    return (k_pos // CHUNK)[None, :] <= (q_pos // CHUNK)[:, None]


def spatial_gate(u, v, w_s, b_s):
    b, s = u.shape[:2]
    c = min(s, GMLP_CHUNK)
    n_c = s // c
    idx = jnp.arange(c)
    w = jnp.where(chunk_visible(idx, idx)[None], w_s[:, :c, :c], 0)
    vc = v.reshape(b, n_c, c, G_A, DG_A)
    mixed = jnp.einsum('gij,bnjgd->bnigd', w, vc) + b_s[:, :c].T[None, None, :, :, None]
    return u * mixed.reshape(b, s, G_A, DG_A)


def mla_attend(q_pos, q_lat, q_rope, ckv, krope, k_pos):
    scores = (jnp.einsum('bqhc,bkc->bhqk', q_lat, ckv)
              + jnp.einsum('bqhr,bkr->bhqk', q_rope, krope)).astype(jnp.float32) * MLA_SCALE
    scores = jnp.where(chunk_visible(q_pos, k_pos)[None, None], scores, -jnp.inf)
    p = jax.nn.softmax(scores, axis=-1).astype(ckv.dtype)
    return jnp.einsum('bhqk,bkc->bqhc', p, ckv)


def sb_attend(q_pos, q, k, v, k_pos):
    z = jnp.einsum('bqhd,bkhd->bhqk', q, k).astype(jnp.float32) * SB_SCALE
    mask = (k_pos[None, :] < q_pos[:, None])[None, None]
    log_skip = jnp.where(mask, jax.nn.log_sigmoid(-z), 0.0)
    suffix = lax.cumsum(log_skip, axis=3, reverse=True) - log_skip
    w = jnp.where(mask, jnp.exp(jax.nn.log_sigmoid(z) + suffix), 0.0)
    return jnp.einsum('bhqk,bkhd->bqhd', w.astype(v.dtype), v)


def sweep_query_blocks(attend, q_pos, *q_args):
    nq = q_pos.shape[0]
    if nq <= Q_BLOCK:
        return attend(q_pos, *q_args)
    nb = nq // Q_BLOCK
    blocked = tuple(jnp.moveaxis(a.reshape((a.shape[0], nb, Q_BLOCK) + a.shape[2:]), 1, 0) for a in q_args)
    out = lax.map(lambda xs: attend(xs[0], *xs[1]), (q_pos.reshape(nb, Q_BLOCK), blocked))
    out = jnp.moveaxis(out, 0, 1)
    return out.reshape((out.shape[0], nq) + out.shape[3:])


def trunk_layer(x, q_pos, past, w_in, w_s, b_s, g_cq, g_ckv, w_uq, w_uk, w_uv, g_mix, w_out,
                ln1_g, ln1_b, w_up, b_up, w_down, b_down, ln2_g, ln2_b):
    b, s, _ = x.shape
    p = x @ w_in
    a_u, a_v, b_cq, b_ckv, b_kr, c_q, c_k, c_v = jnp.split(p, in_split_points(), axis=-1)

    u = jax.nn.gelu(a_u).reshape(b, s, G_A, DG_A)
    v = layer_norm(jax.nn.gelu(a_v).reshape(b, s, G_A, DG_A))
    y_a = spatial_gate(u, v, w_s, b_s).reshape(b, s, W_A)

    q = jnp.einsum('bsc,chd->bshd', rms_norm(b_cq, g_cq), w_uq)
    q_rope = rope(q[..., NOPE_DIM:], q_pos)
    q_lat = jnp.einsum('bshn,hcn->bshc', q[..., :NOPE_DIM], w_uk)
    ckv = rms_norm(b_ckv, g_ckv)
    krope = rope(b_kr, q_pos)

    qc = c_q.reshape(b, s, H_C, D_C)
    kc = c_k.reshape(b, s, H_C, D_C)
    vc = c_v.reshape(b, s, H_C, D_C)

    if past is None:
        ckv_all, kr_all, k_all, v_all, k_pos = ckv, krope, kc, vc, q_pos
    else:
        ckv_p, kr_p, k_p, v_p = past
        ckv_all = jnp.concatenate([ckv_p, ckv], 1)
        kr_all = jnp.concatenate([kr_p, krope], 1)
        k_all = jnp.concatenate([k_p, kc], 1)
        v_all = jnp.concatenate([v_p, vc], 1)
        k_pos = jnp.concatenate([jnp.arange(ckv_p.shape[1], dtype=jnp.int32), q_pos])

    out_lat = sweep_query_blocks(lambda qp, ql, qr: mla_attend(qp, ql, qr, ckv_all, kr_all, k_pos),
                                 q_pos, q_lat, q_rope)
    y_b = jnp.einsum('bshc,hcd->bshd', out_lat, w_uv).reshape(b, s, W_B)
    y_c = sweep_query_blocks(lambda qp, qq: sb_attend(qp, qq, k_all, v_all, k_pos),
                             q_pos, qc).reshape(b, s, W_C)

    g_a, g_b, g_c = jnp.split(g_mix, [W_A, W_A + W_B])
    y = jnp.concatenate([rms_norm(y_a, g_a), rms_norm(y_b, g_b), rms_norm(y_c, g_c)], -1)
    x = layer_norm(ALPHA * x + y @ w_out, ln1_g, ln1_b)

    h = jnp.square(jax.nn.relu(x @ w_up + b_up)) @ w_down + b_down
    x = layer_norm(ALPHA * x + h, ln2_g, ln2_b)
    return x, (ckv, krope, kc, vc, v)


def setup_inputs(seed: int = 0) -> dict:
    key = jax.random.key(seed)
    ks = jax.random.split(key, 26)

    def nrm(k, shape, scale=1.0):
        return jax.random.normal(k, shape, jnp.float32) * scale

    return {
        'x_prompt': nrm(ks[0], (BATCH, SEQ, D_MODEL)),
        'x_sample': nrm(ks[1], (DEC_BATCH, DEC_SEQ, D_MODEL)),
        'cache_mla_ckv': nrm(ks[2], (DEPTH, DEC_BATCH, PAST_LEN, KV_LORA)),
        'cache_mla_krope': nrm(ks[3], (DEPTH, DEC_BATCH, PAST_LEN, ROPE_DIM)),
        'cache_sb_k': nrm(ks[4], (DEPTH, DEC_BATCH, PAST_LEN, H_C, D_C)),
        'cache_sb_v': nrm(ks[5], (DEPTH, DEC_BATCH, PAST_LEN, H_C, D_C)),
        'w_in': nrm(ks[6], (DEPTH, D_MODEL, W_IN_COLS), D_MODEL ** -0.5),
        'w_s': nrm(ks[7], (DEPTH, G_A, GMLP_CHUNK, GMLP_CHUNK), GMLP_CHUNK ** -0.5),
        'b_s': 1.0 + nrm(ks[8], (DEPTH, G_A, GMLP_CHUNK), 0.1),
        'g_cq': 1.0 + nrm(ks[9], (DEPTH, Q_LORA), 0.1),
        'g_ckv': 1.0 + nrm(ks[10], (DEPTH, KV_LORA), 0.1),
        'w_uq': nrm(ks[11], (DEPTH, Q_LORA, H_B, NOPE_DIM + ROPE_DIM), Q_LORA ** -0.5),
        'w_uk': nrm(ks[12], (DEPTH, H_B, KV_LORA, NOPE_DIM), KV_LORA ** -0.5),
        'w_uv': nrm(ks[13], (DEPTH, H_B, KV_LORA, V_DIM), KV_LORA ** -0.5),
        'g_mix': 1.0 + nrm(ks[14], (DEPTH, MIX_WIDTH), 0.1),
        'w_out': nrm(ks[15], (DEPTH, MIX_WIDTH, D_MODEL), MIX_WIDTH ** -0.5 * BETA),
        'ln1_g': 1.0 + nrm(ks[16], (DEPTH, D_MODEL), 0.1),
        'ln1_b': nrm(ks[17], (DEPTH, D_MODEL), 0.02),
        'w_up': nrm(ks[18], (DEPTH, D_MODEL, D_FF), D_MODEL ** -0.5 * BETA),
        'b_up': nrm(ks[19], (DEPTH, D_FF), 0.02),
        'w_down': nrm(ks[20], (DEPTH, D_FF, D_MODEL), D_FF ** -0.5 * BETA),
        'b_down': nrm(ks[21], (DEPTH, D_MODEL), 0.02),
        'ln2_g': 1.0 + nrm(ks[22], (DEPTH, D_MODEL), 0.1),
        'ln2_b': nrm(ks[23], (DEPTH, D_MODEL), 0.02),
    }


def reference(x_prompt, x_sample, cache_mla_ckv, cache_mla_krope, cache_sb_k, cache_sb_v,
              w_in, w_s, b_s, g_cq, g_ckv, w_uq, w_uk, w_uv, g_mix, w_out,
              ln1_g, ln1_b, w_up, b_up, w_down, b_down, ln2_g, ln2_b):
    past_len = cache_mla_ckv.shape[2]
    pos_p = jnp.arange(x_prompt.shape[1], dtype=jnp.int32)
    pos_s = past_len + jnp.arange(x_sample.shape[1], dtype=jnp.int32)
    yp, ys = x_prompt, x_sample
    ckv_p, kr_p, k_p, v_p = [], [], [], []
    ckv_s, kr_s, k_s, v_s, gv_s = [], [], [], [], []
    for l in range(DEPTH):
        params = (w_in[l], w_s[l], b_s[l], g_cq[l], g_ckv[l], w_uq[l], w_uk[l], w_uv[l], g_mix[l],
                  w_out[l], ln1_g[l], ln1_b[l], w_up[l], b_up[l], w_down[l], b_down[l],
                  ln2_g[l], ln2_b[l])
        yp, st_p = trunk_layer(yp, pos_p, None, *params)
        ys, st_s = trunk_layer(ys, pos_s, (cache_mla_ckv[l], cache_mla_krope[l],
                                           cache_sb_k[l], cache_sb_v[l]), *params)
        ckv_p.append(st_p[0]); kr_p.append(st_p[1]); k_p.append(st_p[2]); v_p.append(st_p[3])
        ckv_s.append(st_s[0]); kr_s.append(st_s[1]); k_s.append(st_s[2]); v_s.append(st_s[3])
        gv_s.append(st_s[4])
    return (yp, ys,
            jnp.stack(ckv_p), jnp.stack(kr_p), jnp.stack(k_p), jnp.stack(v_p),
            jnp.stack(ckv_s), jnp.stack(kr_s), jnp.stack(k_s), jnp.stack(v_s), jnp.stack(gv_s))
```

```python
import os
import numpy as np
from contextlib import ExitStack
import concourse.bass as bass
import concourse.mybir as mybir
from concourse.bass_utils import run_bass_kernel_spmd

F32 = mybir.dt.float32
BF16 = mybir.dt.bfloat16
AF = mybir.ActivationFunctionType
ALU = mybir.AluOpType
AX = mybir.AxisListType

D = 1024
NLAYERS = 4
SEQ = 2048
PAST = 4096
NS = 16
DFF = 4096
WIN = 1984
ALPHA = float((2 * 4) ** 0.25)
EPS = 1e-5
MLA_SCALE = float(192 ** -0.5)
SB_SCALE = float(64 ** -0.5)
N_CORES = 8


class Res:
    __slots__ = ("name", "w", "r")

    def __init__(self, name=""):
        self.name = name
        self.w = None
        self.r = {}


class Prog:
    ENG = ("pe", "act", "dve", "pool", "sp")
    EPOCH = 30000

    def __init__(self, nc, same_engine_sync=True):
        self.nc = nc
        self.ops = {e: [] for e in self.ENG}
        self.dma_cnt = {}
        self.dma_cnt_raw = {}
        self.dma_maxwait = {}
        self.same_engine_sync = same_engine_sync
        self.nrec = 0
        self.limit = int(os.environ.get("K_LIMIT", "0")) or None
        self.marks = []
        self.trace_lines = bool(os.environ.get("K_TRACE"))

    def _collect(self, eng, reads, writes):
        toks = []
        for r in reads:
            if r.w is not None:
                toks.append(r.w)
        for w in writes:
            if w.w is not None:
                toks.append(w.w)
            toks.extend(w.r.values())
        waits = []
        for t in toks:
            if t[0] == 'e':
                if t[1] == eng and (eng == 'pe' or not self.same_engine_sync):
                    continue
                self.ops[t[1]][t[2]][2] = True
                waits.append(t)
            else:
                v = self.dma_cnt[t[1]] * 16
                waits.append(('d', t[1], v))
                if self.dma_maxwait.get(t[1], 0) < v:
                    self.dma_maxwait[t[1]] = v
        return waits

    def mark(self, name):
        self.marks.append((name, self.nrec, len(self.ops['pe'])))

    def op(self, eng, fn, reads=(), writes=()):
        self.nrec += 1
        if self.trace_lines:
            import sys as _s
            f = _s._getframe(1)
            ln = []
            while f is not None and len(ln) < 3:
                ln.append(f.f_lineno); f = f.f_back
            print("OP", self.nrec, eng, ln)
        if self.limit is not None and self.nrec > self.limit:
            return None
        waits = self._collect(eng, reads, writes)
        idx = len(self.ops[eng])
        self.ops[eng].append([fn, waits, False, None])
        tok = ('e', eng, idx)
        for r in reads:
            r.r[eng] = tok
        for w in writes:
            w.w = tok
            w.r = {}
        return tok

    def dma(self, q, fn, sem, reads=(), writes=()):
        self.nrec += 1
        if self.trace_lines:
            import sys as _s
            f = _s._getframe(1)
            ln = []
            while f is not None and len(ln) < 3:
                ln.append(f.f_lineno); f = f.f_back
            print("OP", self.nrec, "dma:" + sem, ln)
        if self.limit is not None and self.nrec > self.limit:
            return None
        sem = f"{sem}_{self.dma_cnt_raw.get(sem, 0) // 1500}"
        base = sem.rsplit("_", 1)[0]
        self.dma_cnt_raw[base] = self.dma_cnt_raw.get(base, 0) + 1
        waits = self._collect(q, reads, writes)
        if self.dma_maxwait.get(sem, 0) > 0:
            waits.append(('d', sem, self.dma_maxwait[sem]))
        self.ops[q].append([fn, waits, False, sem])
        self.dma_cnt[sem] = self.dma_cnt.get(sem, 0) + 1
        tok = ('d', sem, self.dma_cnt[sem] * 16)
        for r in reads:
            r.r['d' + sem] = tok
        for w in writes:
            w.w = tok
            w.r = {}
        return tok

    def emit(self, stack):
        nc = self.nc
        E = self.EPOCH
        sig = {}
        nsig = {}
        for e in self.ENG:
            n = 0
            for i, o in enumerate(self.ops[e]):
                if o[2]:
                    n += 1
                    sig[(e, i)] = n
            nsig[e] = n
        semh = {}
        for e in self.ENG:
            for k in range((nsig[e] + E - 1) // E):
                semh[('e', e, k)] = stack.enter_context(nc.semaphore(f"s_{e}_{k}"))
        for name in self.dma_cnt:
            semh[('d', name)] = stack.enter_context(nc.semaphore(f"d_{name}"))
        block = stack.enter_context(nc.Block())
        ops = self.ops
        dma_cnt = self.dma_cnt

        def run(e, eng):
            waited = {}
            for i, (fn, waits, signal, dsem) in enumerate(ops[e]):
                need = {}
                for t in waits:
                    if t[0] == 'e':
                        n = sig[(t[1], t[2])]
                        key = ('e', t[1], (n - 1) // E)
                        val = (n - 1) % E + 1
                    else:
                        key = ('d', t[1])
                        val = t[2]
                    if need.get(key, 0) < val:
                        need[key] = val
                for key, val in need.items():
                    if waited.get(key, 0) >= val:
                        continue
                    waited[key] = val
                    eng.wait_ge(semh[key], val)
                ins = fn(eng)
                if signal:
                    n = sig[(e, i)]
                    ins.then_inc(semh[('e', e, (n - 1) // E)], 1)
                if dsem is not None:
                    ins.then_inc(semh[('d', dsem)], 16)
            if e == 'sp':
                for name, c in dma_cnt.items():
                    eng.wait_ge(semh[('d', name)], c * 16)

        @block.tensor
        def _(pe):
            run('pe', pe)

        @block.scalar
        def _(act):
            run('act', act)

        @block.vector
        def _(dve):
            run('dve', dve)

        @block.gpsimd
        def _(pool):
            run('pool', pool)

        @block.sync
        def _(sp):
            run('sp', sp)

    def mm(self, out, lhsT, rhs, start=True, stop=True, reads=(), writes=(), **kw):
        return self.op('pe', lambda e: e.matmul(out, lhsT, rhs, start=start, stop=stop, **kw), reads, writes)

    def tr(self, out, in_, ident, reads=(), writes=()):
        return self.op('pe', lambda e: e.transpose(out, in_, ident), reads, writes)

    def act(self, out, in_, func, reads=(), writes=(), **kw):
        return self.op('act', lambda e: e.activation(out, in_, func, **kw), reads, writes)

    def tt(self, eng, out, in0, in1, op, reads=(), writes=()):
        return self.op(eng, lambda e: e.tensor_tensor(out, in0, in1, op), reads, writes)

    def ts(self, eng, out, in0, s1, s2, op0, op1=None, reads=(), writes=(), **kw):
        if op1 is None:
            return self.op(eng, lambda e: e.tensor_scalar(out, in0, s1, None, op0, **kw), reads, writes)
        return self.op(eng, lambda e: e.tensor_scalar(out, in0, s1, s2, op0, op1, **kw), reads, writes)

    def stt(self, out, in0, scalar, in1, op0, op1, reads=(), writes=(), **kw):
        return self.op('dve', lambda e: e.scalar_tensor_tensor(out, in0, scalar, in1, op0, op1, **kw), reads, writes)

    def copy(self, eng, out, in_, reads=(), writes=()):
        if eng == 'act':
            return self.op('act', lambda e: e.copy(out, in_), reads, writes)
        return self.op(eng, lambda e: e.tensor_copy(out, in_), reads, writes)

    def memset(self, eng, ap, val, writes=()):
        return self.op(eng, lambda e: e.memset(ap, val), (), writes)

    def load(self, out, in_, sem, reads=(), writes=(), q='sp', **kw):
        return self.dma(q, lambda e: e.dma_start(out, in_, **kw), sem, reads, writes)


WNAMES = ["w_in", "w_s", "b_s", "g_cq", "g_ckv", "w_uq", "w_uk", "w_uv", "g_mix", "w_out",
          "ln1_g", "ln1_b", "w_up", "b_up", "w_down", "b_down", "ln2_g", "ln2_b"]
WIN_PIECES = [(0, 256), (256, 256), (512, 256), (768, 128), (896, 256), (1152, 64), (1216, 256), (1472, 256), (1728, 256)]


def build_program(S=SEQ, NL=NLAYERS, PASTL=PAST, nseq=2, do_sample=True):
    nc = bass.Bass("TRN2", target_bir_lowering=False)

    def din(name, shape):
        return nc.dram_tensor(name, list(shape), F32, kind="ExternalInput").ap()

    def dout(name, shape):
        return nc.dram_tensor(name, list(shape), F32, kind="ExternalOutput").ap()

    xp = din("xp", [nseq, S, D])
    xs = din("xs", [NS, D])
    c_ckv = din("c_ckv", [NL, PASTL, 256])
    c_kr = din("c_kr", [NL, PASTL, 64])
    c_k = din("c_k", [NL, PASTL, 256])
    c_v = din("c_v", [NL, PASTL, 256])
    W = {}
    wshapes = {"w_in": [NL, D, WIN], "w_s": [NL, 4, 128, 128], "b_s": [NL, 4, 128], "g_cq": [NL, 384],
               "g_ckv": [NL, 256], "w_uq": [NL, 384, 768], "w_uk": [NL, 4, 256, 128], "w_uv": [NL, 4, 256, 128],
               "g_mix": [NL, 1024], "w_out": [NL, 1024, 1024], "ln1_g": [NL, 1024], "ln1_b": [NL, 1024],
               "w_up": [NL, 1024, DFF], "b_up": [NL, DFF], "w_down": [NL, DFF, 1024], "b_down": [NL, 1024],
               "ln2_g": [NL, 1024], "ln2_b": [NL, 1024]}
    for k in WNAMES:
        W[k] = din(k, wshapes[k])
    tab = {"p": (din("tp_cos", [S, 32]), din("tp_sin", [S, 32]), din("tp_cosF", [64, S]), din("tp_sinF", [64, S])),
           "s": (din("ts_cos", [NS, 32]), din("ts_sin", [NS, 32]), din("ts_cosF", [64, NS]), din("ts_sinF", [64, NS]))}
    yp = dout("yp", [nseq, S, D])
    ys = dout("ys", [NS, D])
    o_p = (dout("o_ckv_p", [NL, nseq, S, 256]), dout("o_kr_p", [NL, nseq, S, 64]),
           dout("o_k_p", [NL, nseq, S, 256]), dout("o_v_p", [NL, nseq, S, 256]))
    o_s = (dout("o_ckv_s", [NL, NS, 256]), dout("o_kr_s", [NL, NS, 64]),
           dout("o_k_s", [NL, NS, 256]), dout("o_v_s", [NL, NS, 256]))
    o_gv = dout("o_gv_s", [NL, NS, 256])

    NTP = S // 128
    KS = max(NTP, 9)
    NPIECE = NL * (9 + 4 + 32)
    wscr = nc.dram_tensor("wscr", [NPIECE, 128, 2048], BF16, kind="Internal").ap()

    with ExitStack() as st:
        P = Prog(nc, same_engine_sync=not bool(os.environ.get("K_NOSES")))

        def sb(name, shape, dt=F32):
            return st.enter_context(nc.sbuf_tensor("sb_" + name, list(shape), dt))

        def RL(name, n):
            return [Res(f"{name}{i}") for i in range(n)]

        x_t = sb("x_t", [128, NTP, D]); x_res = RL("x", NTP)
        GP = 2
        xT = sb("xT", [128, 8, GP * 128], BF16); xT_res = RL("xT", 4)
        y_t = sb("y_t", [128, GP, D], BF16); y_res = RL("y", 4)
        knT = sb("knT", [128, 4, KS * 128], BF16)
        krT = sb("krT", [64, KS * 128], BF16)
        vp = sb("vp", [128, KS, 4, 130], BF16)
        sbKT = sb("sbKT", [128, 2, KS * 128], BF16)
        sbV = sb("sbV", [128, KS, 256], BF16)
        kv_res = RL("kv", KS)
        x1T_res = Res("x1T")
        qnT = sb("qnT", [128, 4, GP * 128], BF16)
        qrT = sb("qrT", [64, 4, GP * 128], BF16)
        sbQT = sb("sbQT", [128, 2, GP * 128], BF16)
        cqT = sb("cqT", [128, 3, GP * 128], BF16)
        ckvT = sb("ckvT", [128, 2, 512], BF16)
        q_res = Res("q"); sbq_res = RL("sbq", 4); cqT_res = RL("cqT", 4); ckvT_res = RL("ckvT", 4)
        wuq = sb("wuq", [128, 3, 768], BF16); wrot = sb("wrot", [128, 3, 4, 64], BF16)
        wuk = sb("wuk", [128, 2, 512], BF16); wuv = sb("wuv", [128, 2, 512], BF16)
        wsT = sb("wsT", [128, 4, 128], BF16)
        smallw_res = Res("smallw")
        g_ckv = sb("g_ckv", [128, 256])
        lnb = sb("lnb", [128, 2, 1024]); lnb_res = RL("lnb", 2)
        prm = sb("prm", [128, 47])
        bup = prm[:, 0:32]; bsb = prm[:, 32:36]
        prow_res = None
        identf = sb("identf", [64, 64])
        gains_res = Res("gains")
        NRING = 4
        ring = sb("ring", [128, NRING, 2048], BF16); ring_res = RL("ring", NRING)
        stg = sb("stg", [128, 2, 512]); stg_res = RL("stg", 2)
        tcos = sb("tcos", [128, GP, 32]); tsin = sb("tsin", [128, GP, 32])
        tcosF = sb("tcosF", [64, GP * 128]); tsinF = sb("tsinF", [64, GP * 128])
        tab_res = Res("tab"); tabF_res = Res("tabF")
        ident = sb("ident", [128, 128], BF16); negU = sb("negU", [128, 128], BF16)
        maskSB = sb("maskSB", [128, 128], BF16); ones1 = sb("ones1", [128, 2], BF16)
        const_res = Res("const")
        u_bf = sb("u_bf", [128, GP, 256], BF16); u_res = RL("u", 4)
        f32t = sb("f32t", [128, 3, 512]); f32_res = RL("f32t", 3)
        cqraw = sb("cqraw", [128, GP, 384]); cqraw_res = Res("cqraw")
        kvst = sb("kvst", [128, 2, 832]); kvst_res = RL("kvst", 2)
        bf512 = sb("bf512", [128, 12, 256], BF16)
        bfp_res = RL("bfp", 6)
        bf_res = [bfp_res[i // 2] for i in range(8)]
        kvbf = bf512[:, 8:12, :].rearrange("p a b -> p (a b)")[:, 0:832]
        kvbf_rl = [bfp_res[4], bfp_res[5]]
        wsb = bf512[:, 0:2, :].rearrange("p a (g j) -> p (a g) j", g=2)
        hacc = sb("hacc", [128, 2048], BF16)
        accsb = hacc[:, 0:GP * 512].bitcast(F32).rearrange("p (q h d) -> p q h d", q=GP, h=4); accsb_res = RL("accsb", 4)
        fexp = sb("fexp", [128, 2, 4]); fexp_res = RL("fexp", 2)
        yT = sb("yT", [128, 1, 8, 128], BF16); yT_res = RL("yT", 1)
        xb = sb("xb", [128, 1024], BF16); xb_res = Res("xb")
        hT = hacc[:, :].rearrange("p (f c) -> p f c", f=4); hT_res = RL("hT", 4)
        stat = sb("stat", [128, 64]); stat_res = Res("stat")
        rtmp = sb("rtmp", [128, 4, 32]); rtmp_res = Res("rtmp")
        prow_res = rtmp_res
        prow = rtmp[0:47, :, :].rearrange("p a b -> p (a b)")

        ps = [st.enter_context(nc.psum_tensor(f"ps{i}", [128, 512], F32)) for i in range(8)]
        ps_res = RL("ps", 8)

        P.memset('pool', ident[:], 1.0, writes=[const_res])
        P.op('pool', lambda e: e.affine_select(ident[:], ident[:], pattern=[[-1, 128]], compare_op=ALU.is_equal,
                                               fill=0.0, base=0, channel_multiplier=1), writes=[const_res])
        P.memset('pool', negU[:], -1.0, writes=[const_res])
        P.op('pool', lambda e: e.affine_select(negU[:], negU[:], pattern=[[-1, 128]], compare_op=ALU.is_ge,
                                               fill=0.0, base=0, channel_multiplier=1), writes=[const_res])
        P.memset('pool', maskSB[:], 1.0, writes=[const_res])
        P.op('pool', lambda e: e.affine_select(maskSB[:], maskSB[:], pattern=[[1, 128]], compare_op=ALU.is_gt,
                                               fill=0.0, base=0, channel_multiplier=-1), writes=[const_res])
        P.memset('pool', ones1[:], 1.0, writes=[const_res])
        P.memset('pool', identf[:], 1.0, writes=[const_res])
        P.op('pool', lambda e: e.affine_select(identf[:], identf[:], pattern=[[-1, 64]], compare_op=ALU.is_equal,
                                               fill=0.0, base=0, channel_multiplier=1), writes=[const_res])
        gcqc = prm[:, 36:39]; gmixc = prm[:, 39:47]
        P.memset('pool', vp[:, :, :, 128:130], 1.0, writes=kv_res)

        cnt = {"f32": 0, "bf": 0, "stg": 0, "ring": 0, "mla": 0, "sb": 0, "bfp": 0, "mlap": 0}

        def f32buf():
            if cnt.get("f32fix") is not None:
                i = cnt["f32fix"]
            else:
                i = cnt["f32"] % 3; cnt["f32"] += 1
            return f32t[:, i, :], f32_res[i]

        def bfpair_mla():
            if cnt.get("mla_pool"):
                j = 4 + cnt["mlap"] % 2; cnt["mlap"] += 1
                return bf512[:, 2 * j:2 * j + 2, :], bfp_res[j]
            return bfpair()

        def bfbuf():
            i = cnt["bf"] % 8; cnt["bf"] += 1
            return bf512[:, i, :], bf_res[i]

        class PairRes:
            pass

        def bfpair():
            j = cnt["bfp"] % 4; cnt["bfp"] += 1
            cnt["bf"] = 2 * j + 2
            return bf512[:, 2 * j:2 * j + 2, :], bfp_res[j]

        def quarters(a, b):
            out = []
            if a * b <= 512:
                return [(0, a, 0, b)]
            if b <= 512:
                step = max(1, 512 // b)
                for a0 in range(0, a, step):
                    out.append((a0, min(a, a0 + step), 0, b))
            else:
                for a0 in range(a):
                    for b0 in range(0, b, 512):
                        out.append((a0, a0 + 1, b0, min(b, b0 + 512)))
            return out

        def staged_cast(dst3, src3, a, b, dst_res, scale_cols=None, scale_res=None):
            for (a0, a1, b0, b1) in quarters(a, b):
                si = cnt["stg"] % 2; cnt["stg"] += 1
                na, nb = a1 - a0, b1 - b0
                sview = stg[:, si, 0:na * nb].rearrange("p (a b) -> p a b", a=na)
                P.load(sview, src3[:, a0:a1, b0:b1], f"stg{si}", writes=[stg_res[si]])
                if scale_cols is None:
                    P.copy('pool', dst3[:, a0:a1, b0:b1], sview, reads=[stg_res[si]], writes=[dst_res])
                else:
                    P.tt('pool', dst3[:, a0:a1, b0:b1], sview, scale_cols[:, a0:a1].unsqueeze(2).to_broadcast([128, na, nb]), ALU.mult,
                         reads=[stg_res[si], scale_res], writes=[dst_res])

        def stream_piece(src_ap, shape3, scale_cols=None, scale_res=None):
            a, b = shape3
            ri = cnt["ring"] % NRING; cnt["ring"] += 1
            rview = ring[:, ri, 0:a * b].rearrange("p (a b) -> p a b", a=a)
            staged_cast(rview, src_ap, a, b, ring_res[ri], scale_cols, scale_res)
            return rview, ring_res[ri]

        scr_ids = {}
        scr_res = {}

        class Streamer:
            def __init__(self, specs):
                self.specs = specs
                self.pos_req = 0
                self.pos_get = 0
                self.out = 0
                self.slots = {}

            def request(self, sp):
                key, src, (a, b), scale = sp
                ri = cnt["ring"] % NRING; cnt["ring"] += 1
                rview = ring[:, ri, 0:a * b].rearrange("p (a b) -> p a b", a=a)
                if key in scr_ids:
                    pid = scr_ids[key]
                    P.load(rview, wscr[pid, :, 0:a * b].rearrange("p (a b) -> p a b", a=a), f"ring{ri}",
                           reads=[scr_res[key]], writes=[ring_res[ri]])
                else:
                    pid = len(scr_ids)
                    scr_ids[key] = pid
                    scr_res[key] = Res(f"scr{pid}")
                    if scale:
                        staged_cast(rview, src, a, b, ring_res[ri], gmixc, gains_res)
                    else:
                        staged_cast(rview, src, a, b, ring_res[ri])
                    P.load(wscr[pid, :, 0:a * b].rearrange("p (a b) -> p a b", a=a), rview, f"wst{ri}",
                           reads=[ring_res[ri]], writes=[scr_res[key]])
                return rview, ring_res[ri]

            def top_up(self):
                while self.out < NRING and self.pos_req < len(self.specs):
                    self.slots[self.pos_req] = self.request(self.specs[self.pos_req])
                    self.pos_req += 1
                    self.out += 1

            def get(self, key):
                assert self.specs[self.pos_get][0] == key, (self.specs[self.pos_get][0], key)
                if self.pos_get >= self.pos_req:
                    self.top_up()
                assert self.pos_get < self.pos_req, "ring exhausted (missing release)"
                r = self.slots.pop(self.pos_get)
                self.pos_get += 1
                return r

            def release(self, n=1):
                self.out -= n
                self.top_up()

        def make_specs(NG_):
            sp = []
            for l in range(NL):
                for g in range(NG_):
                    for pi, (c0, ncols) in enumerate(WIN_PIECES):
                        sp.append(((l, 'in', pi), W["w_in"][l][:, c0:c0 + ncols].rearrange("(k p) c -> p k c", p=128), (8, ncols), False))
                    for c in range(4):
                        sp.append(((l, 'out', c), W["w_out"][l][:, c * 256:(c + 1) * 256].rearrange("(k p) c -> p k c", p=128), (8, 256), True))
                for e8 in range(8):
                    for hh in range(2):
                        sp.append(((l, 'up', e8, hh), W["w_up"][l][:, e8 * 512 + hh * 256:e8 * 512 + (hh + 1) * 256].rearrange("(k p) c -> p k c", p=128), (8, 256), False))
                    for hh in range(2):
                        sp.append(((l, 'dn', e8, hh), W["w_down"][l][e8 * 512 + hh * 256:e8 * 512 + (hh + 1) * 256, :].rearrange("(f p) c -> p f c", p=128), (2, 1024), False))
            return sp

        def cast_load(dst_ap, src_ap, shape3, reads_extra=(), dst_res=None):
            a, b = shape3
            staged_cast(dst_ap, src_ap, a, b, dst_res)

        def load_layer_small(l, nt_s):
            for kc in range(3):
                cast_load(wuq[:, kc:kc + 1, :], W["w_uq"][l, kc * 128:(kc + 1) * 128, :].rearrange("p (a b) -> p a b", a=1),
                          (1, 768), dst_res=smallw_res)
            P.load(prow[0:32, :], W["b_up"][l].rearrange("(f p) -> f p", p=128), "prow", writes=[prow_res])
            P.load(prow[32:36, :], W["b_s"][l], "prow", writes=[prow_res])
            P.load(prow[36:39, :], W["g_cq"][l].rearrange("(k p) -> k p", p=128), "prow", writes=[prow_res])
            P.load(prow[39:47, :], W["g_mix"][l].rearrange("(k p) -> k p", p=128), "prow", writes=[prow_res])
            P.tr(ps[7][:, 0:47], prow[:, :], identf[0:47, 0:47], reads=[prow_res, const_res], writes=[ps_res[7]])
            P.copy('dve', prm[:, :], ps[7][:, 0:47], reads=[ps_res[7]], writes=[gains_res])
            P.tt('pool', wuq[:], wuq[:], gcqc.unsqueeze(2).to_broadcast([128, 3, 768]), ALU.mult, reads=[smallw_res, gains_res], writes=[smallw_res])
            wq4 = wuq[:].rearrange("p k (h d) -> p k h d", h=4)
            P.ts('pool', wrot[:, :, :, 0:32], wq4[:, :, :, 160:192], -1.0, None, ALU.mult, reads=[smallw_res], writes=[smallw_res])
            P.copy('pool', wrot[:, :, :, 32:64], wq4[:, :, :, 128:160], reads=[smallw_res], writes=[smallw_res])
            for kc in range(2):
                cast_load(wuk[:, kc, :].rearrange("p (h n) -> p h n", h=4),
                          W["w_uk"][l][:, kc * 128:(kc + 1) * 128, :].rearrange("h c n -> c h n"), (4, 128), dst_res=smallw_res)
                cast_load(wuv[:, kc, :].rearrange("p (h n) -> p h n", h=4),
                          W["w_uv"][l][:, kc * 128:(kc + 1) * 128, :].rearrange("h c n -> c h n"), (4, 128), dst_res=smallw_res)
            cast_load(wsb, W["w_s"][l].rearrange("g i j -> i g j"), (4, 128), dst_res=bfp_res[0])
            for g in range(4):
                pst = ps[6][:].bitcast(BF16)
                P.tr(pst[0:nt_s, g * 128:g * 128 + nt_s], wsb[0:nt_s, g, 0:nt_s], ident[0:nt_s, 0:nt_s],
                     reads=[bfp_res[0], const_res], writes=[ps_res[6]])
            P.copy('dve', wsT[0:nt_s, :, 0:nt_s], ps[6][:].bitcast(BF16)[0:nt_s, 0:512].rearrange("p (g i) -> p g i", g=4)[:, :, 0:nt_s],
                   reads=[ps_res[6]], writes=[smallw_res])
            if nt_s == 128:
                P.memset('pool', wsT[64:128, :, 0:64], 0.0, writes=[smallw_res])
            P.load(g_ckv[:], W["g_ckv"][l:l + 1, :].to_broadcast([128, 256]), "gains", writes=[gains_res])

        def load_ln(l, which):
            names = ("ln1_g", "ln1_b") if which == 1 else ("ln2_g", "ln2_b")
            for i, nm in enumerate(names):
                P.load(lnb[:, i, :], W[nm][l:l + 1, :].to_broadcast([128, 1024]), f"lnb{i}", writes=[lnb_res[i]])

        def rstd_from_ss(col_ss, col_out, n, nt):
            P.act(stat[0:nt, col_out:col_out + 1], stat[0:nt, col_ss:col_ss + 1], AF.Ln, scale=1.0 / n, bias=EPS,
                  reads=[stat_res], writes=[stat_res])
            P.act(stat[0:nt, col_out:col_out + 1], stat[0:nt, col_out:col_out + 1], AF.Exp, scale=-0.5,
                  reads=[stat_res], writes=[stat_res])

        def make_xT(t, slot, nt, dst, dst_res, col0):
            P.copy('act', xb[0:nt, :], x_t[0:nt, t, :], reads=[x_res[t]], writes=[xb_res])
            pst = ps[7][:].bitcast(BF16)
            for k in range(8):
                P.tr(pst[:, k * 128:k * 128 + nt], xb[0:nt, k * 128:(k + 1) * 128], ident[0:nt, 0:nt],
                     reads=[xb_res, const_res], writes=[ps_res[7]])
            P.copy('dve', dst[:, :, col0:col0 + nt], pst[:, :].rearrange("p (k c) -> p k c", k=8)[:, :, 0:nt],
                   reads=[ps_res[7]], writes=[dst_res])

        def layer_norm_tile(t, nt, l):
            xv = x_t[0:nt, t, :]
            P.op('dve', lambda e: e.bn_stats(stat[0:nt, 0:6], x_t[0:nt, t, 0:512]), reads=[x_res[t]], writes=[stat_res])
            P.op('dve', lambda e: e.bn_stats(stat[0:nt, 6:12], x_t[0:nt, t, 512:1024]), reads=[x_res[t]], writes=[stat_res])
            P.op('dve', lambda e: e.bn_aggr(stat[0:nt, 12:14], stat[0:nt, 0:12]), reads=[stat_res], writes=[stat_res])
            P.act(stat[0:nt, 14:15], stat[0:nt, 13:14], AF.Ln, bias=EPS, reads=[stat_res], writes=[stat_res])
            P.act(stat[0:nt, 14:15], stat[0:nt, 14:15], AF.Exp, scale=-0.5, reads=[stat_res], writes=[stat_res])
            P.ts('dve', xv, xv, stat[0:nt, 12:13], stat[0:nt, 14:15], ALU.subtract, ALU.mult,
                 reads=[x_res[t], stat_res], writes=[x_res[t]])
            P.tt('pool', xv, xv, lnb[0:nt, 0, :], ALU.mult, reads=[x_res[t], lnb_res[0]], writes=[x_res[t]])
            P.tt('pool', xv, xv, lnb[0:nt, 1, :], ALU.add, reads=[x_res[t], lnb_res[1]], writes=[x_res[t]])

        def ingest_kv(slot, nk, st_i):
            src = kvst[0:nk, st_i, :]
            P.copy('act', kvbf[0:nk, :], src, reads=[kvst_res[st_i]], writes=kvbf_rl)
            c0 = slot * 128
            pst = ps[6][:].bitcast(BF16)
            P.tr(pst[:, 0:nk], kvbf[0:nk, 0:128], ident[0:nk, 0:nk], reads=[*kvbf_rl, const_res], writes=[ps_res[6]])
            P.tr(pst[:, 128:128 + nk], kvbf[0:nk, 128:256], ident[0:nk, 0:nk], reads=kvbf_rl, writes=[ps_res[6]])
            P.tr(pst[:, 256:256 + nk], kvbf[0:nk, 320:448], ident[0:nk, 0:nk], reads=kvbf_rl, writes=[ps_res[6]])
            P.tr(pst[:, 384:384 + nk], kvbf[0:nk, 448:576], ident[0:nk, 0:nk], reads=kvbf_rl, writes=[ps_res[6]])
            P.tr(pst[0:64, 512:512 + nk], kvbf[0:nk, 256:320], ident[0:nk, 0:nk], reads=kvbf_rl, writes=[ps_res[6]])
            gi = slot % 4
            P.copy('dve', ckvT[:, :, gi * 128:gi * 128 + nk], pst[:, 0:256].rearrange("p (k c) -> p k c", k=2)[:, :, 0:nk],
                   reads=[ps_res[6]], writes=[ckvT_res[gi]])
            P.copy('dve', sbKT[:, :, c0:c0 + nk], pst[:, 256:512].rearrange("p (k c) -> p k c", k=2)[:, :, 0:nk],
                   reads=[ps_res[6]], writes=[kv_res[slot]])
            P.copy('act', krT[:, c0:c0 + nk], pst[0:64, 512:512 + nk], reads=[ps_res[6]], writes=[kv_res[slot]])
            P.copy('pool', sbV[0:nk, slot, :], kvbf[0:nk, 576:832], reads=kvbf_rl, writes=[kv_res[slot]])

        def project_keys(slots, nk, kbanks=(0, 1, 2, 3), vbanks=(4, 5)):
            ncol = (len(slots) - 1) * 128 + nk
            g0 = (slots[0] % 4) * 128
            c0 = slots[0] * 128
            rd = [ckvT_res[s % 4] for s in slots] + [smallw_res]
            for h in range(4):
                b = kbanks[h % len(kbanks)]
                for kc in range(2):
                    P.mm(ps[b][:, 0:ncol], wuk[:, kc, h * 128:(h + 1) * 128], ckvT[:, kc, g0:g0 + ncol],
                         start=(kc == 0), stop=(kc == 1), reads=rd, writes=[ps_res[b]])
                P.copy('act' if h % 2 == 0 else 'dve', knT[:, h, c0:c0 + ncol], ps[b][:, 0:ncol], reads=[ps_res[b]],
                       writes=[kv_res[s] for s in slots])
            for i, s in enumerate(slots):
                nkk = 128 if i < len(slots) - 1 else nk
                b = vbanks[i % len(vbanks)]
                for kc in range(2):
                    P.mm(ps[b][0:nkk, :], ckvT[:, kc, (s % 4) * 128:(s % 4) * 128 + nkk], wuv[:, kc, :],
                         start=(kc == 0), stop=(kc == 1), reads=[ckvT_res[s % 4], smallw_res], writes=[ps_res[b]])
                P.copy('dve' if i % 2 == 0 else 'act', vp[0:nkk, s, :, 0:128], ps[b][0:nkk, :].rearrange("p (h d) -> p h d", h=4),
                       reads=[ps_res[b]], writes=[kv_res[s]])

        def run_pass(kind, seq_i):
            prompt = (kind == "p")
            nt = 128 if prompt else NS
            NT = NTP if prompt else 1
            G = GP if prompt else 1
            NG = NT // G
            gq = G * nt
            tcs, tsn, tcF, tsF = tab[kind]
            STR = Streamer(make_specs(NG))
            if prompt:
                for t in range(NT):
                    P.load(x_t[:, t, :], xp[seq_i, t * 128:(t + 1) * 128, :], f"xin{t % 8}", writes=[x_res[t]])
            else:
                P.load(tcos[0:nt, 0, :], tcs, "tab", writes=[tab_res])
                P.load(tsin[0:nt, 0, :], tsn, "tab", writes=[tab_res])
                P.load(x_t[0:nt, 0, :], xs, "xin", writes=[x_res[0]])

            for l in range(NL):
                P.memset('pool', vp[:, :, :, 128:130], 1.0, writes=kv_res + [x1T_res] + accsb_res + hT_res)
                P.mark(f"{kind}{seq_i} L{l} start")
                load_layer_small(l, nt)
                load_ln(l, 1)
                P.mark(f"{kind}{seq_i} L{l} small loaded")
                new_slot0 = 0 if prompt else 8

                for g in range(NG):
                    tiles = [g * G + i for i in range(G)]
                    if prompt:
                        P.load(tcosF[:, 0:gq], tcF[:, g * gq:(g + 1) * gq], "tabF", writes=[tabF_res])
                        P.load(tsinF[:, 0:gq], tsF[:, g * gq:(g + 1) * gq], "tabF", writes=[tabF_res])
                        P.load(tcos[:, 0:G, :], tcs[g * gq:(g + 1) * gq, :].rearrange("(t p) d -> p t d", p=128), "tab", writes=[tab_res])
                        P.load(tsin[:, 0:G, :], tsn[g * gq:(g + 1) * gq, :].rearrange("(t p) d -> p t d", p=128), "tab", writes=[tab_res])
                    else:
                        P.load(tcosF[:, 0:gq], tcF, "tabF", writes=[tabF_res])
                        P.load(tsinF[:, 0:gq], tsF, "tabF", writes=[tabF_res])
                    for i, t in enumerate(tiles):
                        make_xT(t, i, nt, xT, xT_res[i], i * 128)
                    pbank = [0]

                    def proj(pi):
                        c0, ncols = WIN_PIECES[pi]
                        wv, wr = STR.get((l, 'in', pi))
                        outs = []
                        for i, t in enumerate(tiles):
                            b = pbank[0] % 4; pbank[0] += 1
                            for k in range(8):
                                P.mm(ps[b][0:nt, 0:ncols], xT[:, k, i * 128:i * 128 + nt], wv[:, k, :],
                                     start=(k == 0), stop=(k == 7), reads=[xT_res[i], wr], writes=[ps_res[b]])
                            outs.append((ps[b][0:nt, 0:ncols], ps_res[b]))
                        STR.release()
                        return outs

                    def cons(pi, i, t, pv, pr):
                        slot = (new_slot0 + t) if prompt else 8
                        sti = i % 2
                        if pi == 0:
                            P.act(u_bf[0:nt, i, :], pv, AF.Gelu_apprx_tanh, reads=[pr], writes=[u_res[i]])
                            yield "evac"
                        elif pi == 1:
                            gv, gvr = f32buf()
                            yield
                            gv = gv[0:nt, 0:256]
                            yield
                            P.act(gv, pv, AF.Gelu_apprx_tanh, reads=[pr], writes=[gvr])
                            yield
                            gv3 = gv.rearrange("p (g d) -> p g d", g=4)
                            yield
                            sq, sqr = f32buf()
                            yield
                            sq = sq[0:nt, 0:256]
                            yield
                            P.op('dve', lambda e, gv3=gv3: e.tensor_reduce(stat[0:nt, 16:20], gv3, AX.X, ALU.add), reads=[gvr], writes=[stat_res])
                            yield
                            P.act(sq, gv, AF.Square, reads=[gvr], writes=[sqr])
                            yield
                            P.op('dve', lambda e, sq=sq: e.tensor_reduce(stat[0:nt, 20:24], sq.rearrange("p (g d) -> p g d", g=4), AX.X, ALU.add),
                                 reads=[sqr], writes=[stat_res])
                            yield
                            P.ts('dve', stat[0:nt, 16:20], stat[0:nt, 16:20], 1.0 / 64, None, ALU.mult, reads=[stat_res], writes=[stat_res])
                            yield
                            P.tt('dve', stat[0:nt, 24:28], stat[0:nt, 16:20], stat[0:nt, 16:20], ALU.mult, reads=[stat_res], writes=[stat_res])
                            yield
                            P.stt(stat[0:nt, 20:24], stat[0:nt, 20:24], 1.0 / 64, stat[0:nt, 24:28], ALU.mult, ALU.subtract,
                                  reads=[stat_res], writes=[stat_res])
                            yield
                            P.act(stat[0:nt, 20:24], stat[0:nt, 20:24], AF.Ln, bias=EPS, reads=[stat_res], writes=[stat_res])
                            yield
                            P.act(stat[0:nt, 20:24], stat[0:nt, 20:24], AF.Exp, scale=-0.5, reads=[stat_res], writes=[stat_res])
                            yield
                            P.tt('dve', gv3, gv3, stat[0:nt, 16:20].unsqueeze(2).to_broadcast([nt, 4, 64]), ALU.subtract,
                                 reads=[gvr, stat_res], writes=[gvr])
                            yield
                            P.tt('dve', gv3, gv3, stat[0:nt, 20:24].unsqueeze(2).to_broadcast([nt, 4, 64]), ALU.mult,
                                 reads=[gvr, stat_res], writes=[gvr])
                            yield
                            v_bf, vbf_res = bfbuf()
                            yield
                            P.copy('pool', v_bf[0:nt, :], gv, reads=[gvr], writes=[vbf_res])
                            yield
                            if not prompt:
                                P.load(o_gv[l], gv, "ogv", reads=[gvr])
                            yield
                            yield "defer"
                            for gg in range(4):
                                P.mm(ps[5][0:nt, gg * 64:(gg + 1) * 64], wsT[0:nt, gg, 0:nt], v_bf[0:nt, gg * 64:(gg + 1) * 64],
                                     reads=[vbf_res, smallw_res], writes=[ps_res[5]])
                            yield
                            ya, yar = f32buf()
                            yield
                            ya = ya[0:nt, 0:256]
                            yield
                            for gg in range(4):
                                P.stt(ya[:, gg * 64:(gg + 1) * 64], ps[5][0:nt, gg * 64:(gg + 1) * 64], bsb[0:nt, gg:gg + 1],
                                      u_bf[0:nt, i, gg * 64:(gg + 1) * 64], ALU.add, ALU.mult,
                                      reads=[ps_res[5], gains_res, u_res[i]], writes=[yar])
                            yield
                            sq3, sq3r = bfbuf()
                            P.act(sq3[0:nt, 0:256], ya, AF.Square, accum_out=stat[0:nt, 28:29], reads=[yar], writes=[sq3r, stat_res])
                            yield
                            rstd_from_ss(28, 29, 256, nt)
                            yield
                            P.ts('dve', y_t[0:nt, i, 0:256], ya, stat[0:nt, 29:30], None, ALU.mult,
                                 reads=[yar, stat_res], writes=[y_res[i]])
                            yield
                        elif pi == 2:
                            sq, sqr = f32buf()
                            yield
                            P.copy('dve', cqraw[0:nt, i, 0:256], pv, reads=[pr], writes=[cqraw_res])
                            yield "evac"
                            P.act(sq[0:nt, 0:256], cqraw[0:nt, i, 0:256], AF.Square, accum_out=stat[0:nt, 48 + 2 * i:49 + 2 * i], reads=[cqraw_res], writes=[sqr, stat_res])
                            yield
                        elif pi == 3:
                            sq, sqr = f32buf()
                            yield
                            P.copy('dve', cqraw[0:nt, i, 256:384], pv, reads=[pr], writes=[cqraw_res])
                            yield "evac"
                            P.act(sq[0:nt, 0:128], cqraw[0:nt, i, 256:384], AF.Square, accum_out=stat[0:nt, 49 + 2 * i:50 + 2 * i], reads=[cqraw_res], writes=[sqr, stat_res])
                            yield
                            P.tt('dve', stat[0:nt, 32:33], stat[0:nt, 48 + 2 * i:49 + 2 * i], stat[0:nt, 49 + 2 * i:50 + 2 * i], ALU.add, reads=[stat_res], writes=[stat_res])
                            yield
                            rstd_from_ss(32, 33, 384, nt)
                            yield
                            cqa, cqar = bfbuf()
                            cqb, cqbr = bfbuf()
                            P.ts('dve', cqa[0:nt, 0:256], cqraw[0:nt, i, 0:256], stat[0:nt, 33:34], None, ALU.mult,
                                 reads=[cqraw_res, stat_res], writes=[cqar])
                            P.ts('dve', cqb[0:nt, 0:128], cqraw[0:nt, i, 256:384], stat[0:nt, 33:34], None, ALU.mult,
                                 reads=[cqraw_res, stat_res], writes=[cqbr])
                            yield
                            yield "defer"
                            pst = ps[4][:].bitcast(BF16)
                            yield
                            for kc in range(3):
                                src_ = cqa[0:nt, kc * 128:(kc + 1) * 128] if kc < 2 else cqb[0:nt, 0:128]
                                P.tr(pst[:, kc * 128:kc * 128 + nt], src_, ident[0:nt, 0:nt],
                                     reads=[cqar if kc < 2 else cqbr, const_res], writes=[ps_res[4]])
                            yield
                            P.copy('act', cqT[:, :, i * 128:i * 128 + nt], pst[:, 0:384].rearrange("p (k c) -> p k c", k=3)[:, :, 0:nt],
                                   reads=[ps_res[4]], writes=[cqT_res[i]])
                            yield
                        elif pi == 4:
                            sq, sqr = f32buf()
                            yield
                            raw, rawr = f32buf()
                            yield
                            P.copy('dve', raw[0:nt, 0:256], pv, reads=[pr], writes=[rawr])
                            yield
                            P.act(sq[0:nt, 0:256], raw[0:nt, 0:256], AF.Square, accum_out=stat[0:nt, 34:35], reads=[rawr], writes=[sqr, stat_res])
                            yield
                            rstd_from_ss(34, 35, 256, nt)
                            yield
                            P.stt(kvst[0:nt, sti, 0:256], raw[0:nt, 0:256], stat[0:nt, 35:36], g_ckv[0:nt, :], ALU.mult, ALU.mult,
                                  reads=[rawr, stat_res, gains_res], writes=[kvst_res[sti]])
                            yield
                        elif pi == 5:
                            tt_ = i
                            yield
                            cs_ = tcos[0:nt, tt_, :]; sn_ = tsin[0:nt, tt_, :]
                            yield
                            x1 = pv[:, 0:32]; x2 = pv[:, 32:64]
                            yield
                            P.tt('dve', rtmp[0:nt, 0, :], x1, cs_, ALU.mult, reads=[pr, tab_res], writes=[rtmp_res])
                            yield
                            P.tt('dve', rtmp[0:nt, 1, :], x2, sn_, ALU.mult, reads=[pr, tab_res], writes=[rtmp_res])
                            yield
                            P.tt('dve', rtmp[0:nt, 2, :], x2, cs_, ALU.mult, reads=[pr, tab_res], writes=[rtmp_res])
                            yield
                            P.tt('dve', rtmp[0:nt, 3, :], x1, sn_, ALU.mult, reads=[pr, tab_res], writes=[rtmp_res])
                            yield
                            P.tt('pool', kvst[0:nt, sti, 256:288], rtmp[0:nt, 0, :], rtmp[0:nt, 1, :], ALU.subtract,
                                 reads=[rtmp_res], writes=[kvst_res[sti]])
                            yield
                            P.tt('pool', kvst[0:nt, sti, 288:320], rtmp[0:nt, 2, :], rtmp[0:nt, 3, :], ALU.add,
                                 reads=[rtmp_res], writes=[kvst_res[sti]])
                            yield
                        elif pi == 6:
                            sbqb, sbqb_res = bfbuf()
                            yield
                            P.act(sbqb[0:nt, :], pv, AF.Copy, scale=SB_SCALE, reads=[pr], writes=[sbqb_res])
                            yield "evac"
                            yield "defer"
                            pst = ps[4][:].bitcast(BF16)
                            yield
                            for hp in range(2):
                                P.tr(pst[:, 512 + hp * 128:512 + hp * 128 + nt], sbqb[0:nt, hp * 128:(hp + 1) * 128], ident[0:nt, 0:nt],
                                     reads=[sbqb_res, const_res], writes=[ps_res[4]])
                            yield
                            P.copy('dve', sbQT[:, :, i * 128:i * 128 + nt], pst[:, 512:768].rearrange("p (k c) -> p k c", k=2)[:, :, 0:nt],
                                   reads=[ps_res[4]], writes=[sbq_res[i]])
                            yield
                        elif pi == 7:
                            P.copy('act', kvst[0:nt, sti, 320:576], pv, reads=[pr], writes=[kvst_res[sti]])
                            yield "evac"
                        elif pi == 8:
                            P.copy('dve', kvst[0:nt, sti, 576:832], pv, reads=[pr], writes=[kvst_res[sti]])
                            yield "evac"
                            if prompt:
                                rows = slice(t * 128, (t + 1) * 128)
                                outs = [o[l, seq_i, rows, :] for o in o_p]
                            else:
                                outs = [o[l] for o in o_s]
                            yield
                            for oo, (a0, a1) in zip(outs, [(0, 256), (256, 320), (320, 576), (576, 832)]):
                                P.load(oo, kvst[0:nt, sti, a0:a1], f"okv{sti}", reads=[kvst_res[sti]])
                            yield
                            yield "defer"
                            ingest_kv(slot, nt, sti)
                            yield

                    pending = []
                    nxt = proj(0)
                    for pi in range(len(WIN_PIECES)):
                        cur = nxt
                        if pi + 1 < len(WIN_PIECES):
                            nxt = proj(pi + 1)
                        newg = [cons(pi, i, t, cur[i][0], cur[i][1]) for i, t in enumerate(tiles)]
                        if pi in (0, 2, 3, 6, 7, 8):
                            for g_ in newg:
                                for r_ in g_:
                                    if r_ == "evac":
                                        break
                        active = pending + newg
                        pending = []
                        for g_ in active:
                            for r_ in g_:
                                if r_ == "defer":
                                    pending.append(g_)
                                    break
                    for g_ in pending:
                        for r_ in g_:
                            pass
                    P.mark(f"{kind}{seq_i} L{l} g{g} phaseB")
                    if prompt:
                        project_keys([new_slot0 + t for t in tiles], 128)
                    else:
                        project_keys([8], nt, (0, 1), (0, 1))
                    rdq = cqT_res[0:G] + [smallw_res]
                    for h in range(4):
                        b = h % 2
                        for kc in range(3):
                            P.mm(ps[b][:, 0:gq], wuq[:, kc, h * 192:h * 192 + 128], cqT[:, kc, 0:gq],
                                 start=(kc == 0), stop=(kc == 2), reads=rdq, writes=[ps_res[b]])
                        P.act(qnT[:, h, 0:gq], ps[b][:, 0:gq], AF.Copy, scale=MLA_SCALE, reads=[ps_res[b]], writes=[q_res])
                        for kc in range(3):
                            P.mm(ps[2][0:64, 0:gq], wuq[:, kc, h * 192 + 128:h * 192 + 192], cqT[:, kc, 0:gq],
                                 start=(kc == 0), stop=(kc == 2), reads=rdq, writes=[ps_res[2]])
                        for kc in range(3):
                            P.mm(ps[3][0:64, 0:gq], wrot[:, kc, h, :], cqT[:, kc, 0:gq],
                                 start=(kc == 0), stop=(kc == 2), reads=rdq, writes=[ps_res[3]])
                        t1, t1r = f32buf(); t2, t2r = f32buf()
                        P.tt('dve', t1[0:64, 0:gq], ps[2][0:64, 0:gq], tcosF[:, 0:gq], ALU.mult, reads=[ps_res[2], tabF_res], writes=[t1r])
                        P.tt('dve', t2[0:64, 0:gq], ps[3][0:64, 0:gq], tsinF[:, 0:gq], ALU.mult, reads=[ps_res[3], tabF_res], writes=[t2r])
                        P.tt('pool', qrT[:, h, 0:gq], t1[0:64, 0:gq], t2[0:64, 0:gq], ALU.add, reads=[t1r, t2r], writes=[q_res])

                    P.mark(f"{kind}{seq_i} L{l} g{g} attention")
                    def key_list_prompt():
                        return [(kt, 128, (kt - g * G) if kt >= g * G else -1) for kt in range(g * G + G)]

                    def mla_attend(hp, keys, first, last, accbs, sbanks=(0, 1), stages_only=False):
                        n = len(keys)
                        stt_ = [None] * n

                        def S1(k):
                            slot, nk, di = keys[k]
                            q0 = 0 if di < 0 else di * 128
                            ncol = gq - q0
                            c0 = slot * 128
                            b = sbanks[cnt["mla"] % len(sbanks)]; cnt["mla"] += 1
                            for hh in range(2):
                                h = 2 * hp + hh
                                P.mm(ps[b][0:nk, hh * gq:hh * gq + ncol], knT[:, h, c0:c0 + nk], qnT[:, h, q0:gq], start=True, stop=False,
                                     skip_group_check=True, reads=[kv_res[slot], q_res], writes=[ps_res[b]])
                                P.mm(ps[b][0:nk, hh * gq:hh * gq + ncol], krT[:, c0:c0 + nk], qrT[:, h, q0:gq], start=False, stop=True,
                                     skip_group_check=True, reads=[kv_res[slot], q_res], writes=[ps_res[b]])
                            pT, pTr = bfpair_mla()
                            pin = ps[b][0:nk, 0:2 * gq].rearrange("p (h c) -> p h c", h=2)[:, :, 0:ncol]
                            P.act(pT[0:nk, :, 0:ncol], pin, AF.Exp, reads=[ps_res[b]], writes=[pTr])
                            if di >= 0 and prompt:
                                P.memset('pool', pT[64:128, :, 0:64], 0.0, writes=[pTr])
                            stt_[k] = (pT, pTr, q0)

                        def S2(k):
                            slot, nk, di = keys[k]
                            pT, pTr, q0 = stt_[k]
                            for hh in range(2):
                                h = 2 * hp + hh
                                for qb in range(q0 // 128, G):
                                    ab = accbs[hh] if prompt else accbs[0]
                                    col = (qb % 2) * 129 if prompt else (h % 2) * 129
                                    st_flag = first[0].get(ab, True)
                                    first[0][ab] = False
                                    nq = nt
                                    P.mm(ps[ab][0:nq, col:col + 129], pT[0:nk, hh, qb * 128 - q0:qb * 128 - q0 + nq], vp[0:nk, slot, h, 0:129],
                                         start=st_flag, stop=False, skip_group_check=True, reads=[pTr, kv_res[slot]], writes=[ps_res[ab]])

                        def fin():
                            mla_final(hp, accbs)

                        if stages_only:
                            return n, S1, S2, fin
                        for step in range(n + 1):
                            if step < n:
                                S1(step)
                            if step >= 1:
                                S2(step - 1)
                        if last:
                            fin()

                    def mla_final(hp, accbs):
                        if True:
                            for hh in range(2):
                                h = 2 * hp + hh
                                for qb in range(G):
                                    ab = accbs[hh] if prompt else accbs[0]
                                    col = (qb % 2) * 129 if prompt else (h % 2) * 129
                                    sc = 40 + (cnt["mla"] % 2); cnt["mla"] += 1
                                    P.op('dve', lambda e, ab=ab, col=col, sc=sc: e.reciprocal(stat[0:nt, sc:sc + 1], ps[ab][0:nt, col + 128:col + 129]),
                                         reads=[ps_res[ab]], writes=[stat_res])
                                    yb, ybr = mla_out[qb]
                                    P.act(yb[0:nt, h * 128:(h + 1) * 128], ps[ab][0:nt, col:col + 128], AF.Copy, scale=stat[0:nt, sc:sc + 1],
                                          reads=[ps_res[ab], stat_res], writes=[ybr])

                    def sb_attend(hp, keys, first, banks, stages_only=False):
                        z1b, z2b, pob = banks
                        n = len(keys)
                        stt_ = [None] * n
                        ares = [accsb_res[hp], accsb_res[hp + 2]]

                        def geom(k):
                            slot, nk, di = keys[k]
                            q0 = 0 if di < 0 else di * 128
                            return slot, nk, di, q0, gq - q0, slot * 128

                        def kq(hh, c0, nk, q0):
                            base = 64 * hp
                            return sbKT[base:base + 64, hh, c0:c0 + nk], sbQT[base:base + 64, hh, q0:gq]

                        def S1(k):
                            slot, nk, di, q0, ncol, c0 = geom(k)
                            j = cnt["sb"]; cnt["sb"] += 1
                            b1 = z1b[j % len(z1b)]
                            for hh in range(2):
                                kT, qT = kq(hh, c0, nk, q0)
                                P.mm(ps[b1][0:nk, hh * gq:hh * gq + ncol], kT, qT, skip_group_check=True,
                                     reads=[kv_res[slot]] + sbq_res[0:G], writes=[ps_res[b1]])
                            e_, er = f32buf()
                            ev = e_[:, 0:2 * gq].rearrange("p (h c) -> p h c", h=2)[0:nk, :, 0:ncol]
                            zin = ps[b1][0:nk, 0:2 * gq].rearrange("p (h c) -> p h c", h=2)[:, :, 0:ncol]
                            P.act(ev, zin, AF.Exp, reads=[ps_res[b1]], writes=[er])
                            sp, spr = bfpair()
                            P.act(sp[0:nk, :, 0:ncol], ev, AF.Ln, bias=1.0, reads=[er], writes=[spr])
                            if di >= 0:
                                P.tt('pool', sp[0:nk, :, 0:nt], sp[0:nk, :, 0:nt], maskSB[0:nk, 0:nt].unsqueeze(1).to_broadcast([nk, 2, nt]), ALU.mult,
                                     reads=[spr, const_res], writes=[spr])
                            stt_[k] = dict(sp=sp, spr=spr, j=j)

                        def S2(k):
                            slot, nk, di, q0, ncol, c0 = geom(k)
                            d_ = stt_[k]
                            b2 = z2b[d_["j"] % len(z2b)]
                            for hh in range(2):
                                kT, qT = kq(hh, c0, nk, q0)
                                P.mm(ps[b2][0:nk, hh * gq:hh * gq + ncol], kT, qT, start=True, stop=False, skip_group_check=True,
                                     reads=[kv_res[slot]] + sbq_res[0:G], writes=[ps_res[b2]])
                                P.mm(ps[b2][0:nk, hh * gq:hh * gq + ncol], negU[0:nk, 0:nk], d_["sp"][0:nk, hh, 0:ncol], start=False, stop=True,
                                     skip_group_check=True, reads=[d_["spr"], const_res], writes=[ps_res[b2]])
                            wT, wTr = bfpair()
                            win = ps[b2][0:nk, 0:2 * gq].rearrange("p (h c) -> p h c", h=2)[:, :, 0:ncol]
                            P.act(wT[0:nk, :, 0:ncol], win, AF.Exp, reads=[ps_res[b2]], writes=[wTr])
                            if di >= 0:
                                P.tt('pool', wT[0:nk, :, 0:nt], wT[0:nk, :, 0:nt], maskSB[0:nk, 0:nt].unsqueeze(1).to_broadcast([nk, 2, nt]), ALU.mult,
                                     reads=[wTr, const_res], writes=[wTr])
                            d_["wT"] = wT; d_["wTr"] = wTr

                        def S3(k):
                            slot, nk, di, q0, ncol, c0 = geom(k)
                            d_ = stt_[k]
                            j = d_["j"]
                            b3 = pob[j % len(pob)]
                            sp, spr, wT, wTr = d_["sp"], d_["spr"], d_["wT"], d_["wTr"]
                            qb0 = q0 // 128
                            po = ps[b3][:, 0:G * 2 * 66].rearrange("p (q h c) -> p q h c", q=G, h=2)
                            for hh in range(2):
                                h = hp + 2 * hh
                                for qb in range(qb0, G):
                                    cc = qb * 128 - q0
                                    P.mm(po[0:nt, qb, hh, 0:64], wT[0:nk, hh, cc:cc + nt], sbV[0:nk, slot, h * 64:(h + 1) * 64],
                                         skip_group_check=True, reads=[wTr, kv_res[slot]], writes=[ps_res[b3]])
                                    P.mm(po[0:nt, qb, hh, 64:66], sp[0:nk, hh, cc:cc + nt], ones1[0:nk, 0:2],
                                         skip_group_check=True, reads=[spr, const_res], writes=[ps_res[b3]])
                            acc = accsb[0:nt, qb0:G, hp:4:2, :]
                            if first[0]:
                                P.copy('dve', acc, po[0:nt, qb0:G, :, 0:64], reads=[ps_res[b3]], writes=ares)
                            else:
                                fx = fexp[0:nt, j % 2, :].rearrange("p (q h) -> p q h", h=2)[:, qb0:G, :]
                                P.act(fx, po[0:nt, qb0:G, :, 64], AF.Exp, scale=-1.0, reads=[ps_res[b3]], writes=[fexp_res[j % 2]])
                                P.tt('dve', acc, acc, fx.unsqueeze(3).to_broadcast([nt, G - qb0, 2, 64]), ALU.mult,
                                     reads=ares + [fexp_res[j % 2]], writes=ares)
                                P.tt('dve', acc, acc, po[0:nt, qb0:G, :, 0:64], ALU.add, reads=ares + [ps_res[b3]], writes=ares)
                            first[0] = False
                            stt_[k] = None

                        if stages_only:
                            return n, S1, S2, S3
                        for step in range(n + 2):
                            if step >= 2:
                                S3(step - 2)
                            if step < n:
                                S1(step)
                            if 1 <= step <= n:
                                S2(step - 1)

                    mla_out = []
                    cnt["f32"] = 0
                    for qb in range(G):
                        yb, ybr = f32buf()
                        mla_out.append((yb, ybr))

                    def finish_mla():
                        for qb in range(G):
                            yb, ybr = mla_out[qb]
                            P.act(xb[0:nt, 0:512], yb[0:nt, :], AF.Square, accum_out=stat[0:nt, 42:43], reads=[ybr], writes=[xb_res, stat_res])
                            rstd_from_ss(42, 43, 512, nt)
                            P.ts('dve', y_t[0:nt, qb, 256:768], yb[0:nt, :], stat[0:nt, 43:44], None, ALU.mult,
                                 reads=[ybr, stat_res], writes=[y_res[qb]])

                    def finish_sb():
                        for qb in range(G):
                            yc = accsb[0:nt, qb, :, :]
                            sq2, sq2r = bfbuf()
                            P.act(sq2[0:nt, 0:256].rearrange("p (h d) -> p h d", h=4), yc, AF.Square, accum_out=stat[0:nt, 44:45],
                                  reads=accsb_res, writes=[sq2r, stat_res])
                            rstd_from_ss(44, 45, 256, nt)
                            P.ts('dve', y_t[0:nt, qb, 768:1024].rearrange("p (h d) -> p h d", h=4), yc, stat[0:nt, 45:46], None, ALU.mult,
                                 reads=accsb_res + [stat_res], writes=[y_res[qb]])

                    if prompt:
                        keys = key_list_prompt()
                        cnt["f32fix"] = 2
                        cnt["mla_pool"] = True
                        for hp_ in range(2):
                            n_, M1, M2, Mfin = mla_attend(hp_, keys, [dict()], True, (1, 2), (0,), stages_only=True)
                            n2_, B1, B2, B3 = sb_attend(hp_, keys, [True], ((3, 4), (5, 6), (7,)), stages_only=True)
                            for step in range(n_ + 2):
                                if step >= 2:
                                    B3(step - 2)
                                if 1 <= step <= n_:
                                    M2(step - 1)
                                if step < n_:
                                    B1(step)
                                    M1(step)
                                if 1 <= step <= n_:
                                    B2(step - 1)
                            Mfin()
                        cnt["f32fix"] = None
                        cnt["mla_pool"] = False
                        finish_mla()
                        finish_sb()
                    else:
                        firsts_m = [[dict()]] * 4
                        firsts_s = [[True] for _ in range(4)]
                        npg = PASTL // 512

                        def ingest_group(kg):
                            slots = [(kg % 2) * 4 + i for i in range(4)]
                            for i, s_ in enumerate(slots):
                                r0 = kg * 512 + i * 128
                                sti = i % 2
                                P.load(kvst[:, sti, 0:256], c_ckv[l, r0:r0 + 128, :], f"cin{sti}", writes=[kvst_res[sti]])
                                P.load(kvst[:, sti, 256:320], c_kr[l, r0:r0 + 128, :], f"cin{sti}", writes=[kvst_res[sti]])
                                P.load(kvst[:, sti, 320:576], c_k[l, r0:r0 + 128, :], f"cin{sti}", writes=[kvst_res[sti]])
                                P.load(kvst[:, sti, 576:832], c_v[l, r0:r0 + 128, :], f"cin{sti}", writes=[kvst_res[sti]])
                                ingest_kv(s_, 128, sti)
                            project_keys(slots, 128, (0, 1), (0, 1))
                            return slots

                        nxt_slots = ingest_group(0)
                        for kg in range(npg):
                            slots = nxt_slots
                            if kg + 1 < npg:
                                nxt_slots = ingest_group(kg + 1)
                            keys = [(s_, 128, -1) for s_ in slots]
                            for hp_ in range(2):
                                mla_attend(hp_, keys, firsts_m[hp_], False, (4 + hp_,), (2,))
                                sb_attend(hp_, keys, firsts_s[hp_], ((3,), (6,), (7,)))
                        keys = [(8, nt, 0)]
                        for hp_ in range(2):
                            mla_attend(hp_, keys, firsts_m[hp_], True, (4 + hp_,), (2,))
                        finish_mla()
                        for hp_ in range(2):
                            sb_attend(hp_, keys, firsts_s[hp_], ((3,), (6,), (7,)))
                        finish_sb()

                    P.mark(f"{kind}{seq_i} L{l} g{g} phaseD")
                    yTs = [(yT[:, 0, :, :], yT_res[0]), (xb[:, :].rearrange("p (k c) -> p k c", k=8), xb_res)]
                    for i, t in enumerate(tiles):
                        yv, yr = yTs[i % 2]
                        pst = ps[6 + (i % 2)][:].bitcast(BF16)
                        for k in range(8):
                            P.tr(pst[:, k * 128:k * 128 + nt], y_t[0:nt, i, k * 128:(k + 1) * 128], ident[0:nt, 0:nt],
                                 reads=[y_res[i], const_res], writes=[ps_res[6 + (i % 2)]])
                        P.copy('act' if i % 2 == 0 else 'dve', yv[:, :, 0:nt], pst[:, :].rearrange("p (k c) -> p k c", k=8)[:, :, 0:nt],
                               reads=[ps_res[6 + (i % 2)]], writes=[yr])
                    for c in range(4):
                        wv, wr = STR.get((l, 'out', c))
                        for i, t in enumerate(tiles):
                            yv, yr = yTs[i % 2]
                            b = (c * G + i) % 4
                            for k in range(8):
                                P.mm(ps[b][0:nt, 0:256], yv[:, k, 0:nt], wv[:, k, :], start=(k == 0), stop=(k == 7),
                                     reads=[yr, wr], writes=[ps_res[b]])
                            xv = x_t[0:nt, t, c * 256:(c + 1) * 256]
                            P.stt(xv, xv, ALPHA, ps[b][0:nt, 0:256], ALU.mult, ALU.add, reads=[x_res[t], ps_res[b]], writes=[x_res[t]])
                        STR.release()
                    for i, t in enumerate(tiles):
                        layer_norm_tile(t, nt, l)

                P.mark(f"{kind}{seq_i} L{l} MLP")
                P.memset('pool', stat[:, 61:62], 0.0, writes=kv_res + [x1T_res] + accsb_res + hT_res)
                P.load(lnb[:, 0, :], W["b_down"][l:l + 1, :].to_broadcast([128, 1024]), "lnb0", writes=[lnb_res[0]])
                x1T_lo = knT
                x1T_hi = vp[:].rearrange("p a b c -> p (a b c)")[:, 0:4 * KS * 128].rearrange("p (k c) -> p k c", k=4)

                class X1:
                    pass

                for t in range(NT):
                    P.copy('act', xb[0:nt, :], x_t[0:nt, t, :], reads=[x_res[t]], writes=[xb_res])
                    pst = ps[7][:].bitcast(BF16)
                    for k in range(8):
                        P.tr(pst[:, k * 128:k * 128 + nt], xb[0:nt, k * 128:(k + 1) * 128], ident[0:nt, 0:nt],
                             reads=[xb_res, const_res], writes=[ps_res[7]])
                    P.copy('dve', x1T_lo[:, :, t * 128:t * 128 + nt], pst[:, 0:512].rearrange("p (k c) -> p k c", k=4)[:, :, 0:nt],
                           reads=[ps_res[7]], writes=[x1T_res])
                    P.copy('dve', x1T_hi[:, :, t * 128:t * 128 + nt], pst[:, 512:1024].rearrange("p (k c) -> p k c", k=4)[:, :, 0:nt],
                           reads=[ps_res[7]], writes=[x1T_res])
                    xv = x_t[0:nt, t, :]
                    P.stt(xv, xv, ALPHA, lnb[0:nt, 0, :], ALU.mult, ALU.add, reads=[x_res[t], lnb_res[0]], writes=[x_res[t]])
                load_ln(l, 2)

                def x1T_ap(k, c0, n):
                    return (x1T_lo if k < 4 else x1T_hi)[:, k % 4, c0:c0 + n]

                for e8 in range(8):
                    wup = [STR.get((l, 'up', e8, hh)) for hh in range(2)]
                    wdn = [STR.get((l, 'dn', e8, hh)) for hh in range(2)]
                    MG = min(4, NT)
                    mq = MG * nt
                    for g in range(NT // MG):
                        c0 = g * MG * 128
                        for fc in range(4):
                            b = fc % 2
                            wv, wr = wup[fc // 2]
                            for k in range(8):
                                P.mm(ps[b][:, 0:mq], wv[:, k, (fc % 2) * 128:(fc % 2) * 128 + 128], x1T_ap(k, c0, mq),
                                     start=(k == 0), stop=(k == 7), reads=[x1T_res, wr], writes=[ps_res[b]])
                            r_, rr = f32buf()
                            f_idx = e8 * 4 + fc
                            P.act(r_[:, 0:mq], ps[b][:, 0:mq], AF.Relu, bias=bup[:, f_idx:f_idx + 1], reads=[ps_res[b], gains_res], writes=[rr])
                            P.act(hT[:, fc, 0:mq], r_[:, 0:mq], AF.Square, reads=[rr], writes=[hT_res[fc]])
                        if g == NT // MG - 1:
                            STR.release(2)
                        for i in range(MG):
                            t = g * MG + i
                            for nh in range(2):
                                b = 2 + (i * 2 + nh) % 4
                                for fc in range(4):
                                    wv, wr = wdn[fc // 2]
                                    P.mm(ps[b][0:nt, :], hT[:, fc, i * 128:i * 128 + nt], wv[:, fc % 2, nh * 512:(nh + 1) * 512],
                                         start=(fc == 0), stop=(fc == 3), reads=[hT_res[fc], wr], writes=[ps_res[b]])
                                xv = x_t[0:nt, t, nh * 512:(nh + 1) * 512]
                                P.tt('dve', xv, xv, ps[b][0:nt, :], ALU.add, reads=[x_res[t], ps_res[b]], writes=[x_res[t]])
                    STR.release(2)
                for t in range(NT):
                    layer_norm_tile(t, nt, l)
                    if l == NL - 1:
                        if prompt:
                            P.load(yp[seq_i, t * 128:(t + 1) * 128, :], x_t[:, t, :], "yout", reads=[x_res[t]])
                        else:
                            P.load(ys, x_t[0:nt, 0, :], "yout", reads=[x_res[0]])

        for s_i in range(nseq):
            run_pass("p", s_i)
        if do_sample:
            run_pass("s", 0)
        if os.environ.get("K_MARKS"):
            for m in P.marks:
                print("MARK", m)
            print("TOTAL OPS", P.nrec, {e: len(P.ops[e]) for e in P.ENG})
        P.emit(st)
    return nc


def rope_tables(pos):
    half = 32
    inv = (np.float32(10000.0) ** (-np.arange(half, dtype=np.float32) / np.float32(half))).astype(np.float32)
    ang = pos.astype(np.float32)[:, None] * inv[None, :]
    cos = np.cos(ang).astype(np.float32)
    sin = np.sin(ang).astype(np.float32)
    cosF = np.concatenate([cos, cos], 1).T * np.float32(MLA_SCALE)
    sinF = np.concatenate([sin, sin], 1).T * np.float32(MLA_SCALE)
    return cos, sin, np.ascontiguousarray(cosF.astype(np.float32)), np.ascontiguousarray(sinF.astype(np.float32))


_CACHE = {}


def kernel(**inputs):
    x_prompt = np.asarray(inputs["x_prompt"], np.float32)
    x_sample = np.asarray(inputs["x_sample"], np.float32)
    B, S, _ = x_prompt.shape
    NL = inputs["w_in"].shape[0]
    PASTL = inputs["cache_mla_ckv"].shape[2]
    nseq = B // N_CORES
    key = (S, NL, PASTL, nseq)
    if key not in _CACHE:
        _CACHE[key] = build_program(S, NL, PASTL, nseq)
    nc = _CACHE[key]
    tp = rope_tables(np.arange(S))
    tsm = rope_tables(PASTL + np.arange(NS))
    shared = {}
    for k in WNAMES:
        a = np.ascontiguousarray(np.asarray(inputs[k], np.float32))
        if k == "w_uq":
            a = a.reshape(NL, 384, 768)
        shared[k] = a
    for nm, arr in zip(["tp_cos", "tp_sin", "tp_cosF", "tp_sinF"], tp):
        shared[nm] = arr
    for nm, arr in zip(["ts_cos", "ts_sin", "ts_cosF", "ts_sinF"], tsm):
        shared[nm] = arr
    in_maps = []
    for c in range(N_CORES):
        m = dict(shared)
        m["xp"] = np.ascontiguousarray(x_prompt[c * nseq:(c + 1) * nseq])
        m["xs"] = np.ascontiguousarray(x_sample[c])
        m["c_ckv"] = np.ascontiguousarray(np.asarray(inputs["cache_mla_ckv"], np.float32)[:, c])
        m["c_kr"] = np.ascontiguousarray(np.asarray(inputs["cache_mla_krope"], np.float32)[:, c])
        m["c_k"] = np.ascontiguousarray(np.asarray(inputs["cache_sb_k"], np.float32)[:, c].reshape(NL, PASTL, 256))
        m["c_v"] = np.ascontiguousarray(np.asarray(inputs["cache_sb_v"], np.float32)[:, c].reshape(NL, PASTL, 256))
        in_maps.append(m)
    res = run_bass_kernel_spmd(nc, in_maps, core_ids=list(range(N_CORES)))
    R = res.results
    y_p = np.concatenate([r["yp"] for r in R], 0)
    y_s = np.stack([r["ys"] for r in R], 0)
    ckv_p = np.concatenate([r["o_ckv_p"] for r in R], 1)
    kr_p = np.concatenate([r["o_kr_p"] for r in R], 1)
    k_p = np.concatenate([r["o_k_p"] for r in R], 1).reshape(NL, B, S, 4, 64)
    v_p = np.concatenate([r["o_v_p"] for r in R], 1).reshape(NL, B, S, 4, 64)
    ckv_s = np.stack([r["o_ckv_s"] for r in R], 1)
    kr_s = np.stack([r["o_kr_s"] for r in R], 1)
    k_s = np.stack([r["o_k_s"] for r in R], 1).reshape(NL, len(R), NS, 4, 64)
    v_s = np.stack([r["o_v_s"] for r in R], 1).reshape(NL, len(R), NS, 4, 64)
    gv_s = np.stack([r["o_gv_s"] for r in R], 1).reshape(NL, len(R), NS, 4, 64)
    return (y_p, y_s, ckv_p, kr_p, k_p, v_p, ckv_s, kr_s, k_s, v_s, gv_s)
```

```python
import os
import numpy as np
from contextlib import ExitStack
import concourse.bass as bass
import concourse.mybir as mybir
from concourse.bass_utils import run_bass_kernel_spmd

F32 = mybir.dt.float32
BF16 = mybir.dt.bfloat16
AF = mybir.ActivationFunctionType
ALU = mybir.AluOpType
AX = mybir.AxisListType

D = 1024
NLAYERS = 4
SEQ = 2048
PAST = 4096
NS = 16
DFF = 4096
WIN = 1984
ALPHA = float((2 * 4) ** 0.25)
EPS = 1e-5
MLA_SCALE = float(192 ** -0.5)
SB_SCALE = float(64 ** -0.5)
N_CORES = 8


class Res:
    __slots__ = ("name", "w", "r")

    def __init__(self, name=""):
        self.name = name
        self.w = None
        self.r = {}


class Prog:
    ENG = ("pe", "act", "dve", "pool", "sp")
    EPOCH = 30000

    def __init__(self, nc, same_engine_sync=True):
        self.nc = nc
        self.ops = {e: [] for e in self.ENG}
        self.dma_cnt = {}
        self.dma_cnt_raw = {}
        self.dma_maxwait = {}
        self.same_engine_sync = same_engine_sync
        self.nrec = 0
        self.limit = int(os.environ.get("K_LIMIT", "0")) or None
        self.marks = []
        self.trace_lines = bool(os.environ.get("K_TRACE"))

    def _collect(self, eng, reads, writes):
        toks = []
        for r in reads:
            if r.w is not None:
                toks.append(r.w)
        for w in writes:
            if w.w is not None:
                toks.append(w.w)
            toks.extend(w.r.values())
        waits = []
        for t in toks:
            if t[0] == 'e':
                if t[1] == eng and (eng == 'pe' or not self.same_engine_sync):
                    continue
                self.ops[t[1]][t[2]][2] = True
                waits.append(t)
            else:
                v = self.dma_cnt[t[1]] * 16
                waits.append(('d', t[1], v))
                if self.dma_maxwait.get(t[1], 0) < v:
                    self.dma_maxwait[t[1]] = v
        return waits

    def mark(self, name):
        self.marks.append((name, self.nrec, len(self.ops['pe'])))

    def op(self, eng, fn, reads=(), writes=()):
        self.nrec += 1
        if self.trace_lines:
            import sys as _s
            f = _s._getframe(1)
            ln = []
            while f is not None and len(ln) < 3:
                ln.append(f.f_lineno); f = f.f_back
            print("OP", self.nrec, eng, ln)
        if self.limit is not None and self.nrec > self.limit:
            return None
        waits = self._collect(eng, reads, writes)
        idx = len(self.ops[eng])
        self.ops[eng].append([fn, waits, False, None])
        tok = ('e', eng, idx)
        for r in reads:
            r.r[eng] = tok
        for w in writes:
            w.w = tok
            w.r = {}
        return tok

    def dma(self, q, fn, sem, reads=(), writes=()):
        self.nrec += 1
        if self.trace_lines:
            import sys as _s
            f = _s._getframe(1)
            ln = []
            while f is not None and len(ln) < 3:
                ln.append(f.f_lineno); f = f.f_back
            print("OP", self.nrec, "dma:" + sem, ln)
        if self.limit is not None and self.nrec > self.limit:
            return None
        sem = f"{sem}_{self.dma_cnt_raw.get(sem, 0) // 1500}"
        base = sem.rsplit("_", 1)[0]
        self.dma_cnt_raw[base] = self.dma_cnt_raw.get(base, 0) + 1
        waits = self._collect(q, reads, writes)
        if self.dma_maxwait.get(sem, 0) > 0:
            waits.append(('d', sem, self.dma_maxwait[sem]))
        self.ops[q].append([fn, waits, False, sem])
        self.dma_cnt[sem] = self.dma_cnt.get(sem, 0) + 1
        tok = ('d', sem, self.dma_cnt[sem] * 16)
        for r in reads:
            r.r['d' + sem] = tok
        for w in writes:
            w.w = tok
            w.r = {}
        return tok

    def emit(self, stack):
        nc = self.nc
        E = self.EPOCH
        sig = {}
        nsig = {}
        for e in self.ENG:
            n = 0
            for i, o in enumerate(self.ops[e]):
                if o[2]:
                    n += 1
                    sig[(e, i)] = n
            nsig[e] = n
        semh = {}
        for e in self.ENG:
            for k in range((nsig[e] + E - 1) // E):
                semh[('e', e, k)] = stack.enter_context(nc.semaphore(f"s_{e}_{k}"))
        for name in self.dma_cnt:
            semh[('d', name)] = stack.enter_context(nc.semaphore(f"d_{name}"))
        block = stack.enter_context(nc.Block())
        ops = self.ops
        dma_cnt = self.dma_cnt

        def run(e, eng):
            waited = {}
            for i, (fn, waits, signal, dsem) in enumerate(ops[e]):
                need = {}
                for t in waits:
                    if t[0] == 'e':
                        n = sig[(t[1], t[2])]
                        key = ('e', t[1], (n - 1) // E)
                        val = (n - 1) % E + 1
                    else:
                        key = ('d', t[1])
                        val = t[2]
                    if need.get(key, 0) < val:
                        need[key] = val
                for key, val in need.items():
                    if waited.get(key, 0) >= val:
                        continue
                    waited[key] = val
                    eng.wait_ge(semh[key], val)
                ins = fn(eng)
                if signal:
                    n = sig[(e, i)]
                    ins.then_inc(semh[('e', e, (n - 1) // E)], 1)
                if dsem is not None:
                    ins.then_inc(semh[('d', dsem)], 16)
            if e == 'sp':
                for name, c in dma_cnt.items():
                    eng.wait_ge(semh[('d', name)], c * 16)

        @block.tensor
        def _(pe):
            run('pe', pe)

        @block.scalar
        def _(act):
            run('act', act)

        @block.vector
        def _(dve):
            run('dve', dve)

        @block.gpsimd
        def _(pool):
            run('pool', pool)

        @block.sync
        def _(sp):
            run('sp', sp)

    def mm(self, out, lhsT, rhs, start=True, stop=True, reads=(), writes=(), **kw):
        return self.op('pe', lambda e: e.matmul(out, lhsT, rhs, start=start, stop=stop, **kw), reads, writes)

    def tr(self, out, in_, ident, reads=(), writes=()):
        return self.op('pe', lambda e: e.transpose(out, in_, ident), reads, writes)

    def act(self, out, in_, func, reads=(), writes=(), **kw):
        return self.op('act', lambda e: e.activation(out, in_, func, **kw), reads, writes)

    def tt(self, eng, out, in0, in1, op, reads=(), writes=()):
        return self.op(eng, lambda e: e.tensor_tensor(out, in0, in1, op), reads, writes)

    def ts(self, eng, out, in0, s1, s2, op0, op1=None, reads=(), writes=(), **kw):
        if op1 is None:
            return self.op(eng, lambda e: e.tensor_scalar(out, in0, s1, None, op0, **kw), reads, writes)
        return self.op(eng, lambda e: e.tensor_scalar(out, in0, s1, s2, op0, op1, **kw), reads, writes)

    def stt(self, out, in0, scalar, in1, op0, op1, reads=(), writes=(), **kw):
        return self.op('dve', lambda e: e.scalar_tensor_tensor(out, in0, scalar, in1, op0, op1, **kw), reads, writes)

    def copy(self, eng, out, in_, reads=(), writes=()):
        if eng == 'act':
            return self.op('act', lambda e: e.copy(out, in_), reads, writes)
        return self.op(eng, lambda e: e.tensor_copy(out, in_), reads, writes)

    def memset(self, eng, ap, val, writes=()):
        return self.op(eng, lambda e: e.memset(ap, val), (), writes)

    def load(self, out, in_, sem, reads=(), writes=(), q='sp', **kw):
        return self.dma(q, lambda e: e.dma_start(out, in_, **kw), sem, reads, writes)


WNAMES = ["w_in", "w_s", "b_s", "g_cq", "g_ckv", "w_uq", "w_uk", "w_uv", "g_mix", "w_out",
          "ln1_g", "ln1_b", "w_up", "b_up", "w_down", "b_down", "ln2_g", "ln2_b"]
WIN_PIECES = [(0, 256), (256, 256), (512, 256), (768, 128), (896, 256), (1152, 64), (1216, 256), (1472, 256), (1728, 256)]


def build_program(S=SEQ, NL=NLAYERS, PASTL=PAST, nseq=2, do_sample=True):
    nc = bass.Bass("TRN2", target_bir_lowering=False)

    def din(name, shape):
        return nc.dram_tensor(name, list(shape), F32, kind="ExternalInput").ap()

    def dout(name, shape):
        return nc.dram_tensor(name, list(shape), F32, kind="ExternalOutput").ap()

    xp = din("xp", [nseq, S, D])
    xs = din("xs", [NS, D])
    c_ckv = din("c_ckv", [NL, PASTL, 256])
    c_kr = din("c_kr", [NL, PASTL, 64])
    c_k = din("c_k", [NL, PASTL, 256])
    c_v = din("c_v", [NL, PASTL, 256])
    W = {}
    wshapes = {"w_in": [NL, D, WIN], "w_s": [NL, 4, 128, 128], "b_s": [NL, 4, 128], "g_cq": [NL, 384],
               "g_ckv": [NL, 256], "w_uq": [NL, 384, 768], "w_uk": [NL, 4, 256, 128], "w_uv": [NL, 4, 256, 128],
               "g_mix": [NL, 1024], "w_out": [NL, 1024, 1024], "ln1_g": [NL, 1024], "ln1_b": [NL, 1024],
               "w_up": [NL, 1024, DFF], "b_up": [NL, DFF], "w_down": [NL, DFF, 1024], "b_down": [NL, 1024],
               "ln2_g": [NL, 1024], "ln2_b": [NL, 1024]}
    for k in WNAMES:
        W[k] = din(k, wshapes[k])
    tab = {"p": (din("tp_cos", [S, 32]), din("tp_sin", [S, 32]), din("tp_cosF", [64, S]), din("tp_sinF", [64, S])),
           "s": (din("ts_cos", [NS, 32]), din("ts_sin", [NS, 32]), din("ts_cosF", [64, NS]), din("ts_sinF", [64, NS]))}
    yp = dout("yp", [nseq, S, D])
    ys = dout("ys", [NS, D])
    o_p = (dout("o_ckv_p", [NL, nseq, S, 256]), dout("o_kr_p", [NL, nseq, S, 64]),
           dout("o_k_p", [NL, nseq, S, 256]), dout("o_v_p", [NL, nseq, S, 256]))
    o_s = (dout("o_ckv_s", [NL, NS, 256]), dout("o_kr_s", [NL, NS, 64]),
           dout("o_k_s", [NL, NS, 256]), dout("o_v_s", [NL, NS, 256]))
    o_gv = dout("o_gv_s", [NL, NS, 256])

    NTP = S // 128
    KS = max(NTP, 9)
    NPIECE = NL * (9 + 4 + 32)
    wscr = nc.dram_tensor("wscr", [NPIECE, 128, 2048], BF16, kind="Internal").ap()

    with ExitStack() as st:
        P = Prog(nc, same_engine_sync=not bool(os.environ.get("K_NOSES")))

        def sb(name, shape, dt=F32):
            return st.enter_context(nc.sbuf_tensor("sb_" + name, list(shape), dt))

        def RL(name, n):
            return [Res(f"{name}{i}") for i in range(n)]

        x_t = sb("x_t", [128, NTP, D]); x_res = RL("x", NTP)
        GP = 2
        xT = sb("xT", [128, 8, GP * 128], BF16); xT_res = RL("xT", 4)
        y_t = sb("y_t", [128, GP, D], BF16); y_res = RL("y", 4)
        knT = sb("knT", [128, 4, KS * 128], BF16)
        krT = sb("krT", [64, KS * 128], BF16)
        vp = sb("vp", [128, KS, 4, 130], BF16)
        sbKT = sb("sbKT", [128, 2, KS * 128], BF16)
        sbV = sb("sbV", [128, KS, 256], BF16)
        kv_res = RL("kv", KS)
        x1T_res = Res("x1T")
        qnT = sb("qnT", [128, 4, GP * 128], BF16)
        qrT = sb("qrT", [64, 4, GP * 128], BF16)
        sbQT = sb("sbQT", [128, 2, GP * 128], BF16)
        cqT = sb("cqT", [128, 3, GP * 128], BF16)
        ckvT = sb("ckvT", [128, 2, 512], BF16)
        q_res = Res("q"); sbq_res = RL("sbq", 4); cqT_res = RL("cqT", 4); ckvT_res = RL("ckvT", 4)
        wuq = sb("wuq", [128, 3, 768], BF16); wrot = sb("wrot", [128, 3, 4, 64], BF16)
        wuk = sb("wuk", [128, 2, 512], BF16); wuv = sb("wuv", [128, 2, 512], BF16)
        wsT = sb("wsT", [128, 4, 128], BF16)
        smallw_res = Res("smallw")
        g_ckv = sb("g_ckv", [128, 256])
        lnb = sb("lnb", [128, 2, 1024]); lnb_res = RL("lnb", 2)
        prm = sb("prm", [128, 47])
        bup = prm[:, 0:32]; bsb = prm[:, 32:36]
        prow_res = None
        identf = sb("identf", [64, 64])
        gains_res = Res("gains")
        NRING = 4
        ring = sb("ring", [128, NRING, 2048], BF16); ring_res = RL("ring", NRING)
        stg = sb("stg", [128, 2, 512]); stg_res = RL("stg", 2)
        tcos = sb("tcos", [128, GP, 32]); tsin = sb("tsin", [128, GP, 32])
        tcosF = sb("tcosF", [64, GP * 128]); tsinF = sb("tsinF", [64, GP * 128])
        tab_res = Res("tab"); tabF_res = Res("tabF")
        ident = sb("ident", [128, 128], BF16); negU = sb("negU", [128, 128], BF16)
        maskSB = sb("maskSB", [128, 128], BF16); ones1 = sb("ones1", [128, 2], BF16)
        const_res = Res("const")
        u_bf = sb("u_bf", [128, GP, 256], BF16); u_res = RL("u", 4)
        f32t = sb("f32t", [128, 3, 512]); f32_res = RL("f32t", 3)
        cqraw = sb("cqraw", [128, GP, 384]); cqraw_res = Res("cqraw")
        kvst = sb("kvst", [128, 2, 832]); kvst_res = RL("kvst", 2)
        bf512 = sb("bf512", [128, 12, 256], BF16)
        bfp_res = RL("bfp", 6)
        bf_res = [bfp_res[i // 2] for i in range(8)]
        kvbf = bf512[:, 8:12, :].rearrange("p a b -> p (a b)")[:, 0:832]
        kvbf_rl = [bfp_res[4], bfp_res[5]]
        wsb = bf512[:, 0:2, :].rearrange("p a (g j) -> p (a g) j", g=2)
        hacc = sb("hacc", [128, 2048], BF16)
        accsb = hacc[:, 0:GP * 512].bitcast(F32).rearrange("p (q h d) -> p q h d", q=GP, h=4); accsb_res = RL("accsb", 4)
        fexp = sb("fexp", [128, 2, 4]); fexp_res = RL("fexp", 2)
        yT = sb("yT", [128, 1, 8, 128], BF16); yT_res = RL("yT", 1)
        xb = sb("xb", [128, 1024], BF16); xb_res = Res("xb")
        hT = hacc[:, :].rearrange("p (f c) -> p f c", f=4); hT_res = RL("hT", 4)
        stat = sb("stat", [128, 64]); stat_res = Res("stat")
        rtmp = sb("rtmp", [128, 4, 32]); rtmp_res = Res("rtmp")
        prow_res = rtmp_res
        prow = rtmp[0:47, :, :].rearrange("p a b -> p (a b)")

        ps = [st.enter_context(nc.psum_tensor(f"ps{i}", [128, 512], F32)) for i in range(8)]
        ps_res = RL("ps", 8)

        P.memset('pool', ident[:], 1.0, writes=[const_res])
        P.op('pool', lambda e: e.affine_select(ident[:], ident[:], pattern=[[-1, 128]], compare_op=ALU.is_equal,
                                               fill=0.0, base=0, channel_multiplier=1), writes=[const_res])
        P.memset('pool', negU[:], -1.0, writes=[const_res])
        P.op('pool', lambda e: e.affine_select(negU[:], negU[:], pattern=[[-1, 128]], compare_op=ALU.is_ge,
                                               fill=0.0, base=0, channel_multiplier=1), writes=[const_res])
        P.memset('pool', maskSB[:], 1.0, writes=[const_res])
        P.op('pool', lambda e: e.affine_select(maskSB[:], maskSB[:], pattern=[[1, 128]], compare_op=ALU.is_gt,
                                               fill=0.0, base=0, channel_multiplier=-1), writes=[const_res])
        P.memset('pool', ones1[:], 1.0, writes=[const_res])
        P.memset('pool', identf[:], 1.0, writes=[const_res])
        P.op('pool', lambda e: e.affine_select(identf[:], identf[:], pattern=[[-1, 64]], compare_op=ALU.is_equal,
                                               fill=0.0, base=0, channel_multiplier=1), writes=[const_res])
        gcqc = prm[:, 36:39]; gmixc = prm[:, 39:47]
        P.memset('pool', vp[:, :, :, 128:130], 1.0, writes=kv_res)

        cnt = {"f32": 0, "bf": 0, "stg": 0, "ring": 0, "mla": 0, "sb": 0, "bfp": 0, "mlap": 0}

        def f32buf():
            if cnt.get("f32fix") is not None:
                i = cnt["f32fix"]
            else:
                i = cnt["f32"] % 3; cnt["f32"] += 1
            return f32t[:, i, :], f32_res[i]

        def bfpair_mla():
            if cnt.get("mla_pool"):
                j = 4 + cnt["mlap"] % 2; cnt["mlap"] += 1
                return bf512[:, 2 * j:2 * j + 2, :], bfp_res[j]
            return bfpair()

        def bfbuf():
            i = cnt["bf"] % 8; cnt["bf"] += 1
            return bf512[:, i, :], bf_res[i]

        class PairRes:
            pass

        def bfpair():
            j = cnt["bfp"] % 4; cnt["bfp"] += 1
            cnt["bf"] = 2 * j + 2
            return bf512[:, 2 * j:2 * j + 2, :], bfp_res[j]

        def quarters(a, b):
            out = []
            if a * b <= 512:
                return [(0, a, 0, b)]
            if b <= 512:
                step = max(1, 512 // b)
                for a0 in range(0, a, step):
                    out.append((a0, min(a, a0 + step), 0, b))
            else:
                for a0 in range(a):
                    for b0 in range(0, b, 512):
                        out.append((a0, a0 + 1, b0, min(b, b0 + 512)))
            return out

        def staged_cast(dst3, src3, a, b, dst_res, scale_cols=None, scale_res=None):
            for (a0, a1, b0, b1) in quarters(a, b):
                si = cnt["stg"] % 2; cnt["stg"] += 1
                na, nb = a1 - a0, b1 - b0
                sview = stg[:, si, 0:na * nb].rearrange("p (a b) -> p a b", a=na)
                P.load(sview, src3[:, a0:a1, b0:b1], f"stg{si}", writes=[stg_res[si]])
                if scale_cols is None:
                    P.copy('pool', dst3[:, a0:a1, b0:b1], sview, reads=[stg_res[si]], writes=[dst_res])
                else:
                    P.tt('pool', dst3[:, a0:a1, b0:b1], sview, scale_cols[:, a0:a1].unsqueeze(2).to_broadcast([128, na, nb]), ALU.mult,
                         reads=[stg_res[si], scale_res], writes=[dst_res])

        def stream_piece(src_ap, shape3, scale_cols=None, scale_res=None):
            a, b = shape3
            ri = cnt["ring"] % NRING; cnt["ring"] += 1
            rview = ring[:, ri, 0:a * b].rearrange("p (a b) -> p a b", a=a)
            staged_cast(rview, src_ap, a, b, ring_res[ri], scale_cols, scale_res)
            return rview, ring_res[ri]

        scr_ids = {}
        scr_res = {}

        class Streamer:
            def __init__(self, specs):
                self.specs = specs
                self.pos_req = 0
                self.pos_get = 0
                self.out = 0
                self.slots = {}

            def request(self, sp):
                key, src, (a, b), scale = sp
                ri = cnt["ring"] % NRING; cnt["ring"] += 1
                rview = ring[:, ri, 0:a * b].rearrange("p (a b) -> p a b", a=a)
                if key in scr_ids:
                    pid = scr_ids[key]
                    P.load(rview, wscr[pid, :, 0:a * b].rearrange("p (a b) -> p a b", a=a), f"ring{ri}",
                           reads=[scr_res[key]], writes=[ring_res[ri]])
                else:
                    pid = len(scr_ids)
                    scr_ids[key] = pid
                    scr_res[key] = Res(f"scr{pid}")
                    if scale:
                        staged_cast(rview, src, a, b, ring_res[ri], gmixc, gains_res)
                    else:
                        staged_cast(rview, src, a, b, ring_res[ri])
                    P.load(wscr[pid, :, 0:a * b].rearrange("p (a b) -> p a b", a=a), rview, f"wst{ri}",
                           reads=[ring_res[ri]], writes=[scr_res[key]])
                return rview, ring_res[ri]

            def top_up(self):
                while self.out < NRING and self.pos_req < len(self.specs):
                    self.slots[self.pos_req] = self.request(self.specs[self.pos_req])
                    self.pos_req += 1
                    self.out += 1

            def get(self, key):
                assert self.specs[self.pos_get][0] == key, (self.specs[self.pos_get][0], key)
                if self.pos_get >= self.pos_req:
                    self.top_up()
                assert self.pos_get < self.pos_req, "ring exhausted (missing release)"
                r = self.slots.pop(self.pos_get)
                self.pos_get += 1
                return r

            def release(self, n=1):
                self.out -= n
                self.top_up()

        def make_specs(NG_):
            sp = []
            for l in range(NL):
                for g in range(NG_):
                    for pi, (c0, ncols) in enumerate(WIN_PIECES):
                        sp.append(((l, 'in', pi), W["w_in"][l][:, c0:c0 + ncols].rearrange("(k p) c -> p k c", p=128), (8, ncols), False))
                    for c in range(4):
                        sp.append(((l, 'out', c), W["w_out"][l][:, c * 256:(c + 1) * 256].rearrange("(k p) c -> p k c", p=128), (8, 256), True))
                for e8 in range(8):
                    for hh in range(2):
                        sp.append(((l, 'up', e8, hh), W["w_up"][l][:, e8 * 512 + hh * 256:e8 * 512 + (hh + 1) * 256].rearrange("(k p) c -> p k c", p=128), (8, 256), False))
                    for hh in range(2):
                        sp.append(((l, 'dn', e8, hh), W["w_down"][l][e8 * 512 + hh * 256:e8 * 512 + (hh + 1) * 256, :].rearrange("(f p) c -> p f c", p=128), (2, 1024), False))
            return sp

        def cast_load(dst_ap, src_ap, shape3, reads_extra=(), dst_res=None):
            a, b = shape3
            staged_cast(dst_ap, src_ap, a, b, dst_res)

        def load_layer_small(l, nt_s):
            for kc in range(3):
                cast_load(wuq[:, kc:kc + 1, :], W["w_uq"][l, kc * 128:(kc + 1) * 128, :].rearrange("p (a b) -> p a b", a=1),
                          (1, 768), dst_res=smallw_res)
            P.load(prow[0:32, :], W["b_up"][l].rearrange("(f p) -> f p", p=128), "prow", writes=[prow_res])
            P.load(prow[32:36, :], W["b_s"][l], "prow", writes=[prow_res])
            P.load(prow[36:39, :], W["g_cq"][l].rearrange("(k p) -> k p", p=128), "prow", writes=[prow_res])
            P.load(prow[39:47, :], W["g_mix"][l].rearrange("(k p) -> k p", p=128), "prow", writes=[prow_res])
            P.tr(ps[7][:, 0:47], prow[:, :], identf[0:47, 0:47], reads=[prow_res, const_res], writes=[ps_res[7]])
            P.copy('dve', prm[:, :], ps[7][:, 0:47], reads=[ps_res[7]], writes=[gains_res])
            P.tt('pool', wuq[:], wuq[:], gcqc.unsqueeze(2).to_broadcast([128, 3, 768]), ALU.mult, reads=[smallw_res, gains_res], writes=[smallw_res])
            wq4 = wuq[:].rearrange("p k (h d) -> p k h d", h=4)
            P.ts('pool', wrot[:, :, :, 0:32], wq4[:, :, :, 160:192], -1.0, None, ALU.mult, reads=[smallw_res], writes=[smallw_res])
            P.copy('pool', wrot[:, :, :, 32:64], wq4[:, :, :, 128:160], reads=[smallw_res], writes=[smallw_res])
            for kc in range(2):
                cast_load(wuk[:, kc, :].rearrange("p (h n) -> p h n", h=4),
                          W["w_uk"][l][:, kc * 128:(kc + 1) * 128, :].rearrange("h c n -> c h n"), (4, 128), dst_res=smallw_res)
                cast_load(wuv[:, kc, :].rearrange("p (h n) -> p h n", h=4),
                          W["w_uv"][l][:, kc * 128:(kc + 1) * 128, :].rearrange("h c n -> c h n"), (4, 128), dst_res=smallw_res)
            cast_load(wsb, W["w_s"][l].rearrange("g i j -> i g j"), (4, 128), dst_res=bfp_res[0])
            for g in range(4):
                pst = ps[6][:].bitcast(BF16)
                P.tr(pst[0:nt_s, g * 128:g * 128 + nt_s], wsb[0:nt_s, g, 0:nt_s], ident[0:nt_s, 0:nt_s],
                     reads=[bfp_res[0], const_res], writes=[ps_res[6]])
            P.copy('dve', wsT[0:nt_s, :, 0:nt_s], ps[6][:].bitcast(BF16)[0:nt_s, 0:512].rearrange("p (g i) -> p g i", g=4)[:, :, 0:nt_s],
                   reads=[ps_res[6]], writes=[smallw_res])
            if nt_s == 128:
                P.memset('pool', wsT[64:128, :, 0:64], 0.0, writes=[smallw_res])
            P.load(g_ckv[:], W["g_ckv"][l:l + 1, :].to_broadcast([128, 256]), "gains", writes=[gains_res])

        def load_ln(l, which):
            names = ("ln1_g", "ln1_b") if which == 1 else ("ln2_g", "ln2_b")
            for i, nm in enumerate(names):
                P.load(lnb[:, i, :], W[nm][l:l + 1, :].to_broadcast([128, 1024]), f"lnb{i}", writes=[lnb_res[i]])

        def rstd_from_ss(col_ss, col_out, n, nt):
            P.act(stat[0:nt, col_out:col_out + 1], stat[0:nt, col_ss:col_ss + 1], AF.Ln, scale=1.0 / n, bias=EPS,
                  reads=[stat_res], writes=[stat_res])
            P.act(stat[0:nt, col_out:col_out + 1], stat[0:nt, col_out:col_out + 1], AF.Exp, scale=-0.5,
                  reads=[stat_res], writes=[stat_res])

        def make_xT(t, slot, nt, dst, dst_res, col0):
            P.copy('act', xb[0:nt, :], x_t[0:nt, t, :], reads=[x_res[t]], writes=[xb_res])
            pst = ps[7][:].bitcast(BF16)
            for k in range(8):
                P.tr(pst[:, k * 128:k * 128 + nt], xb[0:nt, k * 128:(k + 1) * 128], ident[0:nt, 0:nt],
                     reads=[xb_res, const_res], writes=[ps_res[7]])
            P.copy('dve', dst[:, :, col0:col0 + nt], pst[:, :].rearrange("p (k c) -> p k c", k=8)[:, :, 0:nt],
                   reads=[ps_res[7]], writes=[dst_res])

        def layer_norm_tile(t, nt, l):
            xv = x_t[0:nt, t, :]
            P.op('dve', lambda e: e.bn_stats(stat[0:nt, 0:6], x_t[0:nt, t, 0:512]), reads=[x_res[t]], writes=[stat_res])
            P.op('dve', lambda e: e.bn_stats(stat[0:nt, 6:12], x_t[0:nt, t, 512:1024]), reads=[x_res[t]], writes=[stat_res])
            P.op('dve', lambda e: e.bn_aggr(stat[0:nt, 12:14], stat[0:nt, 0:12]), reads=[stat_res], writes=[stat_res])
            P.act(stat[0:nt, 14:15], stat[0:nt, 13:14], AF.Ln, bias=EPS, reads=[stat_res], writes=[stat_res])
            P.act(stat[0:nt, 14:15], stat[0:nt, 14:15], AF.Exp, scale=-0.5, reads=[stat_res], writes=[stat_res])
            P.ts('dve', xv, xv, stat[0:nt, 12:13], stat[0:nt, 14:15], ALU.subtract, ALU.mult,
                 reads=[x_res[t], stat_res], writes=[x_res[t]])
            P.tt('pool', xv, xv, lnb[0:nt, 0, :], ALU.mult, reads=[x_res[t], lnb_res[0]], writes=[x_res[t]])
            P.tt('pool', xv, xv, lnb[0:nt, 1, :], ALU.add, reads=[x_res[t], lnb_res[1]], writes=[x_res[t]])

        def ingest_kv(slot, nk, st_i):
            src = kvst[0:nk, st_i, :]
            P.copy('act', kvbf[0:nk, :], src, reads=[kvst_res[st_i]], writes=kvbf_rl)
            c0 = slot * 128
            pst = ps[6][:].bitcast(BF16)
            P.tr(pst[:, 0:nk], kvbf[0:nk, 0:128], ident[0:nk, 0:nk], reads=[*kvbf_rl, const_res], writes=[ps_res[6]])
            P.tr(pst[:, 128:128 + nk], kvbf[0:nk, 128:256], ident[0:nk, 0:nk], reads=kvbf_rl, writes=[ps_res[6]])
            P.tr(pst[:, 256:256 + nk], kvbf[0:nk, 320:448], ident[0:nk, 0:nk], reads=kvbf_rl, writes=[ps_res[6]])
            P.tr(pst[:, 384:384 + nk], kvbf[0:nk, 448:576], ident[0:nk, 0:nk], reads=kvbf_rl, writes=[ps_res[6]])
            P.tr(pst[0:64, 512:512 + nk], kvbf[0:nk, 256:320], ident[0:nk, 0:nk], reads=kvbf_rl, writes=[ps_res[6]])
            gi = slot % 4
            P.copy('dve', ckvT[:, :, gi * 128:gi * 128 + nk], pst[:, 0:256].rearrange("p (k c) -> p k c", k=2)[:, :, 0:nk],
                   reads=[ps_res[6]], writes=[ckvT_res[gi]])
            P.copy('dve', sbKT[:, :, c0:c0 + nk], pst[:, 256:512].rearrange("p (k c) -> p k c", k=2)[:, :, 0:nk],
                   reads=[ps_res[6]], writes=[kv_res[slot]])
            P.copy('act', krT[:, c0:c0 + nk], pst[0:64, 512:512 + nk], reads=[ps_res[6]], writes=[kv_res[slot]])
            P.copy('pool', sbV[0:nk, slot, :], kvbf[0:nk, 576:832], reads=kvbf_rl, writes=[kv_res[slot]])

        def project_keys(slots, nk, kbanks=(0, 1, 2, 3), vbanks=(4, 5)):
            ncol = (len(slots) - 1) * 128 + nk
            g0 = (slots[0] % 4) * 128
            c0 = slots[0] * 128
            rd = [ckvT_res[s % 4] for s in slots] + [smallw_res]
            for h in range(4):
                b = kbanks[h % len(kbanks)]
                for kc in range(2):
                    P.mm(ps[b][:, 0:ncol], wuk[:, kc, h * 128:(h + 1) * 128], ckvT[:, kc, g0:g0 + ncol],
                         start=(kc == 0), stop=(kc == 1), reads=rd, writes=[ps_res[b]])
                P.copy('act' if h % 2 == 0 else 'dve', knT[:, h, c0:c0 + ncol], ps[b][:, 0:ncol], reads=[ps_res[b]],
                       writes=[kv_res[s] for s in slots])
            for i, s in enumerate(slots):
                nkk = 128 if i < len(slots) - 1 else nk
                b = vbanks[i % len(vbanks)]
                for kc in range(2):
                    P.mm(ps[b][0:nkk, :], ckvT[:, kc, (s % 4) * 128:(s % 4) * 128 + nkk], wuv[:, kc, :],
                         start=(kc == 0), stop=(kc == 1), reads=[ckvT_res[s % 4], smallw_res], writes=[ps_res[b]])
                P.copy('dve' if i % 2 == 0 else 'act', vp[0:nkk, s, :, 0:128], ps[b][0:nkk, :].rearrange("p (h d) -> p h d", h=4),
                       reads=[ps_res[b]], writes=[kv_res[s]])

        def run_pass(kind, seq_i):
            prompt = (kind == "p")
            nt = 128 if prompt else NS
            NT = NTP if prompt else 1
            G = GP if prompt else 1
            NG = NT // G
            gq = G * nt
            tcs, tsn, tcF, tsF = tab[kind]
            STR = Streamer(make_specs(NG))
            if prompt:
                for t in range(NT):
                    P.load(x_t[:, t, :], xp[seq_i, t * 128:(t + 1) * 128, :], f"xin{t % 8}", writes=[x_res[t]])
            else:
                P.load(tcos[0:nt, 0, :], tcs, "tab", writes=[tab_res])
                P.load(tsin[0:nt, 0, :], tsn, "tab", writes=[tab_res])
                P.load(x_t[0:nt, 0, :], xs, "xin", writes=[x_res[0]])

            for l in range(NL):
                P.memset('pool', vp[:, :, :, 128:130], 1.0, writes=kv_res + [x1T_res] + accsb_res + hT_res)
                P.mark(f"{kind}{seq_i} L{l} start")
                load_layer_small(l, nt)
                load_ln(l, 1)
                P.mark(f"{kind}{seq_i} L{l} small loaded")
                new_slot0 = 0 if prompt else 8

                for g in range(NG):
                    tiles = [g * G + i for i in range(G)]
                    if prompt:
                        P.load(tcosF[:, 0:gq], tcF[:, g * gq:(g + 1) * gq], "tabF", writes=[tabF_res])
                        P.load(tsinF[:, 0:gq], tsF[:, g * gq:(g + 1) * gq], "tabF", writes=[tabF_res])
                        P.load(tcos[:, 0:G, :], tcs[g * gq:(g + 1) * gq, :].rearrange("(t p) d -> p t d", p=128), "tab", writes=[tab_res])
                        P.load(tsin[:, 0:G, :], tsn[g * gq:(g + 1) * gq, :].rearrange("(t p) d -> p t d", p=128), "tab", writes=[tab_res])
                    else:
                        P.load(tcosF[:, 0:gq], tcF, "tabF", writes=[tabF_res])
                        P.load(tsinF[:, 0:gq], tsF, "tabF", writes=[tabF_res])
                    for i, t in enumerate(tiles):
                        make_xT(t, i, nt, xT, xT_res[i], i * 128)
                    pbank = [0]

                    def proj(pi):
                        c0, ncols = WIN_PIECES[pi]
                        wv, wr = STR.get((l, 'in', pi))
                        outs = []
                        for i, t in enumerate(tiles):
                            b = pbank[0] % 4; pbank[0] += 1
                            for k in range(8):
                                P.mm(ps[b][0:nt, 0:ncols], xT[:, k, i * 128:i * 128 + nt], wv[:, k, :],
                                     start=(k == 0), stop=(k == 7), reads=[xT_res[i], wr], writes=[ps_res[b]])
                            outs.append((ps[b][0:nt, 0:ncols], ps_res[b]))
                        STR.release()
                        return outs

                    def cons(pi, i, t, pv, pr):
                        slot = (new_slot0 + t) if prompt else 8
                        sti = i % 2
                        if pi == 0:
                            P.act(u_bf[0:nt, i, :], pv, AF.Gelu_apprx_tanh, reads=[pr], writes=[u_res[i]])
                            yield "evac"
                        elif pi == 1:
                            gv, gvr = f32buf()
                            yield
                            gv = gv[0:nt, 0:256]
                            yield
                            P.act(gv, pv, AF.Gelu_apprx_tanh, reads=[pr], writes=[gvr])
                            yield
                            gv3 = gv.rearrange("p (g d) -> p g d", g=4)
                            yield
                            sq, sqr = f32buf()
                            yield
                            sq = sq[0:nt, 0:256]
                            yield
                            P.op('dve', lambda e, gv3=gv3: e.tensor_reduce(stat[0:nt, 16:20], gv3, AX.X, ALU.add), reads=[gvr], writes=[stat_res])
                            yield
                            P.act(sq, gv, AF.Square, reads=[gvr], writes=[sqr])
                            yield
                            P.op('dve', lambda e, sq=sq: e.tensor_reduce(stat[0:nt, 20:24], sq.rearrange("p (g d) -> p g d", g=4), AX.X, ALU.add),
                                 reads=[sqr], writes=[stat_res])
                            yield
                            P.ts('dve', stat[0:nt, 16:20], stat[0:nt, 16:20], 1.0 / 64, None, ALU.mult, reads=[stat_res], writes=[stat_res])
                            yield
                            P.tt('dve', stat[0:nt, 24:28], stat[0:nt, 16:20], stat[0:nt, 16:20], ALU.mult, reads=[stat_res], writes=[stat_res])
                            yield
                            P.stt(stat[0:nt, 20:24], stat[0:nt, 20:24], 1.0 / 64, stat[0:nt, 24:28], ALU.mult, ALU.subtract,
                                  reads=[stat_res], writes=[stat_res])
                            yield
                            P.act(stat[0:nt, 20:24], stat[0:nt, 20:24], AF.Ln, bias=EPS, reads=[stat_res], writes=[stat_res])
                            yield
                            P.act(stat[0:nt, 20:24], stat[0:nt, 20:24], AF.Exp, scale=-0.5, reads=[stat_res], writes=[stat_res])
                            yield
                            P.tt('dve', gv3, gv3, stat[0:nt, 16:20].unsqueeze(2).to_broadcast([nt, 4, 64]), ALU.subtract,
                                 reads=[gvr, stat_res], writes=[gvr])
                            yield
                            P.tt('dve', gv3, gv3, stat[0:nt, 20:24].unsqueeze(2).to_broadcast([nt, 4, 64]), ALU.mult,
                                 reads=[gvr, stat_res], writes=[gvr])
                            yield
                            v_bf, vbf_res = bfbuf()
                            yield
                            P.copy('pool', v_bf[0:nt, :], gv, reads=[gvr], writes=[vbf_res])
                            yield
                            if not prompt:
                                P.load(o_gv[l], gv, "ogv", reads=[gvr])
                            yield
                            yield "defer"
                            for gg in range(4):
                                P.mm(ps[5][0:nt, gg * 64:(gg + 1) * 64], wsT[0:nt, gg, 0:nt], v_bf[0:nt, gg * 64:(gg + 1) * 64],
                                     reads=[vbf_res, smallw_res], writes=[ps_res[5]])
                            yield
                            ya, yar = f32buf()
                            yield
                            ya = ya[0:nt, 0:256]
                            yield
                            for gg in range(4):
                                P.stt(ya[:, gg * 64:(gg + 1) * 64], ps[5][0:nt, gg * 64:(gg + 1) * 64], bsb[0:nt, gg:gg + 1],
                                      u_bf[0:nt, i, gg * 64:(gg + 1) * 64], ALU.add, ALU.mult,
                                      reads=[ps_res[5], gains_res, u_res[i]], writes=[yar])
                            yield
                            sq3, sq3r = bfbuf()
                            P.act(sq3[0:nt, 0:256], ya, AF.Square, accum_out=stat[0:nt, 28:29], reads=[yar], writes=[sq3r, stat_res])
                            yield
                            rstd_from_ss(28, 29, 256, nt)
                            yield
                            P.ts('dve', y_t[0:nt, i, 0:256], ya, stat[0:nt, 29:30], None, ALU.mult,
                                 reads=[yar, stat_res], writes=[y_res[i]])
                            yield
                        elif pi == 2:
                            sq, sqr = f32buf()
                            yield
                            P.copy('dve', cqraw[0:nt, i, 0:256], pv, reads=[pr], writes=[cqraw_res])
                            yield "evac"
                            P.act(sq[0:nt, 0:256], cqraw[0:nt, i, 0:256], AF.Square, accum_out=stat[0:nt, 48 + 2 * i:49 + 2 * i], reads=[cqraw_res], writes=[sqr, stat_res])
                            yield
                        elif pi == 3:
                            sq, sqr = f32buf()
                            yield
                            P.copy('dve', cqraw[0:nt, i, 256:384], pv, reads=[pr], writes=[cqraw_res])
                            yield "evac"
                            P.act(sq[0:nt, 0:128], cqraw[0:nt, i, 256:384], AF.Square, accum_out=stat[0:nt, 49 + 2 * i:50 + 2 * i], reads=[cqraw_res], writes=[sqr, stat_res])
                            yield
                            P.tt('dve', stat[0:nt, 32:33], stat[0:nt, 48 + 2 * i:49 + 2 * i], stat[0:nt, 49 + 2 * i:50 + 2 * i], ALU.add, reads=[stat_res], writes=[stat_res])
                            yield
                            rstd_from_ss(32, 33, 384, nt)
                            yield
                            cqa, cqar = bfbuf()
                            cqb, cqbr = bfbuf()
                            P.ts('dve', cqa[0:nt, 0:256], cqraw[0:nt, i, 0:256], stat[0:nt, 33:34], None, ALU.mult,
                                 reads=[cqraw_res, stat_res], writes=[cqar])
                            P.ts('dve', cqb[0:nt, 0:128], cqraw[0:nt, i, 256:384], stat[0:nt, 33:34], None, ALU.mult,
                                 reads=[cqraw_res, stat_res], writes=[cqbr])
                            yield
                            yield "defer"
                            pst = ps[4][:].bitcast(BF16)
                            yield
                            for kc in range(3):
                                src_ = cqa[0:nt, kc * 128:(kc + 1) * 128] if kc < 2 else cqb[0:nt, 0:128]
                                P.tr(pst[:, kc * 128:kc * 128 + nt], src_, ident[0:nt, 0:nt],
                                     reads=[cqar if kc < 2 else cqbr, const_res], writes=[ps_res[4]])
                            yield
                            P.copy('act', cqT[:, :, i * 128:i * 128 + nt], pst[:, 0:384].rearrange("p (k c) -> p k c", k=3)[:, :, 0:nt],
                                   reads=[ps_res[4]], writes=[cqT_res[i]])
                            yield
                        elif pi == 4:
                            sq, sqr = f32buf()
                            yield
                            raw, rawr = f32buf()
                            yield
                            P.copy('dve', raw[0:nt, 0:256], pv, reads=[pr], writes=[rawr])
                            yield
                            P.act(sq[0:nt, 0:256], raw[0:nt, 0:256], AF.Square, accum_out=stat[0:nt, 34:35], reads=[rawr], writes=[sqr, stat_res])
                            yield
                            rstd_from_ss(34, 35, 256, nt)
                            yield
                            P.stt(kvst[0:nt, sti, 0:256], raw[0:nt, 0:256], stat[0:nt, 35:36], g_ckv[0:nt, :], ALU.mult, ALU.mult,
                                  reads=[rawr, stat_res, gains_res], writes=[kvst_res[sti]])
                            yield
                        elif pi == 5:
                            tt_ = i
                            yield
                            cs_ = tcos[0:nt, tt_, :]; sn_ = tsin[0:nt, tt_, :]
                            yield
                            x1 = pv[:, 0:32]; x2 = pv[:, 32:64]
                            yield
                            P.tt('dve', rtmp[0:nt, 0, :], x1, cs_, ALU.mult, reads=[pr, tab_res], writes=[rtmp_res])
                            yield
                            P.tt('dve', rtmp[0:nt, 1, :], x2, sn_, ALU.mult, reads=[pr, tab_res], writes=[rtmp_res])
                            yield
                            P.tt('dve', rtmp[0:nt, 2, :], x2, cs_, ALU.mult, reads=[pr, tab_res], writes=[rtmp_res])
                            yield
                            P.tt('dve', rtmp[0:nt, 3, :], x1, sn_, ALU.mult, reads=[pr, tab_res], writes=[rtmp_res])
                            yield
                            P.tt('pool', kvst[0:nt, sti, 256:288], rtmp[0:nt, 0, :], rtmp[0:nt, 1, :], ALU.subtract,
                                 reads=[rtmp_res], writes=[kvst_res[sti]])
                            yield
                            P.tt('pool', kvst[0:nt, sti, 288:320], rtmp[0:nt, 2, :], rtmp[0:nt, 3, :], ALU.add,
                                 reads=[rtmp_res], writes=[kvst_res[sti]])
                            yield
                        elif pi == 6:
                            sbqb, sbqb_res = bfbuf()
                            yield
                            P.act(sbqb[0:nt, :], pv, AF.Copy, scale=SB_SCALE, reads=[pr], writes=[sbqb_res])
                            yield "evac"
                            yield "defer"
                            pst = ps[4][:].bitcast(BF16)
                            yield
                            for hp in range(2):
                                P.tr(pst[:, 512 + hp * 128:512 + hp * 128 + nt], sbqb[0:nt, hp * 128:(hp + 1) * 128], ident[0:nt, 0:nt],
                                     reads=[sbqb_res, const_res], writes=[ps_res[4]])
                            yield
                            P.copy('dve', sbQT[:, :, i * 128:i * 128 + nt], pst[:, 512:768].rearrange("p (k c) -> p k c", k=2)[:, :, 0:nt],
                                   reads=[ps_res[4]], writes=[sbq_res[i]])
                            yield
                        elif pi == 7:
                            P.copy('act', kvst[0:nt, sti, 320:576], pv, reads=[pr], writes=[kvst_res[sti]])
                            yield "evac"
                        elif pi == 8:
                            P.copy('dve', kvst[0:nt, sti, 576:832], pv, reads=[pr], writes=[kvst_res[sti]])
                            yield "evac"
                            if prompt:
                                rows = slice(t * 128, (t + 1) * 128)
                                outs = [o[l, seq_i, rows, :] for o in o_p]
                            else:
                                outs = [o[l] for o in o_s]
                            yield
                            for oo, (a0, a1) in zip(outs, [(0, 256), (256, 320), (320, 576), (576, 832)]):
                                P.load(oo, kvst[0:nt, sti, a0:a1], f"okv{sti}", reads=[kvst_res[sti]])
                            yield
                            yield "defer"
                            ingest_kv(slot, nt, sti)
                            yield

                    pending = []
                    nxt = proj(0)
                    for pi in range(len(WIN_PIECES)):
                        cur = nxt
                        if pi + 1 < len(WIN_PIECES):
                            nxt = proj(pi + 1)
                        newg = [cons(pi, i, t, cur[i][0], cur[i][1]) for i, t in enumerate(tiles)]
                        if pi in (0, 2, 3, 6, 7, 8):
                            for g_ in newg:
                                for r_ in g_:
                                    if r_ == "evac":
                                        break
                        active = pending + newg
                        pending = []
                        for g_ in active:
                            for r_ in g_:
                                if r_ == "defer":
                                    pending.append(g_)
                                    break
                    for g_ in pending:
                        for r_ in g_:
                            pass
                    P.mark(f"{kind}{seq_i} L{l} g{g} phaseB")
                    if prompt:
                        project_keys([new_slot0 + t for t in tiles], 128)
                    else:
                        project_keys([8], nt, (0, 1), (0, 1))
                    rdq = cqT_res[0:G] + [smallw_res]
                    for h in range(4):
                        b = h % 2
                        for kc in range(3):
                            P.mm(ps[b][:, 0:gq], wuq[:, kc, h * 192:h * 192 + 128], cqT[:, kc, 0:gq],
                                 start=(kc == 0), stop=(kc == 2), reads=rdq, writes=[ps_res[b]])
                        P.act(qnT[:, h, 0:gq], ps[b][:, 0:gq], AF.Copy, scale=MLA_SCALE, reads=[ps_res[b]], writes=[q_res])
                        for kc in range(3):
                            P.mm(ps[2][0:64, 0:gq], wuq[:, kc, h * 192 + 128:h * 192 + 192], cqT[:, kc, 0:gq],
                                 start=(kc == 0), stop=(kc == 2), reads=rdq, writes=[ps_res[2]])
                        for kc in range(3):
                            P.mm(ps[3][0:64, 0:gq], wrot[:, kc, h, :], cqT[:, kc, 0:gq],
                                 start=(kc == 0), stop=(kc == 2), reads=rdq, writes=[ps_res[3]])
                        t1, t1r = f32buf(); t2, t2r = f32buf()
                        P.tt('dve', t1[0:64, 0:gq], ps[2][0:64, 0:gq], tcosF[:, 0:gq], ALU.mult, reads=[ps_res[2], tabF_res], writes=[t1r])
                        P.tt('dve', t2[0:64, 0:gq], ps[3][0:64, 0:gq], tsinF[:, 0:gq], ALU.mult, reads=[ps_res[3], tabF_res], writes=[t2r])
                        P.tt('pool', qrT[:, h, 0:gq], t1[0:64, 0:gq], t2[0:64, 0:gq], ALU.add, reads=[t1r, t2r], writes=[q_res])

                    P.mark(f"{kind}{seq_i} L{l} g{g} attention")
                    def key_list_prompt():
                        return [(kt, 128, (kt - g * G) if kt >= g * G else -1) for kt in range(g * G + G)]

                    def mla_attend(hp, keys, first, last, accbs, sbanks=(0, 1), stages_only=False):
                        n = len(keys)
                        stt_ = [None] * n

                        def S1(k):
                            slot, nk, di = keys[k]
                            q0 = 0 if di < 0 else di * 128
                            ncol = gq - q0
                            c0 = slot * 128
                            b = sbanks[cnt["mla"] % len(sbanks)]; cnt["mla"] += 1
                            for hh in range(2):
                                h = 2 * hp + hh
                                P.mm(ps[b][0:nk, hh * gq:hh * gq + ncol], knT[:, h, c0:c0 + nk], qnT[:, h, q0:gq], start=True, stop=False,
                                     skip_group_check=True, reads=[kv_res[slot], q_res], writes=[ps_res[b]])
                                P.mm(ps[b][0:nk, hh * gq:hh * gq + ncol], krT[:, c0:c0 + nk], qrT[:, h, q0:gq], start=False, stop=True,
                                     skip_group_check=True, reads=[kv_res[slot], q_res], writes=[ps_res[b]])
                            pT, pTr = bfpair_mla()
                            pin = ps[b][0:nk, 0:2 * gq].rearrange("p (h c) -> p h c", h=2)[:, :, 0:ncol]
                            P.act(pT[0:nk, :, 0:ncol], pin, AF.Exp, reads=[ps_res[b]], writes=[pTr])
                            if di >= 0 and prompt:
                                P.memset('pool', pT[64:128, :, 0:64], 0.0, writes=[pTr])
                            stt_[k] = (pT, pTr, q0)

                        def S2(k):
                            slot, nk, di = keys[k]
                            pT, pTr, q0 = stt_[k]
                            for hh in range(2):
                                h = 2 * hp + hh
                                for qb in range(q0 // 128, G):
                                    ab = accbs[hh] if prompt else accbs[0]
                                    col = (qb % 2) * 129 if prompt else (h % 2) * 129
                                    st_flag = first[0].get(ab, True)
                                    first[0][ab] = False
                                    nq = nt
                                    P.mm(ps[ab][0:nq, col:col + 129], pT[0:nk, hh, qb * 128 - q0:qb * 128 - q0 + nq], vp[0:nk, slot, h, 0:129],
                                         start=st_flag, stop=False, skip_group_check=True, reads=[pTr, kv_res[slot]], writes=[ps_res[ab]])

                        def fin():
                            mla_final(hp, accbs)

                        if stages_only:
                            return n, S1, S2, fin
                        for step in range(n + 1):
                            if step < n:
                                S1(step)
                            if step >= 1:
                                S2(step - 1)
                        if last:
                            fin()

                    def mla_final(hp, accbs):
                        if True:
                            for hh in range(2):
                                h = 2 * hp + hh
                                for qb in range(G):
                                    ab = accbs[hh] if prompt else accbs[0]
                                    col = (qb % 2) * 129 if prompt else (h % 2) * 129
                                    sc = 40 + (cnt["mla"] % 2); cnt["mla"] += 1
                                    P.op('dve', lambda e, ab=ab, col=col, sc=sc: e.reciprocal(stat[0:nt, sc:sc + 1], ps[ab][0:nt, col + 128:col + 129]),
                                         reads=[ps_res[ab]], writes=[stat_res])
                                    yb, ybr = mla_out[qb]
                                    P.act(yb[0:nt, h * 128:(h + 1) * 128], ps[ab][0:nt, col:col + 128], AF.Copy, scale=stat[0:nt, sc:sc + 1],
                                          reads=[ps_res[ab], stat_res], writes=[ybr])

                    def sb_attend(hp, keys, first, banks, stages_only=False):
                        z1b, z2b, pob = banks
                        n = len(keys)
                        stt_ = [None] * n
                        ares = [accsb_res[hp], accsb_res[hp + 2]]

                        def geom(k):
                            slot, nk, di = keys[k]
                            q0 = 0 if di < 0 else di * 128
                            return slot, nk, di, q0, gq - q0, slot * 128

                        def kq(hh, c0, nk, q0):
                            base = 64 * hp
                            return sbKT[base:base + 64, hh, c0:c0 + nk], sbQT[base:base + 64, hh, q0:gq]

                        def S1(k):
                            slot, nk, di, q0, ncol, c0 = geom(k)
                            j = cnt["sb"]; cnt["sb"] += 1
                            b1 = z1b[j % len(z1b)]
                            for hh in range(2):
                                kT, qT = kq(hh, c0, nk, q0)
                                P.mm(ps[b1][0:nk, hh * gq:hh * gq + ncol], kT, qT, skip_group_check=True,
                                     reads=[kv_res[slot]] + sbq_res[0:G], writes=[ps_res[b1]])
                            e_, er = f32buf()
                            ev = e_[:, 0:2 * gq].rearrange("p (h c) -> p h c", h=2)[0:nk, :, 0:ncol]
                            zin = ps[b1][0:nk, 0:2 * gq].rearrange("p (h c) -> p h c", h=2)[:, :, 0:ncol]
                            P.act(ev, zin, AF.Exp, reads=[ps_res[b1]], writes=[er])
                            sp, spr = bfpair()
                            P.act(sp[0:nk, :, 0:ncol], ev, AF.Ln, bias=1.0, reads=[er], writes=[spr])
                            if di >= 0:
                                P.tt('pool', sp[0:nk, :, 0:nt], sp[0:nk, :, 0:nt], maskSB[0:nk, 0:nt].unsqueeze(1).to_broadcast([nk, 2, nt]), ALU.mult,
                                     reads=[spr, const_res], writes=[spr])
                            stt_[k] = dict(sp=sp, spr=spr, j=j)

                        def S2(k):
                            slot, nk, di, q0, ncol, c0 = geom(k)
                            d_ = stt_[k]
                            b2 = z2b[d_["j"] % len(z2b)]
                            for hh in range(2):
                                kT, qT = kq(hh, c0, nk, q0)
                                P.mm(ps[b2][0:nk, hh * gq:hh * gq + ncol], kT, qT, start=True, stop=False, skip_group_check=True,
                                     reads=[kv_res[slot]] + sbq_res[0:G], writes=[ps_res[b2]])
                                P.mm(ps[b2][0:nk, hh * gq:hh * gq + ncol], negU[0:nk, 0:nk], d_["sp"][0:nk, hh, 0:ncol], start=False, stop=True,
                                     skip_group_check=True, reads=[d_["spr"], const_res], writes=[ps_res[b2]])
                            wT, wTr = bfpair()
                            win = ps[b2][0:nk, 0:2 * gq].rearrange("p (h c) -> p h c", h=2)[:, :, 0:ncol]
                            P.act(wT[0:nk, :, 0:ncol], win, AF.Exp, reads=[ps_res[b2]], writes=[wTr])
                            if di >= 0:
                                P.tt('pool', wT[0:nk, :, 0:nt], wT[0:nk, :, 0:nt], maskSB[0:nk, 0:nt].unsqueeze(1).to_broadcast([nk, 2, nt]), ALU.mult,
                                     reads=[wTr, const_res], writes=[wTr])
                            d_["wT"] = wT; d_["wTr"] = wTr

                        def S3(k):
                            slot, nk, di, q0, ncol, c0 = geom(k)
                            d_ = stt_[k]
                            j = d_["j"]
                            b3 = pob[j % len(pob)]
                            sp, spr, wT, wTr = d_["sp"], d_["spr"], d_["wT"], d_["wTr"]
                            qb0 = q0 // 128
                            po = ps[b3][:, 0:G * 2 * 66].rearrange("p (q h c) -> p q h c", q=G, h=2)
                            for hh in range(2):
                                h = hp + 2 * hh
                                for qb in range(qb0, G):
                                    cc = qb * 128 - q0
                                    P.mm(po[0:nt, qb, hh, 0:64], wT[0:nk, hh, cc:cc + nt], sbV[0:nk, slot, h * 64:(h + 1) * 64],
                                         skip_group_check=True, reads=[wTr, kv_res[slot]], writes=[ps_res[b3]])
                                    P.mm(po[0:nt, qb, hh, 64:66], sp[0:nk, hh, cc:cc + nt], ones1[0:nk, 0:2],
                                         skip_group_check=True, reads=[spr, const_res], writes=[ps_res[b3]])
                            acc = accsb[0:nt, qb0:G, hp:4:2, :]
                            if first[0]:
                                P.copy('dve', acc, po[0:nt, qb0:G, :, 0:64], reads=[ps_res[b3]], writes=ares)
                            else:
                                fx = fexp[0:nt, j % 2, :].rearrange("p (q h) -> p q h", h=2)[:, qb0:G, :]
                                P.act(fx, po[0:nt, qb0:G, :, 64], AF.Exp, scale=-1.0, reads=[ps_res[b3]], writes=[fexp_res[j % 2]])
                                P.tt('dve', acc, acc, fx.unsqueeze(3).to_broadcast([nt, G - qb0, 2, 64]), ALU.mult,
                                     reads=ares + [fexp_res[j % 2]], writes=ares)
                                P.tt('dve', acc, acc, po[0:nt, qb0:G, :, 0:64], ALU.add, reads=ares + [ps_res[b3]], writes=ares)
                            first[0] = False
                            stt_[k] = None

                        if stages_only:
                            return n, S1, S2, S3
                        for step in range(n + 2):
                            if 1 <= step <= n:
                                S2(step - 1)
                            if step >= 2:
                                S3(step - 2)
                            if step < n:
                                S1(step)

                    mla_out = []
                    cnt["f32"] = 0
                    for qb in range(G):
                        yb, ybr = f32buf()
                        mla_out.append((yb, ybr))

                    def finish_mla():
                        for qb in range(G):
                            yb, ybr = mla_out[qb]
                            P.act(xb[0:nt, 0:512], yb[0:nt, :], AF.Square, accum_out=stat[0:nt, 42:43], reads=[ybr], writes=[xb_res, stat_res])
                            rstd_from_ss(42, 43, 512, nt)
                            P.ts('dve', y_t[0:nt, qb, 256:768], yb[0:nt, :], stat[0:nt, 43:44], None, ALU.mult,
                                 reads=[ybr, stat_res], writes=[y_res[qb]])

                    def finish_sb():
                        for qb in range(G):
                            yc = accsb[0:nt, qb, :, :]
                            sq2, sq2r = bfbuf()
                            P.act(sq2[0:nt, 0:256].rearrange("p (h d) -> p h d", h=4), yc, AF.Square, accum_out=stat[0:nt, 44:45],
                                  reads=accsb_res, writes=[sq2r, stat_res])
                            rstd_from_ss(44, 45, 256, nt)
                            P.ts('dve', y_t[0:nt, qb, 768:1024].rearrange("p (h d) -> p h d", h=4), yc, stat[0:nt, 45:46], None, ALU.mult,
                                 reads=accsb_res + [stat_res], writes=[y_res[qb]])

                    if prompt:
                        keys = key_list_prompt()
                        cnt["f32fix"] = 2
                        cnt["mla_pool"] = True
                        for hp_ in range(2):
                            n_, M1, M2, Mfin = mla_attend(hp_, keys, [dict()], True, (1, 2), (0,), stages_only=True)
                            n2_, B1, B2, B3 = sb_attend(hp_, keys, [True], ((3, 4), (5, 6), (7,)), stages_only=True)
                            for step in range(n_ + 2):
                                if 1 <= step <= n_:
                                    B2(step - 1)
                                if step >= 2:
                                    B3(step - 2)
                                if 1 <= step <= n_:
                                    M2(step - 1)
                                if step < n_:
                                    B1(step)
                                    M1(step)
                            Mfin()
                        cnt["f32fix"] = None
                        cnt["mla_pool"] = False
                        finish_mla()
                        finish_sb()
                    else:
                        firsts_m = [[dict()]] * 4
                        firsts_s = [[True] for _ in range(4)]
                        npg = PASTL // 512

                        def ingest_group(kg):
                            slots = [(kg % 2) * 4 + i for i in range(4)]
                            for i, s_ in enumerate(slots):
                                r0 = kg * 512 + i * 128
                                sti = i % 2
                                P.load(kvst[:, sti, 0:256], c_ckv[l, r0:r0 + 128, :], f"cin{sti}", writes=[kvst_res[sti]])
                                P.load(kvst[:, sti, 256:320], c_kr[l, r0:r0 + 128, :], f"cin{sti}", writes=[kvst_res[sti]])
                                P.load(kvst[:, sti, 320:576], c_k[l, r0:r0 + 128, :], f"cin{sti}", writes=[kvst_res[sti]])
                                P.load(kvst[:, sti, 576:832], c_v[l, r0:r0 + 128, :], f"cin{sti}", writes=[kvst_res[sti]])
                                ingest_kv(s_, 128, sti)
                            project_keys(slots, 128, (0, 1), (0, 1))
                            return slots

                        nxt_slots = ingest_group(0)
                        for kg in range(npg):
                            slots = nxt_slots
                            if kg + 1 < npg:
                                nxt_slots = ingest_group(kg + 1)
                            keys = [(s_, 128, -1) for s_ in slots]
                            for hp_ in range(2):
                                mla_attend(hp_, keys, firsts_m[hp_], False, (4 + hp_,), (2,))
                                sb_attend(hp_, keys, firsts_s[hp_], ((3,), (6,), (7,)))
                        keys = [(8, nt, 0)]
                        for hp_ in range(2):
                            mla_attend(hp_, keys, firsts_m[hp_], True, (4 + hp_,), (2,))
                        finish_mla()
                        for hp_ in range(2):
                            sb_attend(hp_, keys, firsts_s[hp_], ((3,), (6,), (7,)))
                        finish_sb()

                    P.mark(f"{kind}{seq_i} L{l} g{g} phaseD")
                    yTs = [(yT[:, 0, :, :], yT_res[0]), (xb[:, :].rearrange("p (k c) -> p k c", k=8), xb_res)]
                    for i, t in enumerate(tiles):
                        yv, yr = yTs[i % 2]
                        pst = ps[6 + (i % 2)][:].bitcast(BF16)
                        for k in range(8):
                            P.tr(pst[:, k * 128:k * 128 + nt], y_t[0:nt, i, k * 128:(k + 1) * 128], ident[0:nt, 0:nt],
                                 reads=[y_res[i], const_res], writes=[ps_res[6 + (i % 2)]])
                        P.copy('act' if i % 2 == 0 else 'dve', yv[:, :, 0:nt], pst[:, :].rearrange("p (k c) -> p k c", k=8)[:, :, 0:nt],
                               reads=[ps_res[6 + (i % 2)]], writes=[yr])
                    for c in range(4):
                        wv, wr = STR.get((l, 'out', c))
                        for i, t in enumerate(tiles):
                            yv, yr = yTs[i % 2]
                            b = (c * G + i) % 4
                            for k in range(8):
                                P.mm(ps[b][0:nt, 0:256], yv[:, k, 0:nt], wv[:, k, :], start=(k == 0), stop=(k == 7),
                                     reads=[yr, wr], writes=[ps_res[b]])
                            xv = x_t[0:nt, t, c * 256:(c + 1) * 256]
                            P.stt(xv, xv, ALPHA, ps[b][0:nt, 0:256], ALU.mult, ALU.add, reads=[x_res[t], ps_res[b]], writes=[x_res[t]])
                        STR.release()
                    for i, t in enumerate(tiles):
                        layer_norm_tile(t, nt, l)

                P.mark(f"{kind}{seq_i} L{l} MLP")
                P.memset('pool', stat[:, 61:62], 0.0, writes=kv_res + [x1T_res] + accsb_res + hT_res)
                P.load(lnb[:, 0, :], W["b_down"][l:l + 1, :].to_broadcast([128, 1024]), "lnb0", writes=[lnb_res[0]])
                x1T_lo = knT
                x1T_hi = vp[:].rearrange("p a b c -> p (a b c)")[:, 0:4 * KS * 128].rearrange("p (k c) -> p k c", k=4)

                class X1:
                    pass

                for t in range(NT):
                    P.copy('act', xb[0:nt, :], x_t[0:nt, t, :], reads=[x_res[t]], writes=[xb_res])
                    pst = ps[7][:].bitcast(BF16)
                    for k in range(8):
                        P.tr(pst[:, k * 128:k * 128 + nt], xb[0:nt, k * 128:(k + 1) * 128], ident[0:nt, 0:nt],
                             reads=[xb_res, const_res], writes=[ps_res[7]])
                    P.copy('dve', x1T_lo[:, :, t * 128:t * 128 + nt], pst[:, 0:512].rearrange("p (k c) -> p k c", k=4)[:, :, 0:nt],
                           reads=[ps_res[7]], writes=[x1T_res])
                    P.copy('dve', x1T_hi[:, :, t * 128:t * 128 + nt], pst[:, 512:1024].rearrange("p (k c) -> p k c", k=4)[:, :, 0:nt],
                           reads=[ps_res[7]], writes=[x1T_res])
                    xv = x_t[0:nt, t, :]
                    P.stt(xv, xv, ALPHA, lnb[0:nt, 0, :], ALU.mult, ALU.add, reads=[x_res[t], lnb_res[0]], writes=[x_res[t]])
                load_ln(l, 2)

                def x1T_ap(k, c0, n):
                    return (x1T_lo if k < 4 else x1T_hi)[:, k % 4, c0:c0 + n]

                for e8 in range(8):
                    wup = [STR.get((l, 'up', e8, hh)) for hh in range(2)]
                    wdn = [STR.get((l, 'dn', e8, hh)) for hh in range(2)]
                    MG = min(4, NT)
                    mq = MG * nt
                    for g in range(NT // MG):
                        c0 = g * MG * 128
                        for fc in range(4):
                            b = fc % 2
                            wv, wr = wup[fc // 2]
                            for k in range(8):
                                P.mm(ps[b][:, 0:mq], wv[:, k, (fc % 2) * 128:(fc % 2) * 128 + 128], x1T_ap(k, c0, mq),
                                     start=(k == 0), stop=(k == 7), reads=[x1T_res, wr], writes=[ps_res[b]])
                            r_, rr = f32buf()
                            f_idx = e8 * 4 + fc
                            P.act(r_[:, 0:mq], ps[b][:, 0:mq], AF.Relu, bias=bup[:, f_idx:f_idx + 1], reads=[ps_res[b], gains_res], writes=[rr])
                            P.act(hT[:, fc, 0:mq], r_[:, 0:mq], AF.Square, reads=[rr], writes=[hT_res[fc]])
                        if g == NT // MG - 1:
                            STR.release(2)
                        for i in range(MG):
                            t = g * MG + i
                            for nh in range(2):
                                b = 2 + (i * 2 + nh) % 4
                                for fc in range(4):
                                    wv, wr = wdn[fc // 2]
                                    P.mm(ps[b][0:nt, :], hT[:, fc, i * 128:i * 128 + nt], wv[:, fc % 2, nh * 512:(nh + 1) * 512],
                                         start=(fc == 0), stop=(fc == 3), reads=[hT_res[fc], wr], writes=[ps_res[b]])
                                xv = x_t[0:nt, t, nh * 512:(nh + 1) * 512]
                                P.tt('dve', xv, xv, ps[b][0:nt, :], ALU.add, reads=[x_res[t], ps_res[b]], writes=[x_res[t]])
                    STR.release(2)
                for t in range(NT):
                    layer_norm_tile(t, nt, l)
                    if l == NL - 1:
                        if prompt:
                            P.load(yp[seq_i, t * 128:(t + 1) * 128, :], x_t[:, t, :], "yout", reads=[x_res[t]])
                        else:
                            P.load(ys, x_t[0:nt, 0, :], "yout", reads=[x_res[0]])

        for s_i in range(nseq):
            run_pass("p", s_i)
        if do_sample:
            run_pass("s", 0)
        if os.environ.get("K_MARKS"):
            for m in P.marks:
                print("MARK", m)
            print("TOTAL OPS", P.nrec, {e: len(P.ops[e]) for e in P.ENG})
        P.emit(st)
    return nc


def rope_tables(pos):
    half = 32
    inv = (np.float32(10000.0) ** (-np.arange(half, dtype=np.float32) / np.float32(half))).astype(np.float32)
    ang = pos.astype(np.float32)[:, None] * inv[None, :]
    cos = np.cos(ang).astype(np.float32)
    sin = np.sin(ang).astype(np.float32)
    cosF = np.concatenate([cos, cos], 1).T * np.float32(MLA_SCALE)
    sinF = np.concatenate([sin, sin], 1).T * np.float32(MLA_SCALE)
    return cos, sin, np.ascontiguousarray(cosF.astype(np.float32)), np.ascontiguousarray(sinF.astype(np.float32))


_CACHE = {}


def kernel(**inputs):
    x_prompt = np.asarray(inputs["x_prompt"], np.float32)
    x_sample = np.asarray(inputs["x_sample"], np.float32)
    B, S, _ = x_prompt.shape
    NL = inputs["w_in"].shape[0]
    PASTL = inputs["cache_mla_ckv"].shape[2]
    nseq = B // N_CORES
    key = (S, NL, PASTL, nseq)
    if key not in _CACHE:
        _CACHE[key] = build_program(S, NL, PASTL, nseq)
    nc = _CACHE[key]
    tp = rope_tables(np.arange(S))
    tsm = rope_tables(PASTL + np.arange(NS))
    shared = {}
    for k in WNAMES:
        a = np.ascontiguousarray(np.asarray(inputs[k], np.float32))
        if k == "w_uq":
            a = a.reshape(NL, 384, 768)
        shared[k] = a
    for nm, arr in zip(["tp_cos", "tp_sin", "tp_cosF", "tp_sinF"], tp):
        shared[nm] = arr
    for nm, arr in zip(["ts_cos", "ts_sin", "ts_cosF", "ts_sinF"], tsm):
        shared[nm] = arr
    in_maps = []
    for c in range(N_CORES):
        m = dict(shared)
        m["xp"] = np.ascontiguousarray(x_prompt[c * nseq:(c + 1) * nseq])
        m["xs"] = np.ascontiguousarray(x_sample[c])
        m["c_ckv"] = np.ascontiguousarray(np.asarray(inputs["cache_mla_ckv"], np.float32)[:, c])
        m["c_kr"] = np.ascontiguousarray(np.asarray(inputs["cache_mla_krope"], np.float32)[:, c])
        m["c_k"] = np.ascontiguousarray(np.asarray(inputs["cache_sb_k"], np.float32)[:, c].reshape(NL, PASTL, 256))
        m["c_v"] = np.ascontiguousarray(np.asarray(inputs["cache_sb_v"], np.float32)[:, c].reshape(NL, PASTL, 256))
        in_maps.append(m)
    res = run_bass_kernel_spmd(nc, in_maps, core_ids=list(range(N_CORES)))
    R = res.results
    y_p = np.concatenate([r["yp"] for r in R], 0)
    y_s = np.stack([r["ys"] for r in R], 0)
    ckv_p = np.concatenate([r["o_ckv_p"] for r in R], 1)
    kr_p = np.concatenate([r["o_kr_p"] for r in R], 1)
    k_p = np.concatenate([r["o_k_p"] for r in R], 1).reshape(NL, B, S, 4, 64)
    v_p = np.concatenate([r["o_v_p"] for r in R], 1).reshape(NL, B, S, 4, 64)
    ckv_s = np.stack([r["o_ckv_s"] for r in R], 1)
    kr_s = np.stack([r["o_kr_s"] for r in R], 1)
    k_s = np.stack([r["o_k_s"] for r in R], 1).reshape(NL, len(R), NS, 4, 64)
    v_s = np.stack([r["o_v_s"] for r in R], 1).reshape(NL, len(R), NS, 4, 64)
    gv_s = np.stack([r["o_gv_s"] for r in R], 1).reshape(NL, len(R), NS, 4, 64)
    return (y_p, y_s, ckv_p, kr_p, k_p, v_p, ckv_s, kr_s, k_s, v_s, gv_s)
```

```python
import os
import numpy as np
from contextlib import ExitStack
import concourse.bass as bass
import concourse.mybir as mybir
from concourse.bass_utils import run_bass_kernel_spmd

F32 = mybir.dt.float32
BF16 = mybir.dt.bfloat16
AF = mybir.ActivationFunctionType
ALU = mybir.AluOpType
AX = mybir.AxisListType

D = 1024
NLAYERS = 4
SEQ = 2048
PAST = 4096
NS = 16
DFF = 4096
WIN = 1984
ALPHA = float((2 * 4) ** 0.25)
EPS = 1e-5
MLA_SCALE = float(192 ** -0.5)
SB_SCALE = float(64 ** -0.5)
N_CORES = 8


class Res:
    __slots__ = ("name", "w", "r")

    def __init__(self, name=""):
        self.name = name
        self.w = None
        self.r = {}


class Prog:
    ENG = ("pe", "act", "dve", "pool", "sp")
    EPOCH = 30000

    def __init__(self, nc, same_engine_sync=True):
        self.nc = nc
        self.ops = {e: [] for e in self.ENG}
        self.dma_cnt = {}
        self.dma_cnt_raw = {}
        self.dma_maxwait = {}
        self.same_engine_sync = same_engine_sync
        self.nrec = 0
        self.limit = int(os.environ.get("K_LIMIT", "0")) or None
        self.marks = []
        self.trace_lines = bool(os.environ.get("K_TRACE"))

    def _collect(self, eng, reads, writes):
        toks = []
        for r in reads:
            if r.w is not None:
                toks.append(r.w)
        for w in writes:
            if w.w is not None:
                toks.append(w.w)
            toks.extend(w.r.values())
        waits = []
        for t in toks:
            if t[0] == 'e':
                if t[1] == eng and (eng == 'pe' or not self.same_engine_sync):
                    continue
                self.ops[t[1]][t[2]][2] = True
                waits.append(t)
            else:
                v = self.dma_cnt[t[1]] * 16
                waits.append(('d', t[1], v))
                if self.dma_maxwait.get(t[1], 0) < v:
                    self.dma_maxwait[t[1]] = v
        return waits

    def mark(self, name):
        self.marks.append((name, self.nrec, len(self.ops['pe'])))

    def op(self, eng, fn, reads=(), writes=()):
        self.nrec += 1
        if self.trace_lines:
            import sys as _s
            f = _s._getframe(1)
            ln = []
            while f is not None and len(ln) < 3:
                ln.append(f.f_lineno); f = f.f_back
            print("OP", self.nrec, eng, ln)
        if self.limit is not None and self.nrec > self.limit:
            return None
        waits = self._collect(eng, reads, writes)
        idx = len(self.ops[eng])
        self.ops[eng].append([fn, waits, False, None])
        tok = ('e', eng, idx)
        for r in reads:
            r.r[eng] = tok
        for w in writes:
            w.w = tok
            w.r = {}
        return tok

    def dma(self, q, fn, sem, reads=(), writes=()):
        self.nrec += 1
        if self.trace_lines:
            import sys as _s
            f = _s._getframe(1)
            ln = []
            while f is not None and len(ln) < 3:
                ln.append(f.f_lineno); f = f.f_back
            print("OP", self.nrec, "dma:" + sem, ln)
        if self.limit is not None and self.nrec > self.limit:
            return None
        sem = f"{sem}_{self.dma_cnt_raw.get(sem, 0) // 1500}"
        base = sem.rsplit("_", 1)[0]
        self.dma_cnt_raw[base] = self.dma_cnt_raw.get(base, 0) + 1
        waits = self._collect(q, reads, writes)
        if self.dma_maxwait.get(sem, 0) > 0:
            waits.append(('d', sem, self.dma_maxwait[sem]))
        self.ops[q].append([fn, waits, False, sem])
        self.dma_cnt[sem] = self.dma_cnt.get(sem, 0) + 1
        tok = ('d', sem, self.dma_cnt[sem] * 16)
        for r in reads:
            r.r['d' + sem] = tok
        for w in writes:
            w.w = tok
            w.r = {}
        return tok

    def emit(self, stack):
        nc = self.nc
        E = self.EPOCH
        sig = {}
        nsig = {}
        for e in self.ENG:
            n = 0
            for i, o in enumerate(self.ops[e]):
                if o[2]:
                    n += 1
                    sig[(e, i)] = n
            nsig[e] = n
        semh = {}
        for e in self.ENG:
            for k in range((nsig[e] + E - 1) // E):
                semh[('e', e, k)] = stack.enter_context(nc.semaphore(f"s_{e}_{k}"))
        for name in self.dma_cnt:
            semh[('d', name)] = stack.enter_context(nc.semaphore(f"d_{name}"))
        block = stack.enter_context(nc.Block())
        ops = self.ops
        dma_cnt = self.dma_cnt

        def run(e, eng):
            waited = {}
            for i, (fn, waits, signal, dsem) in enumerate(ops[e]):
                need = {}
                for t in waits:
                    if t[0] == 'e':
                        n = sig[(t[1], t[2])]
                        key = ('e', t[1], (n - 1) // E)
                        val = (n - 1) % E + 1
                    else:
                        key = ('d', t[1])
                        val = t[2]
                    if need.get(key, 0) < val:
                        need[key] = val
                for key, val in need.items():
                    if waited.get(key, 0) >= val:
                        continue
                    waited[key] = val
                    eng.wait_ge(semh[key], val)
                ins = fn(eng)
                if signal:
                    n = sig[(e, i)]
                    ins.then_inc(semh[('e', e, (n - 1) // E)], 1)
                if dsem is not None:
                    ins.then_inc(semh[('d', dsem)], 16)
            if e == 'sp':
                for name, c in dma_cnt.items():
                    eng.wait_ge(semh[('d', name)], c * 16)

        @block.tensor
        def _(pe):
            run('pe', pe)

        @block.scalar
        def _(act):
            run('act', act)

        @block.vector
        def _(dve):
            run('dve', dve)

        @block.gpsimd
        def _(pool):
            run('pool', pool)

        @block.sync
        def _(sp):
            run('sp', sp)

    def mm(self, out, lhsT, rhs, start=True, stop=True, reads=(), writes=(), **kw):
        return self.op('pe', lambda e: e.matmul(out, lhsT, rhs, start=start, stop=stop, **kw), reads, writes)

    def tr(self, out, in_, ident, reads=(), writes=()):
        return self.op('pe', lambda e: e.transpose(out, in_, ident), reads, writes)

    def act(self, out, in_, func, reads=(), writes=(), **kw):
        return self.op('act', lambda e: e.activation(out, in_, func, **kw), reads, writes)

    def tt(self, eng, out, in0, in1, op, reads=(), writes=()):
        return self.op(eng, lambda e: e.tensor_tensor(out, in0, in1, op), reads, writes)

    def ts(self, eng, out, in0, s1, s2, op0, op1=None, reads=(), writes=(), **kw):
        if op1 is None:
            return self.op(eng, lambda e: e.tensor_scalar(out, in0, s1, None, op0, **kw), reads, writes)
        return self.op(eng, lambda e: e.tensor_scalar(out, in0, s1, s2, op0, op1, **kw), reads, writes)

    def stt(self, out, in0, scalar, in1, op0, op1, reads=(), writes=(), **kw):
        return self.op('dve', lambda e: e.scalar_tensor_tensor(out, in0, scalar, in1, op0, op1, **kw), reads, writes)

    def copy(self, eng, out, in_, reads=(), writes=()):
        if eng == 'act':
            return self.op('act', lambda e: e.copy(out, in_), reads, writes)
        return self.op(eng, lambda e: e.tensor_copy(out, in_), reads, writes)

    def memset(self, eng, ap, val, writes=()):
        return self.op(eng, lambda e: e.memset(ap, val), (), writes)

    def load(self, out, in_, sem, reads=(), writes=(), q='sp', **kw):
        return self.dma(q, lambda e: e.dma_start(out, in_, **kw), sem, reads, writes)


WNAMES = ["w_in", "w_s", "b_s", "g_cq", "g_ckv", "w_uq", "w_uk", "w_uv", "g_mix", "w_out",
          "ln1_g", "ln1_b", "w_up", "b_up", "w_down", "b_down", "ln2_g", "ln2_b"]
WIN_PIECES = [(0, 256), (256, 256), (512, 256), (768, 128), (896, 256), (1152, 64), (1216, 256), (1472, 256), (1728, 256)]


def build_program(S=SEQ, NL=NLAYERS, PASTL=PAST, nseq=2, do_sample=True):
    nc = bass.Bass("TRN2", target_bir_lowering=False)

    def din(name, shape):
        return nc.dram_tensor(name, list(shape), F32, kind="ExternalInput").ap()

    def dout(name, shape):
        return nc.dram_tensor(name, list(shape), F32, kind="ExternalOutput").ap()

    xp = din("xp", [nseq, S, D])
    xs = din("xs", [NS, D])
    c_ckv = din("c_ckv", [NL, PASTL, 256])
    c_kr = din("c_kr", [NL, PASTL, 64])
    c_k = din("c_k", [NL, PASTL, 256])
    c_v = din("c_v", [NL, PASTL, 256])
    W = {}
    wshapes = {"w_in": [NL, D, WIN], "w_s": [NL, 4, 128, 128], "b_s": [NL, 4, 128], "g_cq": [NL, 384],
               "g_ckv": [NL, 256], "w_uq": [NL, 384, 768], "w_uk": [NL, 4, 256, 128], "w_uv": [NL, 4, 256, 128],
               "g_mix": [NL, 1024], "w_out": [NL, 1024, 1024], "ln1_g": [NL, 1024], "ln1_b": [NL, 1024],
               "w_up": [NL, 1024, DFF], "b_up": [NL, DFF], "w_down": [NL, DFF, 1024], "b_down": [NL, 1024],
               "ln2_g": [NL, 1024], "ln2_b": [NL, 1024]}
    for k in WNAMES:
        W[k] = din(k, wshapes[k])
    tab = {"p": (din("tp_cos", [S, 32]), din("tp_sin", [S, 32]), din("tp_cosF", [64, S]), din("tp_sinF", [64, S])),
           "s": (din("ts_cos", [NS, 32]), din("ts_sin", [NS, 32]), din("ts_cosF", [64, NS]), din("ts_sinF", [64, NS]))}
    yp = dout("yp", [nseq, S, D])
    ys = dout("ys", [NS, D])
    o_p = (dout("o_ckv_p", [NL, nseq, S, 256]), dout("o_kr_p", [NL, nseq, S, 64]),
           dout("o_k_p", [NL, nseq, S, 256]), dout("o_v_p", [NL, nseq, S, 256]))
    o_s = (dout("o_ckv_s", [NL, NS, 256]), dout("o_kr_s", [NL, NS, 64]),
           dout("o_k_s", [NL, NS, 256]), dout("o_v_s", [NL, NS, 256]))
    o_gv = dout("o_gv_s", [NL, NS, 256])

    NTP = S // 128
    KS = max(NTP, 9)
    NPIECE = NL * (9 + 4 + 32)
    wscr = nc.dram_tensor("wscr", [NPIECE, 128, 2048], BF16, kind="Internal").ap()

    with ExitStack() as st:
        P = Prog(nc, same_engine_sync=not bool(os.environ.get("K_NOSES")))

        def sb(name, shape, dt=F32):
            return st.enter_context(nc.sbuf_tensor("sb_" + name, list(shape), dt))

        def RL(name, n):
            return [Res(f"{name}{i}") for i in range(n)]

        x_t = sb("x_t", [128, NTP, D]); x_res = RL("x", NTP)
        GP = 2
        xT = sb("xT", [128, 8, GP * 128], BF16); xT_res = RL("xT", 4)
        y_t = sb("y_t", [128, GP, D], BF16); y_res = RL("y", 4)
        knT = sb("knT", [128, 4, KS * 128], BF16)
        krT = sb("krT", [64, KS * 128], BF16)
        vp = sb("vp", [128, KS, 4, 130], BF16)
        sbKT = sb("sbKT", [128, 2, KS * 128], BF16)
        sbV = sb("sbV", [128, KS, 256], BF16)
        kv_res = RL("kv", KS)
        x1T_res = Res("x1T")
        qnT = sb("qnT", [128, 4, GP * 128], BF16)
        qrT = sb("qrT", [64, 4, GP * 128], BF16)
        sbQT = sb("sbQT", [128, 2, GP * 128], BF16)
        cqT = sb("cqT", [128, 3, GP * 128], BF16)
        ckvT = sb("ckvT", [128, 2, 512], BF16)
        q_res = Res("q"); sbq_res = RL("sbq", 4); cqT_res = RL("cqT", 4); ckvT_res = RL("ckvT", 4)
        wuq = sb("wuq", [128, 3, 768], BF16); wrot = sb("wrot", [128, 3, 4, 64], BF16)
        wuk = sb("wuk", [128, 2, 512], BF16); wuv = sb("wuv", [128, 2, 512], BF16)
        wsT = sb("wsT", [128, 4, 128], BF16)
        smallw_res = Res("smallw")
        g_ckv = sb("g_ckv", [128, 256])
        lnb = sb("lnb", [128, 2, 1024]); lnb_res = RL("lnb", 2)
        prm = sb("prm", [128, 47])
        bup = prm[:, 0:32]; bsb = prm[:, 32:36]
        prow_res = None
        identf = sb("identf", [64, 64])
        gains_res = Res("gains")
        NRING = 4
        ring = sb("ring", [128, NRING, 2048], BF16); ring_res = RL("ring", NRING)
        stg = sb("stg", [128, 2, 512]); stg_res = RL("stg", 2)
        tcos = sb("tcos", [128, GP, 32]); tsin = sb("tsin", [128, GP, 32])
        tcosF = sb("tcosF", [64, GP * 128]); tsinF = sb("tsinF", [64, GP * 128])
        tab_res = Res("tab"); tabF_res = Res("tabF")
        ident = sb("ident", [128, 128], BF16); negU = sb("negU", [128, 128], BF16)
        maskSB = sb("maskSB", [128, 128], BF16); ones1 = sb("ones1", [128, 2], BF16)
        const_res = Res("const")
        u_bf = sb("u_bf", [128, GP, 256], BF16); u_res = RL("u", 4)
        f32t = sb("f32t", [128, 3, 512]); f32_res = RL("f32t", 3)
        cqraw = sb("cqraw", [128, GP, 384]); cqraw_res = Res("cqraw")
        kvst = sb("kvst", [128, 2, 832]); kvst_res = RL("kvst", 2)
        bf512 = sb("bf512", [128, 12, 256], BF16)
        bfp_res = RL("bfp", 6)
        bf_res = [bfp_res[i // 2] for i in range(8)]
        kvbf = bf512[:, 8:12, :].rearrange("p a b -> p (a b)")[:, 0:832]
        kvbf_rl = [bfp_res[4], bfp_res[5]]
        wsb = bf512[:, 0:2, :].rearrange("p a (g j) -> p (a g) j", g=2)
        hacc = sb("hacc", [128, 2048], BF16)
        accsb = hacc[:, 0:GP * 512].bitcast(F32).rearrange("p (q h d) -> p q h d", q=GP, h=4); accsb_res = RL("accsb", 4)
        fexp = sb("fexp", [128, 2, 4]); fexp_res = RL("fexp", 2)
        yT = sb("yT", [128, 1, 8, 128], BF16); yT_res = RL("yT", 1)
        xb = sb("xb", [128, 1024], BF16); xb_res = Res("xb")
        hT = hacc[:, :].rearrange("p (f c) -> p f c", f=4); hT_res = RL("hT", 4)
        stat = sb("stat", [128, 64]); stat_res = Res("stat")
        rtmp = sb("rtmp", [128, 4, 32]); rtmp_res = Res("rtmp")
        prow_res = rtmp_res
        prow = rtmp[0:47, :, :].rearrange("p a b -> p (a b)")

        ps = [st.enter_context(nc.psum_tensor(f"ps{i}", [128, 512], F32)) for i in range(8)]
        ps_res = RL("ps", 8)

        P.memset('pool', ident[:], 1.0, writes=[const_res])
        P.op('pool', lambda e: e.affine_select(ident[:], ident[:], pattern=[[-1, 128]], compare_op=ALU.is_equal,
                                               fill=0.0, base=0, channel_multiplier=1), writes=[const_res])
        P.memset('pool', negU[:], -1.0, writes=[const_res])
        P.op('pool', lambda e: e.affine_select(negU[:], negU[:], pattern=[[-1, 128]], compare_op=ALU.is_ge,
                                               fill=0.0, base=0, channel_multiplier=1), writes=[const_res])
        P.memset('pool', maskSB[:], 1.0, writes=[const_res])
        P.op('pool', lambda e: e.affine_select(maskSB[:], maskSB[:], pattern=[[1, 128]], compare_op=ALU.is_gt,
                                               fill=0.0, base=0, channel_multiplier=-1), writes=[const_res])
        P.memset('pool', ones1[:], 1.0, writes=[const_res])
        P.memset('pool', identf[:], 1.0, writes=[const_res])
        P.op('pool', lambda e: e.affine_select(identf[:], identf[:], pattern=[[-1, 64]], compare_op=ALU.is_equal,
                                               fill=0.0, base=0, channel_multiplier=1), writes=[const_res])
        gcqc = prm[:, 36:39]; gmixc = prm[:, 39:47]
        P.memset('pool', vp[:, :, :, 128:130], 1.0, writes=kv_res)

        cnt = {"f32": 0, "bf": 0, "stg": 0, "ring": 0, "mla": 0, "sb": 0, "bfp": 0, "mlap": 0}

        def f32buf():
            if cnt.get("f32fix") is not None:
                i = cnt["f32fix"]
            else:
                i = cnt["f32"] % 3; cnt["f32"] += 1
            return f32t[:, i, :], f32_res[i]

        def bfpair_mla():
            if cnt.get("mla_pool"):
                j = 4 + cnt["mlap"] % 2; cnt["mlap"] += 1
                return bf512[:, 2 * j:2 * j + 2, :], bfp_res[j]
            return bfpair()

        def bfbuf():
            i = cnt["bf"] % 8; cnt["bf"] += 1
            return bf512[:, i, :], bf_res[i]

        class PairRes:
            pass

        def bfpair():
            j = cnt["bfp"] % 4; cnt["bfp"] += 1
            cnt["bf"] = 2 * j + 2
            return bf512[:, 2 * j:2 * j + 2, :], bfp_res[j]

        def quarters(a, b):
            out = []
            if a * b <= 512:
                return [(0, a, 0, b)]
            if b <= 512:
                step = max(1, 512 // b)
                for a0 in range(0, a, step):
                    out.append((a0, min(a, a0 + step), 0, b))
            else:
                for a0 in range(a):
                    for b0 in range(0, b, 512):
                        out.append((a0, a0 + 1, b0, min(b, b0 + 512)))
            return out

        def staged_cast(dst3, src3, a, b, dst_res, scale_cols=None, scale_res=None):
            for (a0, a1, b0, b1) in quarters(a, b):
                si = cnt["stg"] % 2; cnt["stg"] += 1
                na, nb = a1 - a0, b1 - b0
                sview = stg[:, si, 0:na * nb].rearrange("p (a b) -> p a b", a=na)
                P.load(sview, src3[:, a0:a1, b0:b1], f"stg{si}", writes=[stg_res[si]])
                ceng = 'pool' if si == 0 else 'dve'
                if scale_cols is None:
                    P.copy(ceng, dst3[:, a0:a1, b0:b1], sview, reads=[stg_res[si]], writes=[dst_res])
                else:
                    P.tt('pool', dst3[:, a0:a1, b0:b1], sview, scale_cols[:, a0:a1].unsqueeze(2).to_broadcast([128, na, nb]), ALU.mult,
                         reads=[stg_res[si], scale_res], writes=[dst_res])

        def stream_piece(src_ap, shape3, scale_cols=None, scale_res=None):
            a, b = shape3
            ri = cnt["ring"] % NRING; cnt["ring"] += 1
            rview = ring[:, ri, 0:a * b].rearrange("p (a b) -> p a b", a=a)
            staged_cast(rview, src_ap, a, b, ring_res[ri], scale_cols, scale_res)
            return rview, ring_res[ri]

        scr_ids = {}
        scr_res = {}

        class Streamer:
            def __init__(self, specs):
                self.specs = specs
                self.pos_req = 0
                self.pos_get = 0
                self.out = 0
                self.slots = {}

            def request(self, sp):
                key, src, (a, b), scale = sp
                ri = cnt["ring"] % NRING; cnt["ring"] += 1
                rview = ring[:, ri, 0:a * b].rearrange("p (a b) -> p a b", a=a)
                if key in scr_ids:
                    pid = scr_ids[key]
                    P.load(rview, wscr[pid, :, 0:a * b].rearrange("p (a b) -> p a b", a=a), f"ring{ri}",
                           reads=[scr_res[key]], writes=[ring_res[ri]])
                else:
                    pid = len(scr_ids)
                    scr_ids[key] = pid
                    scr_res[key] = Res(f"scr{pid}")
                    if scale:
                        staged_cast(rview, src, a, b, ring_res[ri], gmixc, gains_res)
                    else:
                        staged_cast(rview, src, a, b, ring_res[ri])
                    P.load(wscr[pid, :, 0:a * b].rearrange("p (a b) -> p a b", a=a), rview, f"wst{ri}",
                           reads=[ring_res[ri]], writes=[scr_res[key]])
                return rview, ring_res[ri]

            def top_up(self):
                while self.out < NRING and self.pos_req < len(self.specs):
                    self.slots[self.pos_req] = self.request(self.specs[self.pos_req])
                    self.pos_req += 1
                    self.out += 1

            def get(self, key):
                assert self.specs[self.pos_get][0] == key, (self.specs[self.pos_get][0], key)
                if self.pos_get >= self.pos_req:
                    self.top_up()
                assert self.pos_get < self.pos_req, "ring exhausted (missing release)"
                r = self.slots.pop(self.pos_get)
                self.pos_get += 1
                return r

            def release(self, n=1):
                self.out -= n
                self.top_up()

        def make_specs(NG_):
            sp = []
            for l in range(NL):
                for g in range(NG_):
                    for pi, (c0, ncols) in enumerate(WIN_PIECES):
                        sp.append(((l, 'in', pi), W["w_in"][l][:, c0:c0 + ncols].rearrange("(k p) c -> p k c", p=128), (8, ncols), False))
                    for c in range(4):
                        sp.append(((l, 'out', c), W["w_out"][l][:, c * 256:(c + 1) * 256].rearrange("(k p) c -> p k c", p=128), (8, 256), True))
                for e8 in range(8):
                    for hh in range(2):
                        sp.append(((l, 'up', e8, hh), W["w_up"][l][:, e8 * 512 + hh * 256:e8 * 512 + (hh + 1) * 256].rearrange("(k p) c -> p k c", p=128), (8, 256), False))
                    for hh in range(2):
                        sp.append(((l, 'dn', e8, hh), W["w_down"][l][e8 * 512 + hh * 256:e8 * 512 + (hh + 1) * 256, :].rearrange("(f p) c -> p f c", p=128), (2, 1024), False))
            return sp

        def cast_load(dst_ap, src_ap, shape3, reads_extra=(), dst_res=None):
            a, b = shape3
            staged_cast(dst_ap, src_ap, a, b, dst_res)

        bup_res = Res("bup")

        def load_bup(l):
            P.load(prow[0:32, :], W["b_up"][l].rearrange("(f p) -> f p", p=128), "prow", writes=[prow_res])
            P.tr(ps[7][:, 0:32], prow[0:32, :], identf[0:32, 0:32], reads=[prow_res, const_res], writes=[ps_res[7]])
            P.copy('dve', prm[:, 0:32], ps[7][:, 0:32], reads=[ps_res[7]], writes=[bup_res])

        def load_layer_small(l, nt_s):
            for kc in range(3):
                cast_load(wuq[:, kc:kc + 1, :], W["w_uq"][l, kc * 128:(kc + 1) * 128, :].rearrange("p (a b) -> p a b", a=1),
                          (1, 768), dst_res=smallw_res)
            P.load(prow[32:36, :], W["b_s"][l], "prow", writes=[prow_res])
            P.load(prow[36:39, :], W["g_cq"][l].rearrange("(k p) -> k p", p=128), "prow", writes=[prow_res])
            P.load(prow[39:47, :], W["g_mix"][l].rearrange("(k p) -> k p", p=128), "prow", writes=[prow_res])
            P.tr(ps[7][:, 32:47], prow[32:47, :], identf[32:47, 32:47], reads=[prow_res, const_res], writes=[ps_res[7]])
            P.copy('dve', prm[:, 32:47], ps[7][:, 32:47], reads=[ps_res[7]], writes=[gains_res])
            P.tt('pool', wuq[:], wuq[:], gcqc.unsqueeze(2).to_broadcast([128, 3, 768]), ALU.mult, reads=[smallw_res, gains_res], writes=[smallw_res])
            wq4 = wuq[:].rearrange("p k (h d) -> p k h d", h=4)
            P.ts('pool', wrot[:, :, :, 0:32], wq4[:, :, :, 160:192], -1.0, None, ALU.mult, reads=[smallw_res], writes=[smallw_res])
            P.copy('pool', wrot[:, :, :, 32:64], wq4[:, :, :, 128:160], reads=[smallw_res], writes=[smallw_res])
            for kc in range(2):
                cast_load(wuk[:, kc, :].rearrange("p (h n) -> p h n", h=4),
                          W["w_uk"][l][:, kc * 128:(kc + 1) * 128, :].rearrange("h c n -> c h n"), (4, 128), dst_res=smallw_res)
                cast_load(wuv[:, kc, :].rearrange("p (h n) -> p h n", h=4),
                          W["w_uv"][l][:, kc * 128:(kc + 1) * 128, :].rearrange("h c n -> c h n"), (4, 128), dst_res=smallw_res)
            cast_load(wsb, W["w_s"][l].rearrange("g i j -> i g j"), (4, 128), dst_res=bfp_res[0])
            for g in range(4):
                pst = ps[6][:].bitcast(BF16)
                P.tr(pst[0:nt_s, g * 128:g * 128 + nt_s], wsb[0:nt_s, g, 0:nt_s], ident[0:nt_s, 0:nt_s],
                     reads=[bfp_res[0], const_res], writes=[ps_res[6]])
            P.copy('dve', wsT[0:nt_s, :, 0:nt_s], ps[6][:].bitcast(BF16)[0:nt_s, 0:512].rearrange("p (g i) -> p g i", g=4)[:, :, 0:nt_s],
                   reads=[ps_res[6]], writes=[smallw_res])
            if nt_s == 128:
                P.memset('pool', wsT[64:128, :, 0:64], 0.0, writes=[smallw_res])
            P.load(g_ckv[:], W["g_ckv"][l:l + 1, :].to_broadcast([128, 256]), "gains", writes=[gains_res])

        def load_ln(l, which):
            names = ("ln1_g", "ln1_b") if which == 1 else ("ln2_g", "ln2_b")
            for i, nm in enumerate(names):
                P.load(lnb[:, i, :], W[nm][l:l + 1, :].to_broadcast([128, 1024]), f"lnb{i}", writes=[lnb_res[i]])

        def rstd_from_ss(col_ss, col_out, n, nt):
            P.act(stat[0:nt, col_out:col_out + 1], stat[0:nt, col_ss:col_ss + 1], AF.Ln, scale=1.0 / n, bias=EPS,
                  reads=[stat_res], writes=[stat_res])
            P.act(stat[0:nt, col_out:col_out + 1], stat[0:nt, col_out:col_out + 1], AF.Exp, scale=-0.5,
                  reads=[stat_res], writes=[stat_res])

        def make_xT(t, slot, nt, dst, dst_res, col0):
            P.copy('act', xb[0:nt, :], x_t[0:nt, t, :], reads=[x_res[t]], writes=[xb_res])
            pst = ps[7][:].bitcast(BF16)
            for k in range(8):
                P.tr(pst[:, k * 128:k * 128 + nt], xb[0:nt, k * 128:(k + 1) * 128], ident[0:nt, 0:nt],
                     reads=[xb_res, const_res], writes=[ps_res[7]])
            P.copy('dve', dst[:, :, col0:col0 + nt], pst[:, :].rearrange("p (k c) -> p k c", k=8)[:, :, 0:nt],
                   reads=[ps_res[7]], writes=[dst_res])

        def layer_norm_tile(t, nt, l):
            xv = x_t[0:nt, t, :]
            P.op('dve', lambda e: e.bn_stats(stat[0:nt, 0:6], x_t[0:nt, t, 0:512]), reads=[x_res[t]], writes=[stat_res])
            P.op('dve', lambda e: e.bn_stats(stat[0:nt, 6:12], x_t[0:nt, t, 512:1024]), reads=[x_res[t]], writes=[stat_res])
            P.op('dve', lambda e: e.bn_aggr(stat[0:nt, 12:14], stat[0:nt, 0:12]), reads=[stat_res], writes=[stat_res])
            P.act(stat[0:nt, 14:15], stat[0:nt, 13:14], AF.Ln, bias=EPS, reads=[stat_res], writes=[stat_res])
            P.act(stat[0:nt, 14:15], stat[0:nt, 14:15], AF.Exp, scale=-0.5, reads=[stat_res], writes=[stat_res])
            P.ts('dve', xv, xv, stat[0:nt, 12:13], stat[0:nt, 14:15], ALU.subtract, ALU.mult,
                 reads=[x_res[t], stat_res], writes=[x_res[t]])
            P.tt('pool', xv, xv, lnb[0:nt, 0, :], ALU.mult, reads=[x_res[t], lnb_res[0]], writes=[x_res[t]])
            P.tt('pool', xv, xv, lnb[0:nt, 1, :], ALU.add, reads=[x_res[t], lnb_res[1]], writes=[x_res[t]])

        def ingest_kv(slot, nk, st_i):
            src = kvst[0:nk, st_i, :]
            P.copy('act', kvbf[0:nk, :], src, reads=[kvst_res[st_i]], writes=kvbf_rl)
            c0 = slot * 128
            pst = ps[6][:].bitcast(BF16)
            P.tr(pst[:, 0:nk], kvbf[0:nk, 0:128], ident[0:nk, 0:nk], reads=[*kvbf_rl, const_res], writes=[ps_res[6]])
            P.tr(pst[:, 128:128 + nk], kvbf[0:nk, 128:256], ident[0:nk, 0:nk], reads=kvbf_rl, writes=[ps_res[6]])
            P.tr(pst[:, 256:256 + nk], kvbf[0:nk, 320:448], ident[0:nk, 0:nk], reads=kvbf_rl, writes=[ps_res[6]])
            P.tr(pst[:, 384:384 + nk], kvbf[0:nk, 448:576], ident[0:nk, 0:nk], reads=kvbf_rl, writes=[ps_res[6]])
            P.tr(pst[0:64, 512:512 + nk], kvbf[0:nk, 256:320], ident[0:nk, 0:nk], reads=kvbf_rl, writes=[ps_res[6]])
            gi = slot % 4
            P.copy('dve', ckvT[:, :, gi * 128:gi * 128 + nk], pst[:, 0:256].rearrange("p (k c) -> p k c", k=2)[:, :, 0:nk],
                   reads=[ps_res[6]], writes=[ckvT_res[gi]])
            P.copy('dve', sbKT[:, :, c0:c0 + nk], pst[:, 256:512].rearrange("p (k c) -> p k c", k=2)[:, :, 0:nk],
                   reads=[ps_res[6]], writes=[kv_res[slot]])
            P.copy('act', krT[:, c0:c0 + nk], pst[0:64, 512:512 + nk], reads=[ps_res[6]], writes=[kv_res[slot]])
            P.copy('pool', sbV[0:nk, slot, :], kvbf[0:nk, 576:832], reads=kvbf_rl, writes=[kv_res[slot]])

        def project_keys(slots, nk, kbanks=(0, 1, 2, 3), vbanks=(4, 5)):
            ncol = (len(slots) - 1) * 128 + nk
            g0 = (slots[0] % 4) * 128
            c0 = slots[0] * 128
            rd = [ckvT_res[s % 4] for s in slots] + [smallw_res]
            for h in range(4):
                b = kbanks[h % len(kbanks)]
                for kc in range(2):
                    P.mm(ps[b][:, 0:ncol], wuk[:, kc, h * 128:(h + 1) * 128], ckvT[:, kc, g0:g0 + ncol],
                         start=(kc == 0), stop=(kc == 1), reads=rd, writes=[ps_res[b]])
                P.copy('act' if h % 2 == 0 else 'dve', knT[:, h, c0:c0 + ncol], ps[b][:, 0:ncol], reads=[ps_res[b]],
                       writes=[kv_res[s] for s in slots])
            for i, s in enumerate(slots):
                nkk = 128 if i < len(slots) - 1 else nk
                b = vbanks[i % len(vbanks)]
                for kc in range(2):
                    P.mm(ps[b][0:nkk, :], ckvT[:, kc, (s % 4) * 128:(s % 4) * 128 + nkk], wuv[:, kc, :],
                         start=(kc == 0), stop=(kc == 1), reads=[ckvT_res[s % 4], smallw_res], writes=[ps_res[b]])
                P.copy('dve' if i % 2 == 0 else 'act', vp[0:nkk, s, :, 0:128], ps[b][0:nkk, :].rearrange("p (h d) -> p h d", h=4),
                       reads=[ps_res[b]], writes=[kv_res[s]])

        def run_pass(kind, seq_i):
            prompt = (kind == "p")
            nt = 128 if prompt else NS
            NT = NTP if prompt else 1
            G = GP if prompt else 1
            NG = NT // G
            gq = G * nt
            tcs, tsn, tcF, tsF = tab[kind]
            STR = Streamer(make_specs(NG))
            if prompt:
                for t in range(NT):
                    P.load(x_t[:, t, :], xp[seq_i, t * 128:(t + 1) * 128, :], f"xin{t % 8}", writes=[x_res[t]])
            else:
                P.load(tcos[0:nt, 0, :], tcs, "tab", writes=[tab_res])
                P.load(tsin[0:nt, 0, :], tsn, "tab", writes=[tab_res])
                P.load(x_t[0:nt, 0, :], xs, "xin", writes=[x_res[0]])

            for l in range(NL):
                P.memset('pool', vp[:, :, :, 128:130], 1.0, writes=kv_res + [x1T_res] + accsb_res + hT_res)
                P.mark(f"{kind}{seq_i} L{l} start")
                if l == 0:
                    load_layer_small(0, nt)
                load_bup(l)
                load_ln(l, 1)
                P.mark(f"{kind}{seq_i} L{l} small loaded")
                new_slot0 = 0 if prompt else 8

                for g in range(NG):
                    tiles = [g * G + i for i in range(G)]
                    if prompt:
                        P.load(tcosF[:, 0:gq], tcF[:, g * gq:(g + 1) * gq], "tabF", writes=[tabF_res])
                        P.load(tsinF[:, 0:gq], tsF[:, g * gq:(g + 1) * gq], "tabF", writes=[tabF_res])
                        P.load(tcos[:, 0:G, :], tcs[g * gq:(g + 1) * gq, :].rearrange("(t p) d -> p t d", p=128), "tab", writes=[tab_res])
                        P.load(tsin[:, 0:G, :], tsn[g * gq:(g + 1) * gq, :].rearrange("(t p) d -> p t d", p=128), "tab", writes=[tab_res])
                    else:
                        P.load(tcosF[:, 0:gq], tcF, "tabF", writes=[tabF_res])
                        P.load(tsinF[:, 0:gq], tsF, "tabF", writes=[tabF_res])
                    for i, t in enumerate(tiles):
                        make_xT(t, i, nt, xT, xT_res[i], i * 128)
                    pbank = [0]

                    def proj(pi):
                        c0, ncols = WIN_PIECES[pi]
                        wv, wr = STR.get((l, 'in', pi))
                        outs = []
                        for i, t in enumerate(tiles):
                            b = pbank[0] % 4; pbank[0] += 1
                            for k in range(8):
                                P.mm(ps[b][0:nt, 0:ncols], xT[:, k, i * 128:i * 128 + nt], wv[:, k, :],
                                     start=(k == 0), stop=(k == 7), reads=[xT_res[i], wr], writes=[ps_res[b]])
                            outs.append((ps[b][0:nt, 0:ncols], ps_res[b]))
                        STR.release()
                        return outs

                    def cons(pi, i, t, pv, pr):
                        slot = (new_slot0 + t) if prompt else 8
                        sti = i % 2
                        if pi == 0:
                            P.act(u_bf[0:nt, i, :], pv, AF.Gelu_apprx_tanh, reads=[pr], writes=[u_res[i]])
                            yield "evac"
                        elif pi == 1:
                            gv, gvr = f32buf()
                            yield
                            gv = gv[0:nt, 0:256]
                            yield
                            P.act(gv, pv, AF.Gelu_apprx_tanh, reads=[pr], writes=[gvr])
                            yield
                            gv3 = gv.rearrange("p (g d) -> p g d", g=4)
                            yield
                            sq, sqr = f32buf()
                            yield
                            sq = sq[0:nt, 0:256]
                            yield
                            P.op('dve', lambda e, gv3=gv3: e.tensor_reduce(stat[0:nt, 16:20], gv3, AX.X, ALU.add), reads=[gvr], writes=[stat_res])
                            yield
                            P.act(sq, gv, AF.Square, reads=[gvr], writes=[sqr])
                            yield
                            P.op('dve', lambda e, sq=sq: e.tensor_reduce(stat[0:nt, 20:24], sq.rearrange("p (g d) -> p g d", g=4), AX.X, ALU.add),
                                 reads=[sqr], writes=[stat_res])
                            yield
                            P.ts('dve', stat[0:nt, 16:20], stat[0:nt, 16:20], 1.0 / 64, None, ALU.mult, reads=[stat_res], writes=[stat_res])
                            yield
                            P.tt('dve', stat[0:nt, 24:28], stat[0:nt, 16:20], stat[0:nt, 16:20], ALU.mult, reads=[stat_res], writes=[stat_res])
                            yield
                            P.stt(stat[0:nt, 20:24], stat[0:nt, 20:24], 1.0 / 64, stat[0:nt, 24:28], ALU.mult, ALU.subtract,
                                  reads=[stat_res], writes=[stat_res])
                            yield
                            P.act(stat[0:nt, 20:24], stat[0:nt, 20:24], AF.Ln, bias=EPS, reads=[stat_res], writes=[stat_res])
                            yield
                            P.act(stat[0:nt, 20:24], stat[0:nt, 20:24], AF.Exp, scale=-0.5, reads=[stat_res], writes=[stat_res])
                            yield
                            P.tt('dve', gv3, gv3, stat[0:nt, 16:20].unsqueeze(2).to_broadcast([nt, 4, 64]), ALU.subtract,
                                 reads=[gvr, stat_res], writes=[gvr])
                            yield
                            P.tt('dve', gv3, gv3, stat[0:nt, 20:24].unsqueeze(2).to_broadcast([nt, 4, 64]), ALU.mult,
                                 reads=[gvr, stat_res], writes=[gvr])
                            yield
                            v_bf, vbf_res = bfbuf()
                            yield
                            P.copy('pool', v_bf[0:nt, :], gv, reads=[gvr], writes=[vbf_res])
                            yield
                            if not prompt:
                                P.load(o_gv[l], gv, "ogv", reads=[gvr])
                            yield
                            yield "defer"
                            for gg in range(4):
                                P.mm(ps[5][0:nt, gg * 64:(gg + 1) * 64], wsT[0:nt, gg, 0:nt], v_bf[0:nt, gg * 64:(gg + 1) * 64],
                                     reads=[vbf_res, smallw_res], writes=[ps_res[5]])
                            yield
                            ya, yar = f32buf()
                            yield
                            ya = ya[0:nt, 0:256]
                            yield
                            for gg in range(4):
                                P.stt(ya[:, gg * 64:(gg + 1) * 64], ps[5][0:nt, gg * 64:(gg + 1) * 64], bsb[0:nt, gg:gg + 1],
                                      u_bf[0:nt, i, gg * 64:(gg + 1) * 64], ALU.add, ALU.mult,
                                      reads=[ps_res[5], gains_res, u_res[i]], writes=[yar])
                            yield
                            sq3, sq3r = bfbuf()
                            P.act(sq3[0:nt, 0:256], ya, AF.Square, accum_out=stat[0:nt, 28:29], reads=[yar], writes=[sq3r, stat_res])
                            yield
                            rstd_from_ss(28, 29, 256, nt)
                            yield
                            P.ts('dve', y_t[0:nt, i, 0:256], ya, stat[0:nt, 29:30], None, ALU.mult,
                                 reads=[yar, stat_res], writes=[y_res[i]])
                            yield
                        elif pi == 2:
                            sq, sqr = f32buf()
                            yield
                            P.copy('dve', cqraw[0:nt, i, 0:256], pv, reads=[pr], writes=[cqraw_res])
                            yield "evac"
                            P.act(sq[0:nt, 0:256], cqraw[0:nt, i, 0:256], AF.Square, accum_out=stat[0:nt, 48 + 2 * i:49 + 2 * i], reads=[cqraw_res], writes=[sqr, stat_res])
                            yield
                        elif pi == 3:
                            sq, sqr = f32buf()
                            yield
                            P.copy('dve', cqraw[0:nt, i, 256:384], pv, reads=[pr], writes=[cqraw_res])
                            yield "evac"
                            P.act(sq[0:nt, 0:128], cqraw[0:nt, i, 256:384], AF.Square, accum_out=stat[0:nt, 49 + 2 * i:50 + 2 * i], reads=[cqraw_res], writes=[sqr, stat_res])
                            yield
                            P.tt('dve', stat[0:nt, 32:33], stat[0:nt, 48 + 2 * i:49 + 2 * i], stat[0:nt, 49 + 2 * i:50 + 2 * i], ALU.add, reads=[stat_res], writes=[stat_res])
                            yield
                            rstd_from_ss(32, 33, 384, nt)
                            yield
                            cqa, cqar = bfbuf()
                            cqb, cqbr = bfbuf()
                            P.ts('dve', cqa[0:nt, 0:256], cqraw[0:nt, i, 0:256], stat[0:nt, 33:34], None, ALU.mult,
                                 reads=[cqraw_res, stat_res], writes=[cqar])
                            P.ts('dve', cqb[0:nt, 0:128], cqraw[0:nt, i, 256:384], stat[0:nt, 33:34], None, ALU.mult,
                                 reads=[cqraw_res, stat_res], writes=[cqbr])
                            yield
                            yield "defer"
                            pst = ps[4][:].bitcast(BF16)
                            yield
                            for kc in range(3):
                                src_ = cqa[0:nt, kc * 128:(kc + 1) * 128] if kc < 2 else cqb[0:nt, 0:128]
                                P.tr(pst[:, kc * 128:kc * 128 + nt], src_, ident[0:nt, 0:nt],
                                     reads=[cqar if kc < 2 else cqbr, const_res], writes=[ps_res[4]])
                            yield
                            P.copy('act', cqT[:, :, i * 128:i * 128 + nt], pst[:, 0:384].rearrange("p (k c) -> p k c", k=3)[:, :, 0:nt],
                                   reads=[ps_res[4]], writes=[cqT_res[i]])
                            yield
                        elif pi == 4:
                            sq, sqr = f32buf()
                            yield
                            raw, rawr = f32buf()
                            yield
                            P.copy('dve', raw[0:nt, 0:256], pv, reads=[pr], writes=[rawr])
                            yield
                            P.act(sq[0:nt, 0:256], raw[0:nt, 0:256], AF.Square, accum_out=stat[0:nt, 34:35], reads=[rawr], writes=[sqr, stat_res])
                            yield
                            rstd_from_ss(34, 35, 256, nt)
                            yield
                            P.stt(kvst[0:nt, sti, 0:256], raw[0:nt, 0:256], stat[0:nt, 35:36], g_ckv[0:nt, :], ALU.mult, ALU.mult,
                                  reads=[rawr, stat_res, gains_res], writes=[kvst_res[sti]])
                            yield
                        elif pi == 5:
                            tt_ = i
                            yield
                            cs_ = tcos[0:nt, tt_, :]; sn_ = tsin[0:nt, tt_, :]
                            yield
                            x1 = pv[:, 0:32]; x2 = pv[:, 32:64]
                            yield
                            P.tt('dve', rtmp[0:nt, 0, :], x1, cs_, ALU.mult, reads=[pr, tab_res], writes=[rtmp_res])
                            yield
                            P.tt('dve', rtmp[0:nt, 1, :], x2, sn_, ALU.mult, reads=[pr, tab_res], writes=[rtmp_res])
                            yield
                            P.tt('dve', rtmp[0:nt, 2, :], x2, cs_, ALU.mult, reads=[pr, tab_res], writes=[rtmp_res])
                            yield
                            P.tt('dve', rtmp[0:nt, 3, :], x1, sn_, ALU.mult, reads=[pr, tab_res], writes=[rtmp_res])
                            yield
                            P.tt('pool', kvst[0:nt, sti, 256:288], rtmp[0:nt, 0, :], rtmp[0:nt, 1, :], ALU.subtract,
                                 reads=[rtmp_res], writes=[kvst_res[sti]])
                            yield
                            P.tt('pool', kvst[0:nt, sti, 288:320], rtmp[0:nt, 2, :], rtmp[0:nt, 3, :], ALU.add,
                                 reads=[rtmp_res], writes=[kvst_res[sti]])
                            yield
                        elif pi == 6:
                            sbqb, sbqb_res = bfbuf()
                            yield
                            P.act(sbqb[0:nt, :], pv, AF.Copy, scale=SB_SCALE, reads=[pr], writes=[sbqb_res])
                            yield "evac"
                            yield "defer"
                            pst = ps[4][:].bitcast(BF16)
                            yield
                            for hp in range(2):
                                P.tr(pst[:, 512 + hp * 128:512 + hp * 128 + nt], sbqb[0:nt, hp * 128:(hp + 1) * 128], ident[0:nt, 0:nt],
                                     reads=[sbqb_res, const_res], writes=[ps_res[4]])
                            yield
                            P.copy('dve', sbQT[:, :, i * 128:i * 128 + nt], pst[:, 512:768].rearrange("p (k c) -> p k c", k=2)[:, :, 0:nt],
                                   reads=[ps_res[4]], writes=[sbq_res[i]])
                            yield
                        elif pi == 7:
                            P.copy('act', kvst[0:nt, sti, 320:576], pv, reads=[pr], writes=[kvst_res[sti]])
                            yield "evac"
                        elif pi == 8:
                            P.copy('dve', kvst[0:nt, sti, 576:832], pv, reads=[pr], writes=[kvst_res[sti]])
                            yield "evac"
                            if prompt:
                                rows = slice(t * 128, (t + 1) * 128)
                                outs = [o[l, seq_i, rows, :] for o in o_p]
                            else:
                                outs = [o[l] for o in o_s]
                            yield
                            for oo, (a0, a1) in zip(outs, [(0, 256), (256, 320), (320, 576), (576, 832)]):
                                P.load(oo, kvst[0:nt, sti, a0:a1], f"okv{sti}", reads=[kvst_res[sti]])
                            yield
                            yield "defer"
                            ingest_kv(slot, nt, sti)
                            yield

                    pending = []
                    nxt = proj(0)
                    for pi in range(len(WIN_PIECES)):
                        cur = nxt
                        if pi + 1 < len(WIN_PIECES):
                            nxt = proj(pi + 1)
                        newg = [cons(pi, i, t, cur[i][0], cur[i][1]) for i, t in enumerate(tiles)]
                        if pi in (0, 2, 3, 6, 7, 8):
                            for g_ in newg:
                                for r_ in g_:
                                    if r_ == "evac":
                                        break
                        active = pending + newg
                        pending = []
                        for g_ in active:
                            for r_ in g_:
                                if r_ == "defer":
                                    pending.append(g_)
                                    break
                    for g_ in pending:
                        for r_ in g_:
                            pass
                    P.mark(f"{kind}{seq_i} L{l} g{g} phaseB")
                    if prompt:
                        project_keys([new_slot0 + t for t in tiles], 128)
                    else:
                        project_keys([8], nt, (0, 1), (0, 1))
                    rdq = cqT_res[0:G] + [smallw_res]
                    for h in range(4):
                        b = h % 2
                        for kc in range(3):
                            P.mm(ps[b][:, 0:gq], wuq[:, kc, h * 192:h * 192 + 128], cqT[:, kc, 0:gq],
                                 start=(kc == 0), stop=(kc == 2), reads=rdq, writes=[ps_res[b]])
                        P.act(qnT[:, h, 0:gq], ps[b][:, 0:gq], AF.Copy, scale=MLA_SCALE, reads=[ps_res[b]], writes=[q_res])
                        for kc in range(3):
                            P.mm(ps[2][0:64, 0:gq], wuq[:, kc, h * 192 + 128:h * 192 + 192], cqT[:, kc, 0:gq],
                                 start=(kc == 0), stop=(kc == 2), reads=rdq, writes=[ps_res[2]])
                        for kc in range(3):
                            P.mm(ps[3][0:64, 0:gq], wrot[:, kc, h, :], cqT[:, kc, 0:gq],
                                 start=(kc == 0), stop=(kc == 2), reads=rdq, writes=[ps_res[3]])
                        t1, t1r = f32buf(); t2, t2r = f32buf()
                        P.tt('dve', t1[0:64, 0:gq], ps[2][0:64, 0:gq], tcosF[:, 0:gq], ALU.mult, reads=[ps_res[2], tabF_res], writes=[t1r])
                        P.tt('dve', t2[0:64, 0:gq], ps[3][0:64, 0:gq], tsinF[:, 0:gq], ALU.mult, reads=[ps_res[3], tabF_res], writes=[t2r])
                        P.tt('pool', qrT[:, h, 0:gq], t1[0:64, 0:gq], t2[0:64, 0:gq], ALU.add, reads=[t1r, t2r], writes=[q_res])

                    P.mark(f"{kind}{seq_i} L{l} g{g} attention")
                    def key_list_prompt():
                        return [(kt, 128, (kt - g * G) if kt >= g * G else -1) for kt in range(g * G + G)]

                    def mla_attend(hp, keys, first, last, accbs, sbanks=(0, 1), stages_only=False):
                        n = len(keys)
                        stt_ = [None] * n

                        def S1(k):
                            slot, nk, di = keys[k]
                            q0 = 0 if di < 0 else di * 128
                            ncol = gq - q0
                            c0 = slot * 128
                            b = sbanks[cnt["mla"] % len(sbanks)]; cnt["mla"] += 1
                            for hh in range(2):
                                h = 2 * hp + hh
                                P.mm(ps[b][0:nk, hh * gq:hh * gq + ncol], knT[:, h, c0:c0 + nk], qnT[:, h, q0:gq], start=True, stop=False,
                                     skip_group_check=True, reads=[kv_res[slot], q_res], writes=[ps_res[b]])
                                P.mm(ps[b][0:nk, hh * gq:hh * gq + ncol], krT[:, c0:c0 + nk], qrT[:, h, q0:gq], start=False, stop=True,
                                     skip_group_check=True, reads=[kv_res[slot], q_res], writes=[ps_res[b]])
                            pT, pTr = bfpair_mla()
                            pin = ps[b][0:nk, 0:2 * gq].rearrange("p (h c) -> p h c", h=2)[:, :, 0:ncol]
                            P.act(pT[0:nk, :, 0:ncol], pin, AF.Exp, reads=[ps_res[b]], writes=[pTr])
                            if di >= 0 and prompt:
                                P.memset('pool', pT[64:128, :, 0:64], 0.0, writes=[pTr])
                            stt_[k] = (pT, pTr, q0)

                        def S2(k):
                            slot, nk, di = keys[k]
                            pT, pTr, q0 = stt_[k]
                            for hh in range(2):
                                h = 2 * hp + hh
                                for qb in range(q0 // 128, G):
                                    ab = accbs[hh] if prompt else accbs[0]
                                    col = (qb % 2) * 129 if prompt else (h % 2) * 129
                                    st_flag = first[0].get(ab, True)
                                    first[0][ab] = False
                                    nq = nt
                                    P.mm(ps[ab][0:nq, col:col + 129], pT[0:nk, hh, qb * 128 - q0:qb * 128 - q0 + nq], vp[0:nk, slot, h, 0:129],
                                         start=st_flag, stop=False, skip_group_check=True, reads=[pTr, kv_res[slot]], writes=[ps_res[ab]])

                        def fin():
                            mla_final(hp, accbs)

                        if stages_only:
                            return n, S1, S2, fin
                        for step in range(n + 1):
                            if step < n:
                                S1(step)
                            if step >= 1:
                                S2(step - 1)
                        if last:
                            fin()

                    def mla_final(hp, accbs):
                        if True:
                            for hh in range(2):
                                h = 2 * hp + hh
                                for qb in range(G):
                                    ab = accbs[hh] if prompt else accbs[0]
                                    col = (qb % 2) * 129 if prompt else (h % 2) * 129
                                    sc = 40 + (cnt["mla"] % 2); cnt["mla"] += 1
                                    P.op('dve', lambda e, ab=ab, col=col, sc=sc: e.reciprocal(stat[0:nt, sc:sc + 1], ps[ab][0:nt, col + 128:col + 129]),
                                         reads=[ps_res[ab]], writes=[stat_res])
                                    yb, ybr = mla_out[qb]
                                    P.act(yb[0:nt, h * 128:(h + 1) * 128], ps[ab][0:nt, col:col + 128], AF.Copy, scale=stat[0:nt, sc:sc + 1],
                                          reads=[ps_res[ab], stat_res], writes=[ybr])

                    def sb_attend(hp, keys, first, banks, stages_only=False):
                        z1b, z2b, pob = banks
                        n = len(keys)
                        stt_ = [None] * n
                        ares = [accsb_res[hp], accsb_res[hp + 2]]

                        def geom(k):
                            slot, nk, di = keys[k]
                            q0 = 0 if di < 0 else di * 128
                            return slot, nk, di, q0, gq - q0, slot * 128

                        def kq(hh, c0, nk, q0):
                            base = 64 * hp
                            return sbKT[base:base + 64, hh, c0:c0 + nk], sbQT[base:base + 64, hh, q0:gq]

                        def S1(k):
                            slot, nk, di, q0, ncol, c0 = geom(k)
                            j = cnt["sb"]; cnt["sb"] += 1
                            b1 = z1b[j % len(z1b)]
                            for hh in range(2):
                                kT, qT = kq(hh, c0, nk, q0)
                                P.mm(ps[b1][0:nk, hh * gq:hh * gq + ncol], kT, qT, skip_group_check=True,
                                     reads=[kv_res[slot]] + sbq_res[0:G], writes=[ps_res[b1]])
                            e_, er = f32buf()
                            ev = e_[:, 0:2 * gq].rearrange("p (h c) -> p h c", h=2)[0:nk, :, 0:ncol]
                            zin = ps[b1][0:nk, 0:2 * gq].rearrange("p (h c) -> p h c", h=2)[:, :, 0:ncol]
                            P.act(ev, zin, AF.Exp, reads=[ps_res[b1]], writes=[er])
                            sp, spr = bfpair()
                            P.act(sp[0:nk, :, 0:ncol], ev, AF.Ln, bias=1.0, reads=[er], writes=[spr])
                            if di >= 0:
                                P.tt('pool', sp[0:nk, :, 0:nt], sp[0:nk, :, 0:nt], maskSB[0:nk, 0:nt].unsqueeze(1).to_broadcast([nk, 2, nt]), ALU.mult,
                                     reads=[spr, const_res], writes=[spr])
                            stt_[k] = dict(sp=sp, spr=spr, j=j)

                        def S2(k):
                            slot, nk, di, q0, ncol, c0 = geom(k)
                            d_ = stt_[k]
                            b2 = z2b[d_["j"] % len(z2b)]
                            for hh in range(2):
                                kT, qT = kq(hh, c0, nk, q0)
                                P.mm(ps[b2][0:nk, hh * gq:hh * gq + ncol], kT, qT, start=True, stop=False, skip_group_check=True,
                                     reads=[kv_res[slot]] + sbq_res[0:G], writes=[ps_res[b2]])
                                P.mm(ps[b2][0:nk, hh * gq:hh * gq + ncol], negU[0:nk, 0:nk], d_["sp"][0:nk, hh, 0:ncol], start=False, stop=True,
                                     skip_group_check=True, reads=[d_["spr"], const_res], writes=[ps_res[b2]])
                            wT, wTr = bfpair()
                            win = ps[b2][0:nk, 0:2 * gq].rearrange("p (h c) -> p h c", h=2)[:, :, 0:ncol]
                            P.act(wT[0:nk, :, 0:ncol], win, AF.Exp, reads=[ps_res[b2]], writes=[wTr])
                            if di >= 0:
                                P.tt('pool', wT[0:nk, :, 0:nt], wT[0:nk, :, 0:nt], maskSB[0:nk, 0:nt].unsqueeze(1).to_broadcast([nk, 2, nt]), ALU.mult,
                                     reads=[wTr, const_res], writes=[wTr])
                            d_["wT"] = wT; d_["wTr"] = wTr

                        def S3(k):
                            slot, nk, di, q0, ncol, c0 = geom(k)
                            d_ = stt_[k]
                            j = d_["j"]
                            b3 = pob[j % len(pob)]
                            sp, spr, wT, wTr = d_["sp"], d_["spr"], d_["wT"], d_["wTr"]
                            qb0 = q0 // 128
                            po = ps[b3][:, 0:G * 2 * 66].rearrange("p (q h c) -> p q h c", q=G, h=2)
                            for hh in range(2):
                                h = hp + 2 * hh
                                for qb in range(qb0, G):
                                    cc = qb * 128 - q0
                                    P.mm(po[0:nt, qb, hh, 0:64], wT[0:nk, hh, cc:cc + nt], sbV[0:nk, slot, h * 64:(h + 1) * 64],
                                         skip_group_check=True, reads=[wTr, kv_res[slot]], writes=[ps_res[b3]])
                                    P.mm(po[0:nt, qb, hh, 64:66], sp[0:nk, hh, cc:cc + nt], ones1[0:nk, 0:2],
                                         skip_group_check=True, reads=[spr, const_res], writes=[ps_res[b3]])
                            acc = accsb[0:nt, qb0:G, hp:4:2, :]
                            if first[0]:
                                P.copy('dve', acc, po[0:nt, qb0:G, :, 0:64], reads=[ps_res[b3]], writes=ares)
                            else:
                                fx = fexp[0:nt, j % 2, :].rearrange("p (q h) -> p q h", h=2)[:, qb0:G, :]
                                P.act(fx, po[0:nt, qb0:G, :, 64], AF.Exp, scale=-1.0, reads=[ps_res[b3]], writes=[fexp_res[j % 2]])
                                P.tt('dve', acc, acc, fx.unsqueeze(3).to_broadcast([nt, G - qb0, 2, 64]), ALU.mult,
                                     reads=ares + [fexp_res[j % 2]], writes=ares)
                                P.tt('dve', acc, acc, po[0:nt, qb0:G, :, 0:64], ALU.add, reads=ares + [ps_res[b3]], writes=ares)
                            first[0] = False
                            stt_[k] = None

                        if stages_only:
                            return n, S1, S2, S3
                        for step in range(n + 2):
                            if 1 <= step <= n:
                                S2(step - 1)
                            if step >= 2:
                                S3(step - 2)
                            if step < n:
                                S1(step)

                    mla_out = []
                    cnt["f32"] = 0
                    for qb in range(G):
                        yb, ybr = f32buf()
                        mla_out.append((yb, ybr))

                    def finish_mla():
                        for qb in range(G):
                            yb, ybr = mla_out[qb]
                            P.act(xb[0:nt, 0:512], yb[0:nt, :], AF.Square, accum_out=stat[0:nt, 42:43], reads=[ybr], writes=[xb_res, stat_res])
                            rstd_from_ss(42, 43, 512, nt)
                            P.ts('dve', y_t[0:nt, qb, 256:768], yb[0:nt, :], stat[0:nt, 43:44], None, ALU.mult,
                                 reads=[ybr, stat_res], writes=[y_res[qb]])

                    def finish_sb():
                        for qb in range(G):
                            yc = accsb[0:nt, qb, :, :]
                            sq2, sq2r = bfbuf()
                            P.act(sq2[0:nt, 0:256].rearrange("p (h d) -> p h d", h=4), yc, AF.Square, accum_out=stat[0:nt, 44:45],
                                  reads=accsb_res, writes=[sq2r, stat_res])
                            rstd_from_ss(44, 45, 256, nt)
                            P.ts('dve', y_t[0:nt, qb, 768:1024].rearrange("p (h d) -> p h d", h=4), yc, stat[0:nt, 45:46], None, ALU.mult,
                                 reads=accsb_res + [stat_res], writes=[y_res[qb]])

                    if prompt:
                        keys = key_list_prompt()
                        cnt["f32fix"] = 2
                        cnt["mla_pool"] = True
                        for hp_ in range(2):
                            n_, M1, M2, Mfin = mla_attend(hp_, keys, [dict()], True, (1, 2), (0,), stages_only=True)
                            n2_, B1, B2, B3 = sb_attend(hp_, keys, [True], ((3, 4), (5, 6), (7,)), stages_only=True)
                            for step in range(n_ + 2):
                                if 1 <= step <= n_:
                                    B2(step - 1)
                                if step >= 2:
                                    B3(step - 2)
                                if 1 <= step <= n_:
                                    M2(step - 1)
                                if step < n_:
                                    B1(step)
                                    M1(step)
                            Mfin()
                        cnt["f32fix"] = None
                        cnt["mla_pool"] = False
                        finish_mla()
                        finish_sb()
                    else:
                        firsts_m = [[dict()]] * 4
                        firsts_s = [[True] for _ in range(4)]
                        npg = PASTL // 512

                        def ingest_group(kg):
                            slots = [(kg % 2) * 4 + i for i in range(4)]
                            for i, s_ in enumerate(slots):
                                r0 = kg * 512 + i * 128
                                sti = i % 2
                                P.load(kvst[:, sti, 0:256], c_ckv[l, r0:r0 + 128, :], f"cin{sti}", writes=[kvst_res[sti]])
                                P.load(kvst[:, sti, 256:320], c_kr[l, r0:r0 + 128, :], f"cin{sti}", writes=[kvst_res[sti]])
                                P.load(kvst[:, sti, 320:576], c_k[l, r0:r0 + 128, :], f"cin{sti}", writes=[kvst_res[sti]])
                                P.load(kvst[:, sti, 576:832], c_v[l, r0:r0 + 128, :], f"cin{sti}", writes=[kvst_res[sti]])
                                ingest_kv(s_, 128, sti)
                            project_keys(slots, 128, (0, 1), (0, 1))
                            return slots

                        nxt_slots = ingest_group(0)
                        for kg in range(npg):
                            slots = nxt_slots
                            if kg + 1 < npg:
                                nxt_slots = ingest_group(kg + 1)
                            keys = [(s_, 128, -1) for s_ in slots]
                            for hp_ in range(2):
                                n_, M1, M2, Mfin = mla_attend(hp_, keys, firsts_m[hp_], False, (4 + hp_,), (2,), stages_only=True)
                                n2_, B1, B2, B3 = sb_attend(hp_, keys, firsts_s[hp_], ((3,), (6,), (7,)), stages_only=True)
                                for step in range(n_ + 2):
                                    if 1 <= step <= n_:
                                        B2(step - 1)
                                    if step >= 2:
                                        B3(step - 2)
                                    if 1 <= step <= n_:
                                        M2(step - 1)
                                    if step < n_:
                                        B1(step)
                                        M1(step)
                        keys = [(8, nt, 0)]
                        for hp_ in range(2):
                            mla_attend(hp_, keys, firsts_m[hp_], True, (4 + hp_,), (2,))
                        finish_mla()
                        for hp_ in range(2):
                            sb_attend(hp_, keys, firsts_s[hp_], ((3,), (6,), (7,)))
                        finish_sb()

                    P.mark(f"{kind}{seq_i} L{l} g{g} phaseD")
                    yTs = [(yT[:, 0, :, :], yT_res[0]), (xb[:, :].rearrange("p (k c) -> p k c", k=8), xb_res)]
                    for i, t in enumerate(tiles):
                        yv, yr = yTs[i % 2]
                        pst = ps[6 + (i % 2)][:].bitcast(BF16)
                        for k in range(8):
                            P.tr(pst[:, k * 128:k * 128 + nt], y_t[0:nt, i, k * 128:(k + 1) * 128], ident[0:nt, 0:nt],
                                 reads=[y_res[i], const_res], writes=[ps_res[6 + (i % 2)]])
                        P.copy('act' if i % 2 == 0 else 'dve', yv[:, :, 0:nt], pst[:, :].rearrange("p (k c) -> p k c", k=8)[:, :, 0:nt],
                               reads=[ps_res[6 + (i % 2)]], writes=[yr])
                    for c in range(4):
                        wv, wr = STR.get((l, 'out', c))
                        for i, t in enumerate(tiles):
                            yv, yr = yTs[i % 2]
                            b = (c * G + i) % 4
                            for k in range(8):
                                P.mm(ps[b][0:nt, 0:256], yv[:, k, 0:nt], wv[:, k, :], start=(k == 0), stop=(k == 7),
                                     reads=[yr, wr], writes=[ps_res[b]])
                            xv = x_t[0:nt, t, c * 256:(c + 1) * 256]
                            P.stt(xv, xv, ALPHA, ps[b][0:nt, 0:256], ALU.mult, ALU.add, reads=[x_res[t], ps_res[b]], writes=[x_res[t]])
                        STR.release()
                    for i, t in enumerate(tiles):
                        layer_norm_tile(t, nt, l)

                P.mark(f"{kind}{seq_i} L{l} MLP")
                P.memset('pool', stat[:, 61:62], 0.0, writes=kv_res + [x1T_res] + accsb_res + hT_res)
                P.load(lnb[:, 0, :], W["b_down"][l:l + 1, :].to_broadcast([128, 1024]), "lnb0", writes=[lnb_res[0]])
                x1T_lo = knT
                x1T_hi = vp[:].rearrange("p a b c -> p (a b c)")[:, 0:4 * KS * 128].rearrange("p (k c) -> p k c", k=4)

                class X1:
                    pass

                for t in range(NT):
                    P.copy('act', xb[0:nt, :], x_t[0:nt, t, :], reads=[x_res[t]], writes=[xb_res])
                    pst = ps[7][:].bitcast(BF16)
                    for k in range(8):
                        P.tr(pst[:, k * 128:k * 128 + nt], xb[0:nt, k * 128:(k + 1) * 128], ident[0:nt, 0:nt],
                             reads=[xb_res, const_res], writes=[ps_res[7]])
                    P.copy('dve', x1T_lo[:, :, t * 128:t * 128 + nt], pst[:, 0:512].rearrange("p (k c) -> p k c", k=4)[:, :, 0:nt],
                           reads=[ps_res[7]], writes=[x1T_res])
                    P.copy('dve', x1T_hi[:, :, t * 128:t * 128 + nt], pst[:, 512:1024].rearrange("p (k c) -> p k c", k=4)[:, :, 0:nt],
                           reads=[ps_res[7]], writes=[x1T_res])
                    xv = x_t[0:nt, t, :]
                    P.stt(xv, xv, ALPHA, lnb[0:nt, 0, :], ALU.mult, ALU.add, reads=[x_res[t], lnb_res[0]], writes=[x_res[t]])
                load_ln(l, 2)
                if l + 1 < NL:
                    load_layer_small(l + 1, nt)

                def x1T_ap(k, c0, n):
                    return (x1T_lo if k < 4 else x1T_hi)[:, k % 4, c0:c0 + n]

                for e8 in range(8):
                    wup = [STR.get((l, 'up', e8, hh)) for hh in range(2)]
                    wdn = [STR.get((l, 'dn', e8, hh)) for hh in range(2)]
                    MG = min(4, NT)
                    mq = MG * nt
                    for g in range(NT // MG):
                        c0 = g * MG * 128
                        for fc in range(4):
                            b = fc % 2
                            wv, wr = wup[fc // 2]
                            for k in range(8):
                                P.mm(ps[b][:, 0:mq], wv[:, k, (fc % 2) * 128:(fc % 2) * 128 + 128], x1T_ap(k, c0, mq),
                                     start=(k == 0), stop=(k == 7), reads=[x1T_res, wr], writes=[ps_res[b]])
                            r_, rr = f32buf()
                            f_idx = e8 * 4 + fc
                            P.act(r_[:, 0:mq], ps[b][:, 0:mq], AF.Relu, bias=bup[:, f_idx:f_idx + 1], reads=[ps_res[b], bup_res], writes=[rr])
                            P.act(hT[:, fc, 0:mq], r_[:, 0:mq], AF.Square, reads=[rr], writes=[hT_res[fc]])
                        if g == NT // MG - 1:
                            STR.release(2)
                        for i in range(MG):
                            t = g * MG + i
                            for nh in range(2):
                                b = 2 + (i * 2 + nh) % 4
                                for fc in range(4):
                                    wv, wr = wdn[fc // 2]
                                    P.mm(ps[b][0:nt, :], hT[:, fc, i * 128:i * 128 + nt], wv[:, fc % 2, nh * 512:(nh + 1) * 512],
                                         start=(fc == 0), stop=(fc == 3), reads=[hT_res[fc], wr], writes=[ps_res[b]])
                                xv = x_t[0:nt, t, nh * 512:(nh + 1) * 512]
                                P.tt('dve', xv, xv, ps[b][0:nt, :], ALU.add, reads=[x_res[t], ps_res[b]], writes=[x_res[t]])
                    STR.release(2)
                for t in range(NT):
                    layer_norm_tile(t, nt, l)
                    if l == NL - 1:
                        if prompt:
                            P.load(yp[seq_i, t * 128:(t + 1) * 128, :], x_t[:, t, :], "yout", reads=[x_res[t]])
                        else:
                            P.load(ys, x_t[0:nt, 0, :], "yout", reads=[x_res[0]])

        for s_i in range(nseq):
            run_pass("p", s_i)
        if do_sample:
            run_pass("s", 0)
        if os.environ.get("K_MARKS"):
            for m in P.marks:
                print("MARK", m)
            print("TOTAL OPS", P.nrec, {e: len(P.ops[e]) for e in P.ENG})
        P.emit(st)
    return nc


def rope_tables(pos):
    half = 32
    inv = (np.float32(10000.0) ** (-np.arange(half, dtype=np.float32) / np.float32(half))).astype(np.float32)
    ang = pos.astype(np.float32)[:, None] * inv[None, :]
    cos = np.cos(ang).astype(np.float32)
    sin = np.sin(ang).astype(np.float32)
    cosF = np.concatenate([cos, cos], 1).T * np.float32(MLA_SCALE)
    sinF = np.concatenate([sin, sin], 1).T * np.float32(MLA_SCALE)
    return cos, sin, np.ascontiguousarray(cosF.astype(np.float32)), np.ascontiguousarray(sinF.astype(np.float32))


_CACHE = {}


def kernel(**inputs):
    x_prompt = np.asarray(inputs["x_prompt"], np.float32)
    x_sample = np.asarray(inputs["x_sample"], np.float32)
    B, S, _ = x_prompt.shape
    NL = inputs["w_in"].shape[0]
    PASTL = inputs["cache_mla_ckv"].shape[2]
    nseq = B // N_CORES
    key = (S, NL, PASTL, nseq)
    if key not in _CACHE:
        _CACHE[key] = build_program(S, NL, PASTL, nseq)
    nc = _CACHE[key]
    tp = rope_tables(np.arange(S))
    tsm = rope_tables(PASTL + np.arange(NS))
    shared = {}
    for k in WNAMES:
        a = np.ascontiguousarray(np.asarray(inputs[k], np.float32))
        if k == "w_uq":
            a = a.reshape(NL, 384, 768)
        shared[k] = a
    for nm, arr in zip(["tp_cos", "tp_sin", "tp_cosF", "tp_sinF"], tp):
        shared[nm] = arr
    for nm, arr in zip(["ts_cos", "ts_sin", "ts_cosF", "ts_sinF"], tsm):
        shared[nm] = arr
    in_maps = []
    for c in range(N_CORES):
        m = dict(shared)
        m["xp"] = np.ascontiguousarray(x_prompt[c * nseq:(c + 1) * nseq])
        m["xs"] = np.ascontiguousarray(x_sample[c])
        m["c_ckv"] = np.ascontiguousarray(np.asarray(inputs["cache_mla_ckv"], np.float32)[:, c])
        m["c_kr"] = np.ascontiguousarray(np.asarray(inputs["cache_mla_krope"], np.float32)[:, c])
        m["c_k"] = np.ascontiguousarray(np.asarray(inputs["cache_sb_k"], np.float32)[:, c].reshape(NL, PASTL, 256))
        m["c_v"] = np.ascontiguousarray(np.asarray(inputs["cache_sb_v"], np.float32)[:, c].reshape(NL, PASTL, 256))
        in_maps.append(m)
    res = run_bass_kernel_spmd(nc, in_maps, core_ids=list(range(N_CORES)))
    R = res.results
    y_p = np.concatenate([r["yp"] for r in R], 0)
    y_s = np.stack([r["ys"] for r in R], 0)
    ckv_p = np.concatenate([r["o_ckv_p"] for r in R], 1)
    kr_p = np.concatenate([r["o_kr_p"] for r in R], 1)
    k_p = np.concatenate([r["o_k_p"] for r in R], 1).reshape(NL, B, S, 4, 64)
    v_p = np.concatenate([r["o_v_p"] for r in R], 1).reshape(NL, B, S, 4, 64)
    ckv_s = np.stack([r["o_ckv_s"] for r in R], 1)
    kr_s = np.stack([r["o_kr_s"] for r in R], 1)
    k_s = np.stack([r["o_k_s"] for r in R], 1).reshape(NL, len(R), NS, 4, 64)
    v_s = np.stack([r["o_v_s"] for r in R], 1).reshape(NL, len(R), NS, 4, 64)
    gv_s = np.stack([r["o_gv_s"] for r in R], 1).reshape(NL, len(R), NS, 4, 64)
    return (y_p, y_s, ckv_p, kr_p, k_p, v_p, ckv_s, kr_s, k_s, v_s, gv_s)
```

```python
import os
import numpy as np
from contextlib import ExitStack
import concourse.bass as bass
import concourse.mybir as mybir
from concourse.bass_utils import run_bass_kernel_spmd

F32 = mybir.dt.float32
BF16 = mybir.dt.bfloat16
AF = mybir.ActivationFunctionType
ALU = mybir.AluOpType
AX = mybir.AxisListType

D = 1024
NLAYERS = 4
SEQ = 2048
PAST = 4096
NS = 16
DFF = 4096
WIN = 1984
ALPHA = float((2 * 4) ** 0.25)
EPS = 1e-5
MLA_SCALE = float(192 ** -0.5)
SB_SCALE = float(64 ** -0.5)
N_CORES = 8


class Res:
    __slots__ = ("name", "w", "r")

    def __init__(self, name=""):
        self.name = name
        self.w = None
        self.r = {}


class Prog:
    ENG = ("pe", "act", "dve", "pool", "sp")
    EPOCH = 30000

    def __init__(self, nc, same_engine_sync=True):
        self.nc = nc
        self.ops = {e: [] for e in self.ENG}
        self.dma_cnt = {}
        self.dma_cnt_raw = {}
        self.dma_maxwait = {}
        self.same_engine_sync = same_engine_sync
        self.nrec = 0
        self.limit = int(os.environ.get("K_LIMIT", "0")) or None
        self.marks = []
        self.trace_lines = bool(os.environ.get("K_TRACE"))

    def _collect(self, eng, reads, writes):
        toks = []
        for r in reads:
            if r.w is not None:
                toks.append(r.w)
        for w in writes:
            if w.w is not None:
                toks.append(w.w)
            toks.extend(w.r.values())
        waits = []
        for t in toks:
            if t[0] == 'e':
                if t[1] == eng and (eng == 'pe' or not self.same_engine_sync):
                    continue
                self.ops[t[1]][t[2]][2] = True
                waits.append(t)
            else:
                v = self.dma_cnt[t[1]] * 16
                waits.append(('d', t[1], v))
                if self.dma_maxwait.get(t[1], 0) < v:
                    self.dma_maxwait[t[1]] = v
        return waits

    def mark(self, name):
        self.marks.append((name, self.nrec, len(self.ops['pe'])))

    def op(self, eng, fn, reads=(), writes=()):
        self.nrec += 1
        if self.trace_lines:
            import sys as _s
            f = _s._getframe(1)
            ln = []
            while f is not None and len(ln) < 3:
                ln.append(f.f_lineno); f = f.f_back
            print("OP", self.nrec, eng, ln)
        if self.limit is not None and self.nrec > self.limit:
            return None
        waits = self._collect(eng, reads, writes)
        idx = len(self.ops[eng])
        self.ops[eng].append([fn, waits, False, None])
        tok = ('e', eng, idx)
        for r in reads:
            r.r[eng] = tok
        for w in writes:
            w.w = tok
            w.r = {}
        return tok

    def dma(self, q, fn, sem, reads=(), writes=()):
        self.nrec += 1
        if self.trace_lines:
            import sys as _s
            f = _s._getframe(1)
            ln = []
            while f is not None and len(ln) < 3:
                ln.append(f.f_lineno); f = f.f_back
            print("OP", self.nrec, "dma:" + sem, ln)
        if self.limit is not None and self.nrec > self.limit:
            return None
        sem = f"{sem}_{self.dma_cnt_raw.get(sem, 0) // 1500}"
        base = sem.rsplit("_", 1)[0]
        self.dma_cnt_raw[base] = self.dma_cnt_raw.get(base, 0) + 1
        waits = self._collect(q, reads, writes)
        if self.dma_maxwait.get(sem, 0) > 0:
            waits.append(('d', sem, self.dma_maxwait[sem]))
        self.ops[q].append([fn, waits, False, sem])
        self.dma_cnt[sem] = self.dma_cnt.get(sem, 0) + 1
        tok = ('d', sem, self.dma_cnt[sem] * 16)
        for r in reads:
            r.r['d' + sem] = tok
        for w in writes:
            w.w = tok
            w.r = {}
        return tok

    def emit(self, stack):
        nc = self.nc
        E = self.EPOCH
        sig = {}
        nsig = {}
        for e in self.ENG:
            n = 0
            for i, o in enumerate(self.ops[e]):
                if o[2]:
                    n += 1
                    sig[(e, i)] = n
            nsig[e] = n
        semh = {}
        for e in self.ENG:
            for k in range((nsig[e] + E - 1) // E):
                semh[('e', e, k)] = stack.enter_context(nc.semaphore(f"s_{e}_{k}"))
        for name in self.dma_cnt:
            semh[('d', name)] = stack.enter_context(nc.semaphore(f"d_{name}"))
        block = stack.enter_context(nc.Block())
        ops = self.ops
        dma_cnt = self.dma_cnt

        def run(e, eng):
            waited = {}
            for i, (fn, waits, signal, dsem) in enumerate(ops[e]):
                need = {}
                for t in waits:
                    if t[0] == 'e':
                        n = sig[(t[1], t[2])]
                        key = ('e', t[1], (n - 1) // E)
                        val = (n - 1) % E + 1
                    else:
                        key = ('d', t[1])
                        val = t[2]
                    if need.get(key, 0) < val:
                        need[key] = val
                for key, val in need.items():
                    if waited.get(key, 0) >= val:
                        continue
                    waited[key] = val
                    eng.wait_ge(semh[key], val)
                ins = fn(eng)
                if signal:
                    n = sig[(e, i)]
                    ins.then_inc(semh[('e', e, (n - 1) // E)], 1)
                if dsem is not None:
                    ins.then_inc(semh[('d', dsem)], 16)
            if e == 'sp':
                for name, c in dma_cnt.items():
                    eng.wait_ge(semh[('d', name)], c * 16)

        @block.tensor
        def _(pe):
            run('pe', pe)

        @block.scalar
        def _(act):
            run('act', act)

        @block.vector
        def _(dve):
            run('dve', dve)

        @block.gpsimd
        def _(pool):
            run('pool', pool)

        @block.sync
        def _(sp):
            run('sp', sp)

    def mm(self, out, lhsT, rhs, start=True, stop=True, reads=(), writes=(), **kw):
        return self.op('pe', lambda e: e.matmul(out, lhsT, rhs, start=start, stop=stop, **kw), reads, writes)

    def tr(self, out, in_, ident, reads=(), writes=()):
        return self.op('pe', lambda e: e.transpose(out, in_, ident), reads, writes)

    def act(self, out, in_, func, reads=(), writes=(), **kw):
        return self.op('act', lambda e: e.activation(out, in_, func, **kw), reads, writes)

    def tt(self, eng, out, in0, in1, op, reads=(), writes=()):
        return self.op(eng, lambda e: e.tensor_tensor(out, in0, in1, op), reads, writes)

    def ts(self, eng, out, in0, s1, s2, op0, op1=None, reads=(), writes=(), **kw):
        if op1 is None:
            return self.op(eng, lambda e: e.tensor_scalar(out, in0, s1, None, op0, **kw), reads, writes)
        return self.op(eng, lambda e: e.tensor_scalar(out, in0, s1, s2, op0, op1, **kw), reads, writes)

    def stt(self, out, in0, scalar, in1, op0, op1, reads=(), writes=(), **kw):
        return self.op('dve', lambda e: e.scalar_tensor_tensor(out, in0, scalar, in1, op0, op1, **kw), reads, writes)

    def copy(self, eng, out, in_, reads=(), writes=()):
        if eng == 'act':
            return self.op('act', lambda e: e.copy(out, in_), reads, writes)
        return self.op(eng, lambda e: e.tensor_copy(out, in_), reads, writes)

    def memset(self, eng, ap, val, writes=()):
        return self.op(eng, lambda e: e.memset(ap, val), (), writes)

    def load(self, out, in_, sem, reads=(), writes=(), q='sp', **kw):
        return self.dma(q, lambda e: e.dma_start(out, in_, **kw), sem, reads, writes)


WNAMES = ["w_in", "w_s", "b_s", "g_cq", "g_ckv", "w_uq", "w_uk", "w_uv", "g_mix", "w_out",
          "ln1_g", "ln1_b", "w_up", "b_up", "w_down", "b_down", "ln2_g", "ln2_b"]
WIN_PIECES = [(0, 256), (256, 256), (512, 256), (768, 128), (896, 256), (1152, 64), (1216, 256), (1472, 256), (1728, 256)]


def build_program(S=SEQ, NL=NLAYERS, PASTL=PAST, nseq=2, do_sample=True):
    nc = bass.Bass("TRN2", target_bir_lowering=False)

    def din(name, shape):
        return nc.dram_tensor(name, list(shape), F32, kind="ExternalInput").ap()

    def dout(name, shape):
        return nc.dram_tensor(name, list(shape), F32, kind="ExternalOutput").ap()

    xp = din("xp", [nseq, S, D])
    xs = din("xs", [NS, D])
    c_ckv = din("c_ckv", [NL, PASTL, 256])
    c_kr = din("c_kr", [NL, PASTL, 64])
    c_k = din("c_k", [NL, PASTL, 256])
    c_v = din("c_v", [NL, PASTL, 256])
    W = {}
    wshapes = {"w_in": [NL, D, WIN], "w_s": [NL, 4, 128, 128], "b_s": [NL, 4, 128], "g_cq": [NL, 384],
               "g_ckv": [NL, 256], "w_uq": [NL, 384, 768], "w_uk": [NL, 4, 256, 128], "w_uv": [NL, 4, 256, 128],
               "g_mix": [NL, 1024], "w_out": [NL, 1024, 1024], "ln1_g": [NL, 1024], "ln1_b": [NL, 1024],
               "w_up": [NL, 1024, DFF], "b_up": [NL, DFF], "w_down": [NL, DFF, 1024], "b_down": [NL, 1024],
               "ln2_g": [NL, 1024], "ln2_b": [NL, 1024]}
    for k in WNAMES:
        W[k] = din(k, wshapes[k])
    tab = {"p": (din("tp_cos", [S, 32]), din("tp_sin", [S, 32]), din("tp_cosF", [64, S]), din("tp_sinF", [64, S])),
           "s": (din("ts_cos", [NS, 32]), din("ts_sin", [NS, 32]), din("ts_cosF", [64, NS]), din("ts_sinF", [64, NS]))}
    yp = dout("yp", [nseq, S, D])
    ys = dout("ys", [NS, D])
    o_p = (dout("o_ckv_p", [NL, nseq, S, 256]), dout("o_kr_p", [NL, nseq, S, 64]),
           dout("o_k_p", [NL, nseq, S, 256]), dout("o_v_p", [NL, nseq, S, 256]))
    o_s = (dout("o_ckv_s", [NL, NS, 256]), dout("o_kr_s", [NL, NS, 64]),
           dout("o_k_s", [NL, NS, 256]), dout("o_v_s", [NL, NS, 256]))
    o_gv = dout("o_gv_s", [NL, NS, 256])

    NTP = S // 128
    KS = max(NTP, 9)
    NPIECE = NL * (9 + 4 + 32)
    wscr = nc.dram_tensor("wscr", [NPIECE, 128, 2048], BF16, kind="Internal").ap()

    with ExitStack() as st:
        P = Prog(nc, same_engine_sync=not bool(os.environ.get("K_NOSES")))

        def sb(name, shape, dt=F32):
            return st.enter_context(nc.sbuf_tensor("sb_" + name, list(shape), dt))

        def RL(name, n):
            return [Res(f"{name}{i}") for i in range(n)]

        x_t = sb("x_t", [128, NTP, D]); x_res = RL("x", NTP)
        GP = 2
        xT = sb("xT", [128, 8, GP * 128], BF16); xT_res = RL("xT", 4)
        y_t = sb("y_t", [128, GP, D], BF16); y_res = RL("y", 4)
        knT = sb("knT", [128, 4, KS * 128], BF16)
        krT = sb("krT", [64, KS * 128], BF16)
        vp = sb("vp", [128, KS, 4, 130], BF16)
        sbKT = sb("sbKT", [128, 2, KS * 128], BF16)
        sbV = sb("sbV", [128, KS, 256], BF16)
        kv_res = RL("kv", KS)
        x1T_rl = RL("x1T", 4)
        qnT = sb("qnT", [128, 4, GP * 128], BF16)
        qrT = sb("qrT", [64, 4, GP * 128], BF16)
        sbQT = sb("sbQT", [128, 2, GP * 128], BF16)
        cqT = sb("cqT", [128, 3, GP * 128], BF16)
        ckvT = sb("ckvT", [128, 2, 512], BF16)
        q_res = Res("q"); sbq_res = RL("sbq", 4); cqT_res = RL("cqT", 4); ckvT_res = RL("ckvT", 4)
        wuq = sb("wuq", [128, 3, 768], BF16); wrot = sb("wrot", [128, 3, 4, 64], BF16)
        wuk = sb("wuk", [128, 2, 512], BF16); wuv = sb("wuv", [128, 2, 512], BF16)
        wsT = sb("wsT", [128, 4, 128], BF16)
        smallw_res = Res("smallw")
        g_ckv = sb("g_ckv", [128, 256])
        lnb = sb("lnb", [128, 2, 1024]); lnb_res = RL("lnb", 2)
        prm = sb("prm", [128, 47])
        bup = prm[:, 0:32]; bsb = prm[:, 32:36]
        prow_res = None
        identf = sb("identf", [64, 64])
        gains_res = Res("gains")
        NRING = 4
        ring = sb("ring", [128, NRING, 2048], BF16); ring_res = RL("ring", NRING)
        stg = sb("stg", [128, 2, 512]); stg_res = RL("stg", 2)
        tcos = sb("tcos", [128, GP, 32]); tsin = sb("tsin", [128, GP, 32])
        tcosF = sb("tcosF", [64, GP * 128]); tsinF = sb("tsinF", [64, GP * 128])
        tab_res = Res("tab"); tabF_res = Res("tabF")
        ident = sb("ident", [128, 128], BF16); negU = sb("negU", [128, 128], BF16)
        maskSB = sb("maskSB", [128, 128], BF16); ones1 = sb("ones1", [128, 2], BF16)
        const_res = Res("const")
        u_bf = sb("u_bf", [128, GP, 256], BF16); u_res = RL("u", 4)
        f32t = sb("f32t", [128, 3, 512]); f32_res = RL("f32t", 3)
        cqraw = sb("cqraw", [128, GP, 384]); cqraw_res = Res("cqraw")
        kvst = sb("kvst", [128, 2, 832]); kvst_res = RL("kvst", 2)
        bf512 = sb("bf512", [128, 12, 256], BF16)
        bfp_res = RL("bfp", 6)
        bf_res = [bfp_res[i // 2] for i in range(8)]
        kvbf = bf512[:, 8:12, :].rearrange("p a b -> p (a b)")[:, 0:832]
        kvbf_rl = [bfp_res[4], bfp_res[5]]
        wsb = bf512[:, 0:2, :].rearrange("p a (g j) -> p (a g) j", g=2)
        hacc = sb("hacc", [128, 2048], BF16)
        accsb = hacc[:, 0:GP * 512].bitcast(F32).rearrange("p (q h d) -> p q h d", q=GP, h=4); accsb_res = RL("accsb", 4)
        fexp = sb("fexp", [128, 2, 4]); fexp_res = RL("fexp", 2)
        yT = sb("yT", [128, 1, 8, 128], BF16); yT_res = RL("yT", 1)
        xb = sb("xb", [128, 1024], BF16); xb_res = Res("xb")
        hT = hacc[:, :].rearrange("p (f c) -> p f c", f=4); hT_res = RL("hT", 4)
        stat = sb("stat", [128, 64]); stat_res = Res("stat")
        rtmp = sb("rtmp", [128, 4, 32]); rtmp_res = Res("rtmp")
        prow_res = rtmp_res
        prow = rtmp[0:47, :, :].rearrange("p a b -> p (a b)")

        ps = [st.enter_context(nc.psum_tensor(f"ps{i}", [128, 512], F32)) for i in range(8)]
        ps_res = RL("ps", 8)

        P.memset('pool', ident[:], 1.0, writes=[const_res])
        P.op('pool', lambda e: e.affine_select(ident[:], ident[:], pattern=[[-1, 128]], compare_op=ALU.is_equal,
                                               fill=0.0, base=0, channel_multiplier=1), writes=[const_res])
        P.memset('pool', negU[:], -1.0, writes=[const_res])
        P.op('pool', lambda e: e.affine_select(negU[:], negU[:], pattern=[[-1, 128]], compare_op=ALU.is_ge,
                                               fill=0.0, base=0, channel_multiplier=1), writes=[const_res])
        P.memset('pool', maskSB[:], 1.0, writes=[const_res])
        P.op('pool', lambda e: e.affine_select(maskSB[:], maskSB[:], pattern=[[1, 128]], compare_op=ALU.is_gt,
                                               fill=0.0, base=0, channel_multiplier=-1), writes=[const_res])
        P.memset('pool', ones1[:], 1.0, writes=[const_res])
        P.memset('pool', identf[:], 1.0, writes=[const_res])
        P.op('pool', lambda e: e.affine_select(identf[:], identf[:], pattern=[[-1, 64]], compare_op=ALU.is_equal,
                                               fill=0.0, base=0, channel_multiplier=1), writes=[const_res])
        gcqc = prm[:, 36:39]; gmixc = prm[:, 39:47]
        P.memset('pool', vp[:, :, :, 128:130], 1.0, writes=kv_res)

        cnt = {"f32": 0, "bf": 0, "stg": 0, "ring": 0, "mla": 0, "sb": 0, "bfp": 0, "mlap": 0}

        def f32buf():
            if cnt.get("f32fix") is not None:
                i = cnt["f32fix"]
            else:
                i = cnt["f32"] % 3; cnt["f32"] += 1
            return f32t[:, i, :], f32_res[i]

        def bfpair_mla():
            if cnt.get("mla_pool"):
                j = 4 + cnt["mlap"] % 2; cnt["mlap"] += 1
                return bf512[:, 2 * j:2 * j + 2, :], bfp_res[j]
            return bfpair()

        def bfbuf():
            i = cnt["bf"] % 8; cnt["bf"] += 1
            return bf512[:, i, :], bf_res[i]

        class PairRes:
            pass

        def bfpair():
            j = cnt["bfp"] % 4; cnt["bfp"] += 1
            cnt["bf"] = 2 * j + 2
            return bf512[:, 2 * j:2 * j + 2, :], bfp_res[j]

        def quarters(a, b):
            out = []
            if a * b <= 512:
                return [(0, a, 0, b)]
            if b <= 512:
                step = max(1, 512 // b)
                for a0 in range(0, a, step):
                    out.append((a0, min(a, a0 + step), 0, b))
            else:
                for a0 in range(a):
                    for b0 in range(0, b, 512):
                        out.append((a0, a0 + 1, b0, min(b, b0 + 512)))
            return out

        def staged_cast(dst3, src3, a, b, dst_res, scale_cols=None, scale_res=None):
            for (a0, a1, b0, b1) in quarters(a, b):
                si = cnt["stg"] % 2; cnt["stg"] += 1
                na, nb = a1 - a0, b1 - b0
                sview = stg[:, si, 0:na * nb].rearrange("p (a b) -> p a b", a=na)
                P.load(sview, src3[:, a0:a1, b0:b1], f"stg{si}", writes=[stg_res[si]])
                ceng = 'pool' if si == 0 else 'dve'
                if scale_cols is None:
                    P.copy(ceng, dst3[:, a0:a1, b0:b1], sview, reads=[stg_res[si]], writes=[dst_res])
                else:
                    P.tt('pool', dst3[:, a0:a1, b0:b1], sview, scale_cols[:, a0:a1].unsqueeze(2).to_broadcast([128, na, nb]), ALU.mult,
                         reads=[stg_res[si], scale_res], writes=[dst_res])

        def stream_piece(src_ap, shape3, scale_cols=None, scale_res=None):
            a, b = shape3
            ri = cnt["ring"] % NRING; cnt["ring"] += 1
            rview = ring[:, ri, 0:a * b].rearrange("p (a b) -> p a b", a=a)
            staged_cast(rview, src_ap, a, b, ring_res[ri], scale_cols, scale_res)
            return rview, ring_res[ri]

        scr_ids = {}
        scr_res = {}

        class Streamer:
            def __init__(self, specs):
                self.specs = specs
                self.pos_req = 0
                self.pos_get = 0
                self.out = 0
                self.slots = {}

            def request(self, sp):
                key, src, (a, b), scale = sp
                ri = cnt["ring"] % NRING; cnt["ring"] += 1
                rview = ring[:, ri, 0:a * b].rearrange("p (a b) -> p a b", a=a)
                if key in scr_ids:
                    pid = scr_ids[key]
                    P.load(rview, wscr[pid, :, 0:a * b].rearrange("p (a b) -> p a b", a=a), f"ring{ri}",
                           reads=[scr_res[key]], writes=[ring_res[ri]])
                else:
                    pid = len(scr_ids)
                    scr_ids[key] = pid
                    scr_res[key] = Res(f"scr{pid}")
                    if scale:
                        staged_cast(rview, src, a, b, ring_res[ri], gmixc, gains_res)
                    else:
                        staged_cast(rview, src, a, b, ring_res[ri])
                    P.load(wscr[pid, :, 0:a * b].rearrange("p (a b) -> p a b", a=a), rview, f"wst{ri}",
                           reads=[ring_res[ri]], writes=[scr_res[key]])
                return rview, ring_res[ri]

            def top_up(self):
                while self.out < NRING and self.pos_req < len(self.specs):
                    self.slots[self.pos_req] = self.request(self.specs[self.pos_req])
                    self.pos_req += 1
                    self.out += 1

            def get(self, key):
                assert self.specs[self.pos_get][0] == key, (self.specs[self.pos_get][0], key)
                if self.pos_get >= self.pos_req:
                    self.top_up()
                assert self.pos_get < self.pos_req, "ring exhausted (missing release)"
                r = self.slots.pop(self.pos_get)
                self.pos_get += 1
                return r

            def release(self, n=1):
                self.out -= n
                self.top_up()

        def make_specs(NG_):
            sp = []
            for l in range(NL):
                for g in range(NG_):
                    for pi, (c0, ncols) in enumerate(WIN_PIECES):
                        sp.append(((l, 'in', pi), W["w_in"][l][:, c0:c0 + ncols].rearrange("(k p) c -> p k c", p=128), (8, ncols), False))
                    for c in range(4):
                        sp.append(((l, 'out', c), W["w_out"][l][:, c * 256:(c + 1) * 256].rearrange("(k p) c -> p k c", p=128), (8, 256), True))
                for e8 in range(8):
                    for hh in range(2):
                        sp.append(((l, 'up', e8, hh), W["w_up"][l][:, e8 * 512 + hh * 256:e8 * 512 + (hh + 1) * 256].rearrange("(k p) c -> p k c", p=128), (8, 256), False))
                    for hh in range(2):
                        sp.append(((l, 'dn', e8, hh), W["w_down"][l][e8 * 512 + hh * 256:e8 * 512 + (hh + 1) * 256, :].rearrange("(f p) c -> p f c", p=128), (2, 1024), False))
            return sp

        def cast_load(dst_ap, src_ap, shape3, reads_extra=(), dst_res=None):
            a, b = shape3
            staged_cast(dst_ap, src_ap, a, b, dst_res)

        bup_res = Res("bup")

        def load_bup(l):
            P.load(prow[0:32, :], W["b_up"][l].rearrange("(f p) -> f p", p=128), "prow", writes=[prow_res])
            P.tr(ps[7][:, 0:32], prow[0:32, :], identf[0:32, 0:32], reads=[prow_res, const_res], writes=[ps_res[7]])
            P.copy('dve', prm[:, 0:32], ps[7][:, 0:32], reads=[ps_res[7]], writes=[bup_res])

        def load_layer_small(l, nt_s):
            for kc in range(3):
                cast_load(wuq[:, kc:kc + 1, :], W["w_uq"][l, kc * 128:(kc + 1) * 128, :].rearrange("p (a b) -> p a b", a=1),
                          (1, 768), dst_res=smallw_res)
            P.load(prow[32:36, :], W["b_s"][l], "prow", writes=[prow_res])
            P.load(prow[36:39, :], W["g_cq"][l].rearrange("(k p) -> k p", p=128), "prow", writes=[prow_res])
            P.load(prow[39:47, :], W["g_mix"][l].rearrange("(k p) -> k p", p=128), "prow", writes=[prow_res])
            P.tr(ps[7][:, 32:47], prow[32:47, :], identf[32:47, 32:47], reads=[prow_res, const_res], writes=[ps_res[7]])
            P.copy('dve', prm[:, 32:47], ps[7][:, 32:47], reads=[ps_res[7]], writes=[gains_res])
            P.tt('pool', wuq[:], wuq[:], gcqc.unsqueeze(2).to_broadcast([128, 3, 768]), ALU.mult, reads=[smallw_res, gains_res], writes=[smallw_res])
            wq4 = wuq[:].rearrange("p k (h d) -> p k h d", h=4)
            P.ts('pool', wrot[:, :, :, 0:32], wq4[:, :, :, 160:192], -1.0, None, ALU.mult, reads=[smallw_res], writes=[smallw_res])
            P.copy('pool', wrot[:, :, :, 32:64], wq4[:, :, :, 128:160], reads=[smallw_res], writes=[smallw_res])
            for kc in range(2):
                cast_load(wuk[:, kc, :].rearrange("p (h n) -> p h n", h=4),
                          W["w_uk"][l][:, kc * 128:(kc + 1) * 128, :].rearrange("h c n -> c h n"), (4, 128), dst_res=smallw_res)
                cast_load(wuv[:, kc, :].rearrange("p (h n) -> p h n", h=4),
                          W["w_uv"][l][:, kc * 128:(kc + 1) * 128, :].rearrange("h c n -> c h n"), (4, 128), dst_res=smallw_res)
            cast_load(wsb, W["w_s"][l].rearrange("g i j -> i g j"), (4, 128), dst_res=bfp_res[0])
            for g in range(4):
                pst = ps[6][:].bitcast(BF16)
                P.tr(pst[0:nt_s, g * 128:g * 128 + nt_s], wsb[0:nt_s, g, 0:nt_s], ident[0:nt_s, 0:nt_s],
                     reads=[bfp_res[0], const_res], writes=[ps_res[6]])
            P.copy('dve', wsT[0:nt_s, :, 0:nt_s], ps[6][:].bitcast(BF16)[0:nt_s, 0:512].rearrange("p (g i) -> p g i", g=4)[:, :, 0:nt_s],
                   reads=[ps_res[6]], writes=[smallw_res])
            if nt_s == 128:
                P.memset('pool', wsT[64:128, :, 0:64], 0.0, writes=[smallw_res])
            P.load(g_ckv[:], W["g_ckv"][l:l + 1, :].to_broadcast([128, 256]), "gains", writes=[gains_res])

        def load_ln(l, which):
            names = ("ln1_g", "ln1_b") if which == 1 else ("ln2_g", "ln2_b")
            for i, nm in enumerate(names):
                P.load(lnb[:, i, :], W[nm][l:l + 1, :].to_broadcast([128, 1024]), f"lnb{i}", writes=[lnb_res[i]])

        def rstd_from_ss(col_ss, col_out, n, nt):
            P.act(stat[0:nt, col_out:col_out + 1], stat[0:nt, col_ss:col_ss + 1], AF.Ln, scale=1.0 / n, bias=EPS,
                  reads=[stat_res], writes=[stat_res])
            P.act(stat[0:nt, col_out:col_out + 1], stat[0:nt, col_out:col_out + 1], AF.Exp, scale=-0.5,
                  reads=[stat_res], writes=[stat_res])

        def make_xT(t, slot, nt, dst, dst_res, col0):
            P.copy('act', xb[0:nt, :], x_t[0:nt, t, :], reads=[x_res[t]], writes=[xb_res])
            pst = ps[7][:].bitcast(BF16)
            for k in range(8):
                P.tr(pst[:, k * 128:k * 128 + nt], xb[0:nt, k * 128:(k + 1) * 128], ident[0:nt, 0:nt],
                     reads=[xb_res, const_res], writes=[ps_res[7]])
            P.copy('dve', dst[:, :, col0:col0 + nt], pst[:, :].rearrange("p (k c) -> p k c", k=8)[:, :, 0:nt],
                   reads=[ps_res[7]], writes=[dst_res])

        def layer_norm_tile(t, nt, l):
            xv = x_t[0:nt, t, :]
            P.op('dve', lambda e: e.bn_stats(stat[0:nt, 0:6], x_t[0:nt, t, 0:512]), reads=[x_res[t]], writes=[stat_res])
            P.op('dve', lambda e: e.bn_stats(stat[0:nt, 6:12], x_t[0:nt, t, 512:1024]), reads=[x_res[t]], writes=[stat_res])
            P.op('dve', lambda e: e.bn_aggr(stat[0:nt, 12:14], stat[0:nt, 0:12]), reads=[stat_res], writes=[stat_res])
            P.act(stat[0:nt, 14:15], stat[0:nt, 13:14], AF.Ln, bias=EPS, reads=[stat_res], writes=[stat_res])
            P.act(stat[0:nt, 14:15], stat[0:nt, 14:15], AF.Exp, scale=-0.5, reads=[stat_res], writes=[stat_res])
            P.ts('dve', xv, xv, stat[0:nt, 12:13], stat[0:nt, 14:15], ALU.subtract, ALU.mult,
                 reads=[x_res[t], stat_res], writes=[x_res[t]])
            P.tt('pool', xv, xv, lnb[0:nt, 0, :], ALU.mult, reads=[x_res[t], lnb_res[0]], writes=[x_res[t]])
            P.tt('pool', xv, xv, lnb[0:nt, 1, :], ALU.add, reads=[x_res[t], lnb_res[1]], writes=[x_res[t]])

        def ingest_kv(slot, nk, st_i):
            src = kvst[0:nk, st_i, :]
            P.copy('act', kvbf[0:nk, :], src, reads=[kvst_res[st_i]], writes=kvbf_rl)
            c0 = slot * 128
            pst = ps[6][:].bitcast(BF16)
            P.tr(pst[:, 0:nk], kvbf[0:nk, 0:128], ident[0:nk, 0:nk], reads=[*kvbf_rl, const_res], writes=[ps_res[6]])
            P.tr(pst[:, 128:128 + nk], kvbf[0:nk, 128:256], ident[0:nk, 0:nk], reads=kvbf_rl, writes=[ps_res[6]])
            P.tr(pst[:, 256:256 + nk], kvbf[0:nk, 320:448], ident[0:nk, 0:nk], reads=kvbf_rl, writes=[ps_res[6]])
            P.tr(pst[:, 384:384 + nk], kvbf[0:nk, 448:576], ident[0:nk, 0:nk], reads=kvbf_rl, writes=[ps_res[6]])
            P.tr(pst[0:64, 512:512 + nk], kvbf[0:nk, 256:320], ident[0:nk, 0:nk], reads=kvbf_rl, writes=[ps_res[6]])
            gi = slot % 4
            P.copy('dve', ckvT[:, :, gi * 128:gi * 128 + nk], pst[:, 0:256].rearrange("p (k c) -> p k c", k=2)[:, :, 0:nk],
                   reads=[ps_res[6]], writes=[ckvT_res[gi]])
            P.copy('dve', sbKT[:, :, c0:c0 + nk], pst[:, 256:512].rearrange("p (k c) -> p k c", k=2)[:, :, 0:nk],
                   reads=[ps_res[6]], writes=[kv_res[slot]])
            P.copy('act', krT[:, c0:c0 + nk], pst[0:64, 512:512 + nk], reads=[ps_res[6]], writes=[kv_res[slot]])
            P.copy('pool', sbV[0:nk, slot, :], kvbf[0:nk, 576:832], reads=kvbf_rl, writes=[kv_res[slot]])

        def project_keys(slots, nk, kbanks=(0, 1, 2, 3), vbanks=(4, 5)):
            ncol = (len(slots) - 1) * 128 + nk
            g0 = (slots[0] % 4) * 128
            c0 = slots[0] * 128
            rd = [ckvT_res[s % 4] for s in slots] + [smallw_res]
            for h in range(4):
                b = kbanks[h % len(kbanks)]
                for kc in range(2):
                    P.mm(ps[b][:, 0:ncol], wuk[:, kc, h * 128:(h + 1) * 128], ckvT[:, kc, g0:g0 + ncol],
                         start=(kc == 0), stop=(kc == 1), reads=rd, writes=[ps_res[b]])
                P.copy('act' if h % 2 == 0 else 'dve', knT[:, h, c0:c0 + ncol], ps[b][:, 0:ncol], reads=[ps_res[b]],
                       writes=[kv_res[s] for s in slots])
            for i, s in enumerate(slots):
                nkk = 128 if i < len(slots) - 1 else nk
                b = vbanks[i % len(vbanks)]
                for kc in range(2):
                    P.mm(ps[b][0:nkk, :], ckvT[:, kc, (s % 4) * 128:(s % 4) * 128 + nkk], wuv[:, kc, :],
                         start=(kc == 0), stop=(kc == 1), reads=[ckvT_res[s % 4], smallw_res], writes=[ps_res[b]])
                P.copy('dve' if i % 2 == 0 else 'act', vp[0:nkk, s, :, 0:128], ps[b][0:nkk, :].rearrange("p (h d) -> p h d", h=4),
                       reads=[ps_res[b]], writes=[kv_res[s]])

        def run_pass(kind, seq_i):
            prompt = (kind == "p")
            nt = 128 if prompt else NS
            NT = NTP if prompt else 1
            G = GP if prompt else 1
            NG = NT // G
            gq = G * nt
            tcs, tsn, tcF, tsF = tab[kind]
            STR = Streamer(make_specs(NG))
            if prompt:
                for t in range(NT):
                    P.load(x_t[:, t, :], xp[seq_i, t * 128:(t + 1) * 128, :], f"xin{t % 8}", writes=[x_res[t]])
            else:
                P.load(tcos[0:nt, 0, :], tcs, "tab", writes=[tab_res])
                P.load(tsin[0:nt, 0, :], tsn, "tab", writes=[tab_res])
                P.load(x_t[0:nt, 0, :], xs, "xin", writes=[x_res[0]])

            for l in range(NL):
                P.memset('pool', vp[:, :, :, 128:130], 1.0, writes=kv_res + x1T_rl + accsb_res + hT_res)
                P.mark(f"{kind}{seq_i} L{l} start")
                if l == 0:
                    load_layer_small(0, nt)
                load_bup(l)
                load_ln(l, 1)
                P.mark(f"{kind}{seq_i} L{l} small loaded")
                new_slot0 = 0 if prompt else 8

                for g in range(NG):
                    tiles = [g * G + i for i in range(G)]
                    if prompt:
                        P.load(tcosF[:, 0:gq], tcF[:, g * gq:(g + 1) * gq], "tabF", writes=[tabF_res])
                        P.load(tsinF[:, 0:gq], tsF[:, g * gq:(g + 1) * gq], "tabF", writes=[tabF_res])
                        P.load(tcos[:, 0:G, :], tcs[g * gq:(g + 1) * gq, :].rearrange("(t p) d -> p t d", p=128), "tab", writes=[tab_res])
                        P.load(tsin[:, 0:G, :], tsn[g * gq:(g + 1) * gq, :].rearrange("(t p) d -> p t d", p=128), "tab", writes=[tab_res])
                    else:
                        P.load(tcosF[:, 0:gq], tcF, "tabF", writes=[tabF_res])
                        P.load(tsinF[:, 0:gq], tsF, "tabF", writes=[tabF_res])
                    for i, t in enumerate(tiles):
                        make_xT(t, i, nt, xT, xT_res[i], i * 128)
                    pbank = [0]

                    def proj(pi):
                        c0, ncols = WIN_PIECES[pi]
                        wv, wr = STR.get((l, 'in', pi))
                        outs = []
                        for i, t in enumerate(tiles):
                            b = pbank[0] % 4; pbank[0] += 1
                            for k in range(8):
                                P.mm(ps[b][0:nt, 0:ncols], xT[:, k, i * 128:i * 128 + nt], wv[:, k, :],
                                     start=(k == 0), stop=(k == 7), reads=[xT_res[i], wr], writes=[ps_res[b]])
                            outs.append((ps[b][0:nt, 0:ncols], ps_res[b]))
                        STR.release()
                        return outs

                    def cons(pi, i, t, pv, pr):
                        slot = (new_slot0 + t) if prompt else 8
                        sti = i % 2
                        if pi == 0:
                            P.act(u_bf[0:nt, i, :], pv, AF.Gelu_apprx_tanh, reads=[pr], writes=[u_res[i]])
                            yield "evac"
                        elif pi == 1:
                            gv, gvr = f32buf()
                            yield
                            gv = gv[0:nt, 0:256]
                            yield
                            P.act(gv, pv, AF.Gelu_apprx_tanh, reads=[pr], writes=[gvr])
                            yield
                            gv3 = gv.rearrange("p (g d) -> p g d", g=4)
                            yield
                            sq, sqr = f32buf()
                            yield
                            sq = sq[0:nt, 0:256]
                            yield
                            P.op('dve', lambda e, gv3=gv3: e.tensor_reduce(stat[0:nt, 16:20], gv3, AX.X, ALU.add), reads=[gvr], writes=[stat_res])
                            yield
                            P.act(sq, gv, AF.Square, reads=[gvr], writes=[sqr])
                            yield
                            P.op('dve', lambda e, sq=sq: e.tensor_reduce(stat[0:nt, 20:24], sq.rearrange("p (g d) -> p g d", g=4), AX.X, ALU.add),
                                 reads=[sqr], writes=[stat_res])
                            yield
                            P.ts('dve', stat[0:nt, 16:20], stat[0:nt, 16:20], 1.0 / 64, None, ALU.mult, reads=[stat_res], writes=[stat_res])
                            yield
                            P.tt('dve', stat[0:nt, 24:28], stat[0:nt, 16:20], stat[0:nt, 16:20], ALU.mult, reads=[stat_res], writes=[stat_res])
                            yield
                            P.stt(stat[0:nt, 20:24], stat[0:nt, 20:24], 1.0 / 64, stat[0:nt, 24:28], ALU.mult, ALU.subtract,
                                  reads=[stat_res], writes=[stat_res])
                            yield
                            P.act(stat[0:nt, 20:24], stat[0:nt, 20:24], AF.Ln, bias=EPS, reads=[stat_res], writes=[stat_res])
                            yield
                            P.act(stat[0:nt, 20:24], stat[0:nt, 20:24], AF.Exp, scale=-0.5, reads=[stat_res], writes=[stat_res])
                            yield
                            P.tt('dve', gv3, gv3, stat[0:nt, 16:20].unsqueeze(2).to_broadcast([nt, 4, 64]), ALU.subtract,
                                 reads=[gvr, stat_res], writes=[gvr])
                            yield
                            P.tt('dve', gv3, gv3, stat[0:nt, 20:24].unsqueeze(2).to_broadcast([nt, 4, 64]), ALU.mult,
                                 reads=[gvr, stat_res], writes=[gvr])
                            yield
                            v_bf, vbf_res = bfbuf()
                            yield
                            P.copy('pool', v_bf[0:nt, :], gv, reads=[gvr], writes=[vbf_res])
                            yield
                            if not prompt:
                                P.load(o_gv[l], gv, "ogv", reads=[gvr])
                            yield
                            yield "defer"
                            for gg in range(4):
                                P.mm(ps[5][0:nt, gg * 64:(gg + 1) * 64], wsT[0:nt, gg, 0:nt], v_bf[0:nt, gg * 64:(gg + 1) * 64],
                                     reads=[vbf_res, smallw_res], writes=[ps_res[5]])
                            yield
                            ya, yar = f32buf()
                            yield
                            ya = ya[0:nt, 0:256]
                            yield
                            for gg in range(4):
                                P.stt(ya[:, gg * 64:(gg + 1) * 64], ps[5][0:nt, gg * 64:(gg + 1) * 64], bsb[0:nt, gg:gg + 1],
                                      u_bf[0:nt, i, gg * 64:(gg + 1) * 64], ALU.add, ALU.mult,
                                      reads=[ps_res[5], gains_res, u_res[i]], writes=[yar])
                            yield
                            sq3, sq3r = bfbuf()
                            P.act(sq3[0:nt, 0:256], ya, AF.Square, accum_out=stat[0:nt, 28:29], reads=[yar], writes=[sq3r, stat_res])
                            yield
                            rstd_from_ss(28, 29, 256, nt)
                            yield
                            P.ts('dve', y_t[0:nt, i, 0:256], ya, stat[0:nt, 29:30], None, ALU.mult,
                                 reads=[yar, stat_res], writes=[y_res[i]])
                            yield
                        elif pi == 2:
                            sq, sqr = f32buf()
                            yield
                            P.copy('dve', cqraw[0:nt, i, 0:256], pv, reads=[pr], writes=[cqraw_res])
                            yield "evac"
                            P.act(sq[0:nt, 0:256], cqraw[0:nt, i, 0:256], AF.Square, accum_out=stat[0:nt, 48 + 2 * i:49 + 2 * i], reads=[cqraw_res], writes=[sqr, stat_res])
                            yield
                        elif pi == 3:
                            sq, sqr = f32buf()
                            yield
                            P.copy('dve', cqraw[0:nt, i, 256:384], pv, reads=[pr], writes=[cqraw_res])
                            yield "evac"
                            P.act(sq[0:nt, 0:128], cqraw[0:nt, i, 256:384], AF.Square, accum_out=stat[0:nt, 49 + 2 * i:50 + 2 * i], reads=[cqraw_res], writes=[sqr, stat_res])
                            yield
                            P.tt('dve', stat[0:nt, 32:33], stat[0:nt, 48 + 2 * i:49 + 2 * i], stat[0:nt, 49 + 2 * i:50 + 2 * i], ALU.add, reads=[stat_res], writes=[stat_res])
                            yield
                            rstd_from_ss(32, 33, 384, nt)
                            yield
                            cqa, cqar = bfbuf()
                            cqb, cqbr = bfbuf()
                            P.ts('dve', cqa[0:nt, 0:256], cqraw[0:nt, i, 0:256], stat[0:nt, 33:34], None, ALU.mult,
                                 reads=[cqraw_res, stat_res], writes=[cqar])
                            P.ts('dve', cqb[0:nt, 0:128], cqraw[0:nt, i, 256:384], stat[0:nt, 33:34], None, ALU.mult,
                                 reads=[cqraw_res, stat_res], writes=[cqbr])
                            yield
                            yield "defer"
                            pst = ps[4][:].bitcast(BF16)
                            yield
                            for kc in range(3):
                                src_ = cqa[0:nt, kc * 128:(kc + 1) * 128] if kc < 2 else cqb[0:nt, 0:128]
                                P.tr(pst[:, kc * 128:kc * 128 + nt], src_, ident[0:nt, 0:nt],
                                     reads=[cqar if kc < 2 else cqbr, const_res], writes=[ps_res[4]])
                            yield
                            P.copy('act', cqT[:, :, i * 128:i * 128 + nt], pst[:, 0:384].rearrange("p (k c) -> p k c", k=3)[:, :, 0:nt],
                                   reads=[ps_res[4]], writes=[cqT_res[i]])
                            yield
                        elif pi == 4:
                            sq, sqr = f32buf()
                            yield
                            raw, rawr = f32buf()
                            yield
                            P.copy('dve', raw[0:nt, 0:256], pv, reads=[pr], writes=[rawr])
                            yield
                            P.act(sq[0:nt, 0:256], raw[0:nt, 0:256], AF.Square, accum_out=stat[0:nt, 34:35], reads=[rawr], writes=[sqr, stat_res])
                            yield
                            rstd_from_ss(34, 35, 256, nt)
                            yield
                            P.stt(kvst[0:nt, sti, 0:256], raw[0:nt, 0:256], stat[0:nt, 35:36], g_ckv[0:nt, :], ALU.mult, ALU.mult,
                                  reads=[rawr, stat_res, gains_res], writes=[kvst_res[sti]])
                            yield
                        elif pi == 5:
                            tt_ = i
                            yield
                            cs_ = tcos[0:nt, tt_, :]; sn_ = tsin[0:nt, tt_, :]
                            yield
                            x1 = pv[:, 0:32]; x2 = pv[:, 32:64]
                            yield
                            P.tt('dve', rtmp[0:nt, 0, :], x1, cs_, ALU.mult, reads=[pr, tab_res], writes=[rtmp_res])
                            yield
                            P.tt('dve', rtmp[0:nt, 1, :], x2, sn_, ALU.mult, reads=[pr, tab_res], writes=[rtmp_res])
                            yield
                            P.tt('dve', rtmp[0:nt, 2, :], x2, cs_, ALU.mult, reads=[pr, tab_res], writes=[rtmp_res])
                            yield
                            P.tt('dve', rtmp[0:nt, 3, :], x1, sn_, ALU.mult, reads=[pr, tab_res], writes=[rtmp_res])
                            yield
                            P.tt('pool', kvst[0:nt, sti, 256:288], rtmp[0:nt, 0, :], rtmp[0:nt, 1, :], ALU.subtract,
                                 reads=[rtmp_res], writes=[kvst_res[sti]])
                            yield
                            P.tt('pool', kvst[0:nt, sti, 288:320], rtmp[0:nt, 2, :], rtmp[0:nt, 3, :], ALU.add,
                                 reads=[rtmp_res], writes=[kvst_res[sti]])
                            yield
                        elif pi == 6:
                            sbqb, sbqb_res = bfbuf()
                            yield
                            P.act(sbqb[0:nt, :], pv, AF.Copy, scale=SB_SCALE, reads=[pr], writes=[sbqb_res])
                            yield "evac"
                            yield "defer"
                            pst = ps[4][:].bitcast(BF16)
                            yield
                            for hp in range(2):
                                P.tr(pst[:, 512 + hp * 128:512 + hp * 128 + nt], sbqb[0:nt, hp * 128:(hp + 1) * 128], ident[0:nt, 0:nt],
                                     reads=[sbqb_res, const_res], writes=[ps_res[4]])
                            yield
                            P.copy('dve', sbQT[:, :, i * 128:i * 128 + nt], pst[:, 512:768].rearrange("p (k c) -> p k c", k=2)[:, :, 0:nt],
                                   reads=[ps_res[4]], writes=[sbq_res[i]])
                            yield
                        elif pi == 7:
                            P.copy('act', kvst[0:nt, sti, 320:576], pv, reads=[pr], writes=[kvst_res[sti]])
                            yield "evac"
                        elif pi == 8:
                            P.copy('dve', kvst[0:nt, sti, 576:832], pv, reads=[pr], writes=[kvst_res[sti]])
                            yield "evac"
                            if prompt:
                                rows = slice(t * 128, (t + 1) * 128)
                                outs = [o[l, seq_i, rows, :] for o in o_p]
                            else:
                                outs = [o[l] for o in o_s]
                            yield
                            for oo, (a0, a1) in zip(outs, [(0, 256), (256, 320), (320, 576), (576, 832)]):
                                P.load(oo, kvst[0:nt, sti, a0:a1], f"okv{sti}", reads=[kvst_res[sti]])
                            yield
                            yield "defer"
                            ingest_kv(slot, nt, sti)
                            yield

                    pending = []
                    nxt = proj(0)
                    for pi in range(len(WIN_PIECES)):
                        cur = nxt
                        if pi + 1 < len(WIN_PIECES):
                            nxt = proj(pi + 1)
                        newg = [cons(pi, i, t, cur[i][0], cur[i][1]) for i, t in enumerate(tiles)]
                        if pi in (0, 2, 3, 6, 7, 8):
                            for g_ in newg:
                                for r_ in g_:
                                    if r_ == "evac":
                                        break
                        active = pending + newg
                        pending = []
                        for g_ in active:
                            for r_ in g_:
                                if r_ == "defer":
                                    pending.append(g_)
                                    break
                    for g_ in pending:
                        for r_ in g_:
                            pass
                    P.mark(f"{kind}{seq_i} L{l} g{g} phaseB")
                    if prompt:
                        project_keys([new_slot0 + t for t in tiles], 128)
                    else:
                        project_keys([8], nt, (0, 1), (0, 1))
                    rdq = cqT_res[0:G] + [smallw_res]
                    for h in range(4):
                        b = h % 2
                        for kc in range(3):
                            P.mm(ps[b][:, 0:gq], wuq[:, kc, h * 192:h * 192 + 128], cqT[:, kc, 0:gq],
                                 start=(kc == 0), stop=(kc == 2), reads=rdq, writes=[ps_res[b]])
                        P.act(qnT[:, h, 0:gq], ps[b][:, 0:gq], AF.Copy, scale=MLA_SCALE, reads=[ps_res[b]], writes=[q_res])
                        for kc in range(3):
                            P.mm(ps[2][0:64, 0:gq], wuq[:, kc, h * 192 + 128:h * 192 + 192], cqT[:, kc, 0:gq],
                                 start=(kc == 0), stop=(kc == 2), reads=rdq, writes=[ps_res[2]])
                        for kc in range(3):
                            P.mm(ps[3][0:64, 0:gq], wrot[:, kc, h, :], cqT[:, kc, 0:gq],
                                 start=(kc == 0), stop=(kc == 2), reads=rdq, writes=[ps_res[3]])
                        t1, t1r = f32buf(); t2, t2r = f32buf()
                        P.tt('dve', t1[0:64, 0:gq], ps[2][0:64, 0:gq], tcosF[:, 0:gq], ALU.mult, reads=[ps_res[2], tabF_res], writes=[t1r])
                        P.tt('dve', t2[0:64, 0:gq], ps[3][0:64, 0:gq], tsinF[:, 0:gq], ALU.mult, reads=[ps_res[3], tabF_res], writes=[t2r])
                        P.tt('pool', qrT[:, h, 0:gq], t1[0:64, 0:gq], t2[0:64, 0:gq], ALU.add, reads=[t1r, t2r], writes=[q_res])

                    P.mark(f"{kind}{seq_i} L{l} g{g} attention")
                    def key_list_prompt():
                        return [(kt, 128, (kt - g * G) if kt >= g * G else -1) for kt in range(g * G + G)]

                    def mla_attend(hp, keys, first, last, accbs, sbanks=(0, 1), stages_only=False):
                        n = len(keys)
                        stt_ = [None] * n

                        def S1(k):
                            slot, nk, di = keys[k]
                            q0 = 0 if di < 0 else di * 128
                            ncol = gq - q0
                            c0 = slot * 128
                            b = sbanks[cnt["mla"] % len(sbanks)]; cnt["mla"] += 1
                            for hh in range(2):
                                h = 2 * hp + hh
                                P.mm(ps[b][0:nk, hh * gq:hh * gq + ncol], knT[:, h, c0:c0 + nk], qnT[:, h, q0:gq], start=True, stop=False,
                                     skip_group_check=True, reads=[kv_res[slot], q_res], writes=[ps_res[b]])
                                P.mm(ps[b][0:nk, hh * gq:hh * gq + ncol], krT[:, c0:c0 + nk], qrT[:, h, q0:gq], start=False, stop=True,
                                     skip_group_check=True, reads=[kv_res[slot], q_res], writes=[ps_res[b]])
                            pT, pTr = bfpair_mla()
                            pin = ps[b][0:nk, 0:2 * gq].rearrange("p (h c) -> p h c", h=2)[:, :, 0:ncol]
                            P.act(pT[0:nk, :, 0:ncol], pin, AF.Exp, reads=[ps_res[b]], writes=[pTr])
                            if di >= 0 and prompt:
                                P.memset('pool', pT[64:128, :, 0:64], 0.0, writes=[pTr])
                            stt_[k] = (pT, pTr, q0)

                        def S2(k):
                            slot, nk, di = keys[k]
                            pT, pTr, q0 = stt_[k]
                            for hh in range(2):
                                h = 2 * hp + hh
                                for qb in range(q0 // 128, G):
                                    ab = accbs[hh] if prompt else accbs[0]
                                    col = (qb % 2) * 129 if prompt else (h % 2) * 129
                                    st_flag = first[0].get(ab, True)
                                    first[0][ab] = False
                                    nq = nt
                                    P.mm(ps[ab][0:nq, col:col + 129], pT[0:nk, hh, qb * 128 - q0:qb * 128 - q0 + nq], vp[0:nk, slot, h, 0:129],
                                         start=st_flag, stop=False, skip_group_check=True, reads=[pTr, kv_res[slot]], writes=[ps_res[ab]])

                        def fin():
                            mla_final(hp, accbs)

                        if stages_only:
                            return n, S1, S2, fin
                        for step in range(n + 1):
                            if step < n:
                                S1(step)
                            if step >= 1:
                                S2(step - 1)
                        if last:
                            fin()

                    def mla_final(hp, accbs):
                        if True:
                            for hh in range(2):
                                h = 2 * hp + hh
                                for qb in range(G):
                                    ab = accbs[hh] if prompt else accbs[0]
                                    col = (qb % 2) * 129 if prompt else (h % 2) * 129
                                    sc = 40 + (cnt["mla"] % 2); cnt["mla"] += 1
                                    P.op('dve', lambda e, ab=ab, col=col, sc=sc: e.reciprocal(stat[0:nt, sc:sc + 1], ps[ab][0:nt, col + 128:col + 129]),
                                         reads=[ps_res[ab]], writes=[stat_res])
                                    yb, ybr = mla_out[qb]
                                    P.act(yb[0:nt, h * 128:(h + 1) * 128], ps[ab][0:nt, col:col + 128], AF.Copy, scale=stat[0:nt, sc:sc + 1],
                                          reads=[ps_res[ab], stat_res], writes=[ybr])

                    def sb_attend(hp, keys, first, banks, stages_only=False):
                        z1b, z2b, pob = banks
                        n = len(keys)
                        stt_ = [None] * n
                        ares = [accsb_res[hp], accsb_res[hp + 2]]

                        def geom(k):
                            slot, nk, di = keys[k]
                            q0 = 0 if di < 0 else di * 128
                            return slot, nk, di, q0, gq - q0, slot * 128

                        def kq(hh, c0, nk, q0):
                            base = 64 * hp
                            return sbKT[base:base + 64, hh, c0:c0 + nk], sbQT[base:base + 64, hh, q0:gq]

                        def S1(k):
                            slot, nk, di, q0, ncol, c0 = geom(k)
                            j = cnt["sb"]; cnt["sb"] += 1
                            b1 = z1b[j % len(z1b)]
                            for hh in range(2):
                                kT, qT = kq(hh, c0, nk, q0)
                                P.mm(ps[b1][0:nk, hh * gq:hh * gq + ncol], kT, qT, skip_group_check=True,
                                     reads=[kv_res[slot]] + sbq_res[0:G], writes=[ps_res[b1]])
                            e_, er = f32buf()
                            ev = e_[:, 0:2 * gq].rearrange("p (h c) -> p h c", h=2)[0:nk, :, 0:ncol]
                            zin = ps[b1][0:nk, 0:2 * gq].rearrange("p (h c) -> p h c", h=2)[:, :, 0:ncol]
                            P.act(ev, zin, AF.Exp, reads=[ps_res[b1]], writes=[er])
                            sp, spr = bfpair()
                            P.act(sp[0:nk, :, 0:ncol], ev, AF.Ln, bias=1.0, reads=[er], writes=[spr])
                            if di >= 0:
                                P.tt('pool', sp[0:nk, :, 0:nt], sp[0:nk, :, 0:nt], maskSB[0:nk, 0:nt].unsqueeze(1).to_broadcast([nk, 2, nt]), ALU.mult,
                                     reads=[spr, const_res], writes=[spr])
                            stt_[k] = dict(sp=sp, spr=spr, j=j)

                        def S2(k):
                            slot, nk, di, q0, ncol, c0 = geom(k)
                            d_ = stt_[k]
                            b2 = z2b[d_["j"] % len(z2b)]
                            for hh in range(2):
                                kT, qT = kq(hh, c0, nk, q0)
                                P.mm(ps[b2][0:nk, hh * gq:hh * gq + ncol], kT, qT, start=True, stop=False, skip_group_check=True,
                                     reads=[kv_res[slot]] + sbq_res[0:G], writes=[ps_res[b2]])
                                P.mm(ps[b2][0:nk, hh * gq:hh * gq + ncol], negU[0:nk, 0:nk], d_["sp"][0:nk, hh, 0:ncol], start=False, stop=True,
                                     skip_group_check=True, reads=[d_["spr"], const_res], writes=[ps_res[b2]])
                            wT, wTr = bfpair()
                            win = ps[b2][0:nk, 0:2 * gq].rearrange("p (h c) -> p h c", h=2)[:, :, 0:ncol]
                            P.act(wT[0:nk, :, 0:ncol], win, AF.Exp, reads=[ps_res[b2]], writes=[wTr])
                            if di >= 0:
                                P.tt('pool', wT[0:nk, :, 0:nt], wT[0:nk, :, 0:nt], maskSB[0:nk, 0:nt].unsqueeze(1).to_broadcast([nk, 2, nt]), ALU.mult,
                                     reads=[wTr, const_res], writes=[wTr])
                            d_["wT"] = wT; d_["wTr"] = wTr

                        def S3(k):
                            slot, nk, di, q0, ncol, c0 = geom(k)
                            d_ = stt_[k]
                            j = d_["j"]
                            b3 = pob[j % len(pob)]
                            sp, spr, wT, wTr = d_["sp"], d_["spr"], d_["wT"], d_["wTr"]
                            qb0 = q0 // 128
                            po = ps[b3][:, 0:G * 2 * 66].rearrange("p (q h c) -> p q h c", q=G, h=2)
                            for hh in range(2):
                                h = hp + 2 * hh
                                for qb in range(qb0, G):
                                    cc = qb * 128 - q0
                                    P.mm(po[0:nt, qb, hh, 0:64], wT[0:nk, hh, cc:cc + nt], sbV[0:nk, slot, h * 64:(h + 1) * 64],
                                         skip_group_check=True, reads=[wTr, kv_res[slot]], writes=[ps_res[b3]])
                                    P.mm(po[0:nt, qb, hh, 64:66], sp[0:nk, hh, cc:cc + nt], ones1[0:nk, 0:2],
                                         skip_group_check=True, reads=[spr, const_res], writes=[ps_res[b3]])
                            acc = accsb[0:nt, qb0:G, hp:4:2, :]
                            if first[0]:
                                P.copy('dve', acc, po[0:nt, qb0:G, :, 0:64], reads=[ps_res[b3]], writes=ares)
                            else:
                                fx = fexp[0:nt, j % 2, :].rearrange("p (q h) -> p q h", h=2)[:, qb0:G, :]
                                P.act(fx, po[0:nt, qb0:G, :, 64], AF.Exp, scale=-1.0, reads=[ps_res[b3]], writes=[fexp_res[j % 2]])
                                P.tt('dve', acc, acc, fx.unsqueeze(3).to_broadcast([nt, G - qb0, 2, 64]), ALU.mult,
                                     reads=ares + [fexp_res[j % 2]], writes=ares)
                                P.tt('dve', acc, acc, po[0:nt, qb0:G, :, 0:64], ALU.add, reads=ares + [ps_res[b3]], writes=ares)
                            first[0] = False
                            stt_[k] = None

                        if stages_only:
                            return n, S1, S2, S3
                        for step in range(n + 2):
                            if 1 <= step <= n:
                                S2(step - 1)
                            if step >= 2:
                                S3(step - 2)
                            if step < n:
                                S1(step)

                    mla_out = []
                    cnt["f32"] = 0
                    for qb in range(G):
                        yb, ybr = f32buf()
                        mla_out.append((yb, ybr))

                    def finish_mla():
                        for qb in range(G):
                            yb, ybr = mla_out[qb]
                            P.act(xb[0:nt, 0:512], yb[0:nt, :], AF.Square, accum_out=stat[0:nt, 42:43], reads=[ybr], writes=[xb_res, stat_res])
                            rstd_from_ss(42, 43, 512, nt)
                            P.ts('dve', y_t[0:nt, qb, 256:768], yb[0:nt, :], stat[0:nt, 43:44], None, ALU.mult,
                                 reads=[ybr, stat_res], writes=[y_res[qb]])

                    def finish_sb():
                        for qb in range(G):
                            yc = accsb[0:nt, qb, :, :]
                            sq2, sq2r = bfbuf()
                            P.act(sq2[0:nt, 0:256].rearrange("p (h d) -> p h d", h=4), yc, AF.Square, accum_out=stat[0:nt, 44:45],
                                  reads=accsb_res, writes=[sq2r, stat_res])
                            rstd_from_ss(44, 45, 256, nt)
                            P.ts('dve', y_t[0:nt, qb, 768:1024].rearrange("p (h d) -> p h d", h=4), yc, stat[0:nt, 45:46], None, ALU.mult,
                                 reads=accsb_res + [stat_res], writes=[y_res[qb]])

                    if prompt:
                        keys = key_list_prompt()
                        cnt["f32fix"] = 2
                        cnt["mla_pool"] = True
                        for hp_ in range(2):
                            n_, M1, M2, Mfin = mla_attend(hp_, keys, [dict()], True, (1, 2), (0,), stages_only=True)
                            n2_, B1, B2, B3 = sb_attend(hp_, keys, [True], ((3,), (5, 6), (4, 7)), stages_only=True)
                            for step in range(n_ + 2):
                                if 1 <= step <= n_:
                                    B2(step - 1)
                                if step >= 2:
                                    B3(step - 2)
                                if 1 <= step <= n_:
                                    M2(step - 1)
                                if step < n_:
                                    B1(step)
                                    M1(step)
                            Mfin()
                        cnt["f32fix"] = None
                        cnt["mla_pool"] = False
                        finish_mla()
                        finish_sb()
                    else:
                        firsts_m = [[dict()]] * 4
                        firsts_s = [[True] for _ in range(4)]
                        npg = PASTL // 512

                        def ingest_group(kg):
                            slots = [(kg % 2) * 4 + i for i in range(4)]
                            for i, s_ in enumerate(slots):
                                r0 = kg * 512 + i * 128
                                sti = i % 2
                                P.load(kvst[:, sti, 0:256], c_ckv[l, r0:r0 + 128, :], f"cin{sti}", writes=[kvst_res[sti]])
                                P.load(kvst[:, sti, 256:320], c_kr[l, r0:r0 + 128, :], f"cin{sti}", writes=[kvst_res[sti]])
                                P.load(kvst[:, sti, 320:576], c_k[l, r0:r0 + 128, :], f"cin{sti}", writes=[kvst_res[sti]])
                                P.load(kvst[:, sti, 576:832], c_v[l, r0:r0 + 128, :], f"cin{sti}", writes=[kvst_res[sti]])
                                ingest_kv(s_, 128, sti)
                            project_keys(slots, 128, (0, 1), (0, 1))
                            return slots

                        nxt_slots = ingest_group(0)
                        for kg in range(npg):
                            slots = nxt_slots
                            if kg + 1 < npg:
                                nxt_slots = ingest_group(kg + 1)
                            keys = [(s_, 128, -1) for s_ in slots]
                            for hp_ in range(2):
                                n_, M1, M2, Mfin = mla_attend(hp_, keys, firsts_m[hp_], False, (4 + hp_,), (2,), stages_only=True)
                                n2_, B1, B2, B3 = sb_attend(hp_, keys, firsts_s[hp_], ((3,), (6,), (7,)), stages_only=True)
                                for step in range(n_ + 2):
                                    if 1 <= step <= n_:
                                        B2(step - 1)
                                    if step >= 2:
                                        B3(step - 2)
                                    if 1 <= step <= n_:
                                        M2(step - 1)
                                    if step < n_:
                                        B1(step)
                                        M1(step)
                        keys = [(8, nt, 0)]
                        for hp_ in range(2):
                            mla_attend(hp_, keys, firsts_m[hp_], True, (4 + hp_,), (2,))
                        finish_mla()
                        for hp_ in range(2):
                            sb_attend(hp_, keys, firsts_s[hp_], ((3,), (6,), (7,)))
                        finish_sb()

                    P.mark(f"{kind}{seq_i} L{l} g{g} phaseD")
                    yTs = [(yT[:, 0, :, :], yT_res[0]), (xb[:, :].rearrange("p (k c) -> p k c", k=8), xb_res)]
                    for i, t in enumerate(tiles):
                        yv, yr = yTs[i % 2]
                        pst = ps[6 + (i % 2)][:].bitcast(BF16)
                        for k in range(8):
                            P.tr(pst[:, k * 128:k * 128 + nt], y_t[0:nt, i, k * 128:(k + 1) * 128], ident[0:nt, 0:nt],
                                 reads=[y_res[i], const_res], writes=[ps_res[6 + (i % 2)]])
                        P.copy('act' if i % 2 == 0 else 'dve', yv[:, :, 0:nt], pst[:, :].rearrange("p (k c) -> p k c", k=8)[:, :, 0:nt],
                               reads=[ps_res[6 + (i % 2)]], writes=[yr])
                    for c in range(4):
                        wv, wr = STR.get((l, 'out', c))
                        for i, t in enumerate(tiles):
                            yv, yr = yTs[i % 2]
                            b = (c * G + i) % 4
                            for k in range(8):
                                P.mm(ps[b][0:nt, 0:256], yv[:, k, 0:nt], wv[:, k, :], start=(k == 0), stop=(k == 7),
                                     reads=[yr, wr], writes=[ps_res[b]])
                            xv = x_t[0:nt, t, c * 256:(c + 1) * 256]
                            P.stt(xv, xv, ALPHA, ps[b][0:nt, 0:256], ALU.mult, ALU.add, reads=[x_res[t], ps_res[b]], writes=[x_res[t]])
                        STR.release()
                    for i, t in enumerate(tiles):
                        layer_norm_tile(t, nt, l)

                P.mark(f"{kind}{seq_i} L{l} MLP")
                P.memset('pool', stat[:, 61:62], 0.0, writes=kv_res + x1T_rl + accsb_res + hT_res)
                P.load(lnb[:, 0, :], W["b_down"][l:l + 1, :].to_broadcast([128, 1024]), "lnb0", writes=[lnb_res[0]])
                x1T_lo = knT
                x1T_hi = vp[:].rearrange("p a b c -> p (a b c)")[:, 0:4 * KS * 128].rearrange("p (k c) -> p k c", k=4)

                class X1:
                    pass

                for t in range(NT):
                    P.copy('act', xb[0:nt, :], x_t[0:nt, t, :], reads=[x_res[t]], writes=[xb_res])
                    pst = ps[7][:].bitcast(BF16)
                    for k in range(8):
                        P.tr(pst[:, k * 128:k * 128 + nt], xb[0:nt, k * 128:(k + 1) * 128], ident[0:nt, 0:nt],
                             reads=[xb_res, const_res], writes=[ps_res[7]])
                    P.copy('dve', x1T_lo[:, :, t * 128:t * 128 + nt], pst[:, 0:512].rearrange("p (k c) -> p k c", k=4)[:, :, 0:nt],
                           reads=[ps_res[7]], writes=[x1T_rl[(t // 4) % 4]])
                    P.copy('dve', x1T_hi[:, :, t * 128:t * 128 + nt], pst[:, 512:1024].rearrange("p (k c) -> p k c", k=4)[:, :, 0:nt],
                           reads=[ps_res[7]], writes=[x1T_rl[(t // 4) % 4]])
                    xv = x_t[0:nt, t, :]
                    P.stt(xv, xv, ALPHA, lnb[0:nt, 0, :], ALU.mult, ALU.add, reads=[x_res[t], lnb_res[0]], writes=[x_res[t]])
                load_ln(l, 2)
                if l + 1 < NL:
                    load_layer_small(l + 1, nt)

                def x1T_ap(k, c0, n):
                    return (x1T_lo if k < 4 else x1T_hi)[:, k % 4, c0:c0 + n]

                for e8 in range(8):
                    wup = [STR.get((l, 'up', e8, hh)) for hh in range(2)]
                    wdn = [STR.get((l, 'dn', e8, hh)) for hh in range(2)]
                    MG = min(4, NT)
                    mq = MG * nt
                    for g in range(NT // MG):
                        c0 = g * MG * 128
                        for fc in range(4):
                            b = fc % 2
                            wv, wr = wup[fc // 2]
                            for k in range(8):
                                P.mm(ps[b][:, 0:mq], wv[:, k, (fc % 2) * 128:(fc % 2) * 128 + 128], x1T_ap(k, c0, mq),
                                     start=(k == 0), stop=(k == 7), reads=[x1T_rl[g % 4], wr], writes=[ps_res[b]])
                            r_, rr = f32buf()
                            f_idx = e8 * 4 + fc
                            P.act(r_[:, 0:mq], ps[b][:, 0:mq], AF.Relu, bias=bup[:, f_idx:f_idx + 1], reads=[ps_res[b], bup_res], writes=[rr])
                            P.act(hT[:, fc, 0:mq], r_[:, 0:mq], AF.Square, reads=[rr], writes=[hT_res[fc]])
                        if g == NT // MG - 1:
                            STR.release(2)
                        for i in range(MG):
                            t = g * MG + i
                            for nh in range(2):
                                b = 2 + (i * 2 + nh) % 4
                                for fc in range(4):
                                    wv, wr = wdn[fc // 2]
                                    P.mm(ps[b][0:nt, :], hT[:, fc, i * 128:i * 128 + nt], wv[:, fc % 2, nh * 512:(nh + 1) * 512],
                                         start=(fc == 0), stop=(fc == 3), reads=[hT_res[fc], wr], writes=[ps_res[b]])
                                xv = x_t[0:nt, t, nh * 512:(nh + 1) * 512]
                                P.tt('dve', xv, xv, ps[b][0:nt, :], ALU.add, reads=[x_res[t], ps_res[b]], writes=[x_res[t]])
                    STR.release(2)
                for t in range(NT):
                    layer_norm_tile(t, nt, l)
                    if l == NL - 1:
                        if prompt:
                            P.load(yp[seq_i, t * 128:(t + 1) * 128, :], x_t[:, t, :], "yout", reads=[x_res[t]])
                        else:
                            P.load(ys, x_t[0:nt, 0, :], "yout", reads=[x_res[0]])

        for s_i in range(nseq):
            run_pass("p", s_i)
        if do_sample:
            run_pass("s", 0)
        if os.environ.get("K_MARKS"):
            for m in P.marks:
                print("MARK", m)
            print("TOTAL OPS", P.nrec, {e: len(P.ops[e]) for e in P.ENG})
        P.emit(st)
    return nc


def rope_tables(pos):
    half = 32
    inv = (np.float32(10000.0) ** (-np.arange(half, dtype=np.float32) / np.float32(half))).astype(np.float32)
    ang = pos.astype(np.float32)[:, None] * inv[None, :]
    cos = np.cos(ang).astype(np.float32)
    sin = np.sin(ang).astype(np.float32)
    cosF = np.concatenate([cos, cos], 1).T * np.float32(MLA_SCALE)
    sinF = np.concatenate([sin, sin], 1).T * np.float32(MLA_SCALE)
    return cos, sin, np.ascontiguousarray(cosF.astype(np.float32)), np.ascontiguousarray(sinF.astype(np.float32))


_CACHE = {}


def kernel(**inputs):
    x_prompt = np.asarray(inputs["x_prompt"], np.float32)
    x_sample = np.asarray(inputs["x_sample"], np.float32)
    B, S, _ = x_prompt.shape
    NL = inputs["w_in"].shape[0]
    PASTL = inputs["cache_mla_ckv"].shape[2]
    nseq = B // N_CORES
    key = (S, NL, PASTL, nseq)
    if key not in _CACHE:
        _CACHE[key] = build_program(S, NL, PASTL, nseq)
    nc = _CACHE[key]
    tp = rope_tables(np.arange(S))
    tsm = rope_tables(PASTL + np.arange(NS))
    shared = {}
    for k in WNAMES:
        a = np.ascontiguousarray(np.asarray(inputs[k], np.float32))
        if k == "w_uq":
            a = a.reshape(NL, 384, 768)
        shared[k] = a
    for nm, arr in zip(["tp_cos", "tp_sin", "tp_cosF", "tp_sinF"], tp):
        shared[nm] = arr
    for nm, arr in zip(["ts_cos", "ts_sin", "ts_cosF", "ts_sinF"], tsm):
        shared[nm] = arr
    in_maps = []
    for c in range(N_CORES):
        m = dict(shared)
        m["xp"] = np.ascontiguousarray(x_prompt[c * nseq:(c + 1) * nseq])
        m["xs"] = np.ascontiguousarray(x_sample[c])
        m["c_ckv"] = np.ascontiguousarray(np.asarray(inputs["cache_mla_ckv"], np.float32)[:, c])
        m["c_kr"] = np.ascontiguousarray(np.asarray(inputs["cache_mla_krope"], np.float32)[:, c])
        m["c_k"] = np.ascontiguousarray(np.asarray(inputs["cache_sb_k"], np.float32)[:, c].reshape(NL, PASTL, 256))
        m["c_v"] = np.ascontiguousarray(np.asarray(inputs["cache_sb_v"], np.float32)[:, c].reshape(NL, PASTL, 256))
        in_maps.append(m)
    res = run_bass_kernel_spmd(nc, in_maps, core_ids=list(range(N_CORES)))
    R = res.results
    y_p = np.concatenate([r["yp"] for r in R], 0)
    y_s = np.stack([r["ys"] for r in R], 0)
    ckv_p = np.concatenate([r["o_ckv_p"] for r in R], 1)
    kr_p = np.concatenate([r["o_kr_p"] for r in R], 1)
    k_p = np.concatenate([r["o_k_p"] for r in R], 1).reshape(NL, B, S, 4, 64)
    v_p = np.concatenate([r["o_v_p"] for r in R], 1).reshape(NL, B, S, 4, 64)
    ckv_s = np.stack([r["o_ckv_s"] for r in R], 1)
    kr_s = np.stack([r["o_kr_s"] for r in R], 1)
    k_s = np.stack([r["o_k_s"] for r in R], 1).reshape(NL, len(R), NS, 4, 64)
    v_s = np.stack([r["o_v_s"] for r in R], 1).reshape(NL, len(R), NS, 4, 64)
    gv_s = np.stack([r["o_gv_s"] for r in R], 1).reshape(NL, len(R), NS, 4, 64)
    return (y_p, y_s, ckv_p, kr_p, k_p, v_p, ckv_s, kr_s, k_s, v_s, gv_s)
```

```python
import os
import numpy as np
from contextlib import ExitStack
import concourse.bass as bass
import concourse.mybir as mybir
from concourse.bass_utils import run_bass_kernel_spmd

F32 = mybir.dt.float32
BF16 = mybir.dt.bfloat16
AF = mybir.ActivationFunctionType
ALU = mybir.AluOpType
AX = mybir.AxisListType

D = 1024
NLAYERS = 4
SEQ = 2048
PAST = 4096
NS = 16
DFF = 4096
WIN = 1984
ALPHA = float((2 * 4) ** 0.25)
EPS = 1e-5
MLA_SCALE = float(192 ** -0.5)
SB_SCALE = float(64 ** -0.5)
N_CORES = 8


class Res:
    __slots__ = ("name", "w", "r")

    def __init__(self, name=""):
        self.name = name
        self.w = None
        self.r = {}


class Prog:
    ENG = ("pe", "act", "dve", "pool", "sp")
    EPOCH = 30000

    def __init__(self, nc, same_engine_sync=True):
        self.nc = nc
        self.ops = {e: [] for e in self.ENG}
        self.dma_cnt = {}
        self.dma_cnt_raw = {}
        self.dma_maxwait = {}
        self.same_engine_sync = same_engine_sync
        self.nrec = 0
        self.limit = int(os.environ.get("K_LIMIT", "0")) or None
        self.marks = []
        self.trace_lines = bool(os.environ.get("K_TRACE"))

    def _collect(self, eng, reads, writes):
        toks = []
        for r in reads:
            if r.w is not None:
                toks.append(r.w)
        for w in writes:
            if w.w is not None:
                toks.append(w.w)
            toks.extend(w.r.values())
        waits = []
        for t in toks:
            if t[0] == 'e':
                if t[1] == eng and (eng == 'pe' or not self.same_engine_sync):
                    continue
                self.ops[t[1]][t[2]][2] = True
                waits.append(t)
            else:
                v = self.dma_cnt[t[1]] * 16
                waits.append(('d', t[1], v))
                if self.dma_maxwait.get(t[1], 0) < v:
                    self.dma_maxwait[t[1]] = v
        return waits

    def mark(self, name):
        self.marks.append((name, self.nrec, len(self.ops['pe'])))

    def op(self, eng, fn, reads=(), writes=()):
        self.nrec += 1
        if self.trace_lines:
            import sys as _s
            f = _s._getframe(1)
            ln = []
            while f is not None and len(ln) < 3:
                ln.append(f.f_lineno); f = f.f_back
            print("OP", self.nrec, eng, ln)
        if self.limit is not None and self.nrec > self.limit:
            return None
        waits = self._collect(eng, reads, writes)
        idx = len(self.ops[eng])
        self.ops[eng].append([fn, waits, False, None])
        tok = ('e', eng, idx)
        for r in reads:
            r.r[eng] = tok
        for w in writes:
            w.w = tok
            w.r = {}
        return tok

    def dma(self, q, fn, sem, reads=(), writes=()):
        self.nrec += 1
        if self.trace_lines:
            import sys as _s
            f = _s._getframe(1)
            ln = []
            while f is not None and len(ln) < 3:
                ln.append(f.f_lineno); f = f.f_back
            print("OP", self.nrec, "dma:" + sem, ln)
        if self.limit is not None and self.nrec > self.limit:
            return None
        sem = f"{sem}_{self.dma_cnt_raw.get(sem, 0) // 1500}"
        base = sem.rsplit("_", 1)[0]
        self.dma_cnt_raw[base] = self.dma_cnt_raw.get(base, 0) + 1
        waits = self._collect(q, reads, writes)
        if self.dma_maxwait.get(sem, 0) > 0:
            waits.append(('d', sem, self.dma_maxwait[sem]))
        self.ops[q].append([fn, waits, False, sem])
        self.dma_cnt[sem] = self.dma_cnt.get(sem, 0) + 1
        tok = ('d', sem, self.dma_cnt[sem] * 16)
        for r in reads:
            r.r['d' + sem] = tok
        for w in writes:
            w.w = tok
            w.r = {}
        return tok

    def emit(self, stack):
        nc = self.nc
        E = self.EPOCH
        sig = {}
        nsig = {}
        for e in self.ENG:
            n = 0
            for i, o in enumerate(self.ops[e]):
                if o[2]:
                    n += 1
                    sig[(e, i)] = n
            nsig[e] = n
        semh = {}
        for e in self.ENG:
            for k in range((nsig[e] + E - 1) // E):
                semh[('e', e, k)] = stack.enter_context(nc.semaphore(f"s_{e}_{k}"))
        for name in self.dma_cnt:
            semh[('d', name)] = stack.enter_context(nc.semaphore(f"d_{name}"))
        block = stack.enter_context(nc.Block())
        ops = self.ops
        dma_cnt = self.dma_cnt

        def run(e, eng):
            waited = {}
            for i, (fn, waits, signal, dsem) in enumerate(ops[e]):
                need = {}
                for t in waits:
                    if t[0] == 'e':
                        n = sig[(t[1], t[2])]
                        key = ('e', t[1], (n - 1) // E)
                        val = (n - 1) % E + 1
                    else:
                        key = ('d', t[1])
                        val = t[2]
                    if need.get(key, 0) < val:
                        need[key] = val
                for key, val in need.items():
                    if waited.get(key, 0) >= val:
                        continue
                    waited[key] = val
                    eng.wait_ge(semh[key], val)
                ins = fn(eng)
                if signal:
                    n = sig[(e, i)]
                    ins.then_inc(semh[('e', e, (n - 1) // E)], 1)
                if dsem is not None:
                    ins.then_inc(semh[('d', dsem)], 16)
            if e == 'sp':
                for name, c in dma_cnt.items():
                    eng.wait_ge(semh[('d', name)], c * 16)

        @block.tensor
        def _(pe):
            run('pe', pe)

        @block.scalar
        def _(act):
            run('act', act)

        @block.vector
        def _(dve):
            run('dve', dve)

        @block.gpsimd
        def _(pool):
            run('pool', pool)

        @block.sync
        def _(sp):
            run('sp', sp)

    def mm(self, out, lhsT, rhs, start=True, stop=True, reads=(), writes=(), **kw):
        return self.op('pe', lambda e: e.matmul(out, lhsT, rhs, start=start, stop=stop, **kw), reads, writes)

    def tr(self, out, in_, ident, reads=(), writes=()):
        return self.op('pe', lambda e: e.transpose(out, in_, ident), reads, writes)

    def act(self, out, in_, func, reads=(), writes=(), **kw):
        return self.op('act', lambda e: e.activation(out, in_, func, **kw), reads, writes)

    def tt(self, eng, out, in0, in1, op, reads=(), writes=()):
        return self.op(eng, lambda e: e.tensor_tensor(out, in0, in1, op), reads, writes)

    def ts(self, eng, out, in0, s1, s2, op0, op1=None, reads=(), writes=(), **kw):
        if op1 is None:
            return self.op(eng, lambda e: e.tensor_scalar(out, in0, s1, None, op0, **kw), reads, writes)
        return self.op(eng, lambda e: e.tensor_scalar(out, in0, s1, s2, op0, op1, **kw), reads, writes)

    def stt(self, out, in0, scalar, in1, op0, op1, reads=(), writes=(), **kw):
        return self.op('dve', lambda e: e.scalar_tensor_tensor(out, in0, scalar, in1, op0, op1, **kw), reads, writes)

    def copy(self, eng, out, in_, reads=(), writes=()):
        if eng == 'act':
            return self.op('act', lambda e: e.copy(out, in_), reads, writes)
        return self.op(eng, lambda e: e.tensor_copy(out, in_), reads, writes)

    def memset(self, eng, ap, val, writes=()):
        return self.op(eng, lambda e: e.memset(ap, val), (), writes)

    def load(self, out, in_, sem, reads=(), writes=(), q='sp', **kw):
        return self.dma(q, lambda e: e.dma_start(out, in_, **kw), sem, reads, writes)


WNAMES = ["w_in", "w_s", "b_s", "g_cq", "g_ckv", "w_uq", "w_uk", "w_uv", "g_mix", "w_out",
          "ln1_g", "ln1_b", "w_up", "b_up", "w_down", "b_down", "ln2_g", "ln2_b"]
WIN_PIECES = [(0, 256), (256, 256), (512, 256), (768, 128), (896, 256), (1152, 64), (1216, 256), (1472, 256), (1728, 256)]


def build_program(S=SEQ, NL=NLAYERS, PASTL=PAST, nseq=2, do_sample=True):
    nc = bass.Bass("TRN2", target_bir_lowering=False)

    def din(name, shape):
        return nc.dram_tensor(name, list(shape), F32, kind="ExternalInput").ap()

    def dout(name, shape):
        return nc.dram_tensor(name, list(shape), F32, kind="ExternalOutput").ap()

    xp = din("xp", [nseq, S, D])
    xs = din("xs", [NS, D])
    c_ckv = din("c_ckv", [NL, PASTL, 256])
    c_kr = din("c_kr", [NL, PASTL, 64])
    c_k = din("c_k", [NL, PASTL, 256])
    c_v = din("c_v", [NL, PASTL, 256])
    W = {}
    wshapes = {"w_in": [NL, D, WIN], "w_s": [NL, 4, 128, 128], "b_s": [NL, 4, 128], "g_cq": [NL, 384],
               "g_ckv": [NL, 256], "w_uq": [NL, 384, 768], "w_uk": [NL, 4, 256, 128], "w_uv": [NL, 4, 256, 128],
               "g_mix": [NL, 1024], "w_out": [NL, 1024, 1024], "ln1_g": [NL, 1024], "ln1_b": [NL, 1024],
               "w_up": [NL, 1024, DFF], "b_up": [NL, DFF], "w_down": [NL, DFF, 1024], "b_down": [NL, 1024],
               "ln2_g": [NL, 1024], "ln2_b": [NL, 1024]}
    for k in WNAMES:
        W[k] = din(k, wshapes[k])
    tab = {"p": (din("tp_cos", [S, 32]), din("tp_sin", [S, 32]), din("tp_cosF", [64, S]), din("tp_sinF", [64, S])),
           "s": (din("ts_cos", [NS, 32]), din("ts_sin", [NS, 32]), din("ts_cosF", [64, NS]), din("ts_sinF", [64, NS]))}
    yp = dout("yp", [nseq, S, D])
    ys = dout("ys", [NS, D])
    o_p = (dout("o_ckv_p", [NL, nseq, S, 256]), dout("o_kr_p", [NL, nseq, S, 64]),
           dout("o_k_p", [NL, nseq, S, 256]), dout("o_v_p", [NL, nseq, S, 256]))
    o_s = (dout("o_ckv_s", [NL, NS, 256]), dout("o_kr_s", [NL, NS, 64]),
           dout("o_k_s", [NL, NS, 256]), dout("o_v_s", [NL, NS, 256]))
    o_gv = dout("o_gv_s", [NL, NS, 256])

    NTP = S // 128
    KS = max(NTP, 9)
    NPIECE = NL * (9 + 4 + 32)
    wscr = nc.dram_tensor("wscr", [NPIECE, 128, 2048], BF16, kind="Internal").ap()

    with ExitStack() as st:
        P = Prog(nc, same_engine_sync=not bool(os.environ.get("K_NOSES")))

        def sb(name, shape, dt=F32):
            return st.enter_context(nc.sbuf_tensor("sb_" + name, list(shape), dt))

        def RL(name, n):
            return [Res(f"{name}{i}") for i in range(n)]

        x_t = sb("x_t", [128, NTP, D]); x_res = RL("x", NTP)
        GP = 2
        xT = sb("xT", [128, 8, GP * 128], BF16); xT_res = RL("xT", 4)
        y_t = sb("y_t", [128, GP, D], BF16); y_res = RL("y", 4)
        knT = sb("knT", [128, 4, KS * 128], BF16)
        krT = sb("krT", [64, KS * 128], BF16)
        vp = sb("vp", [128, KS, 4, 130], BF16)
        sbKT = sb("sbKT", [128, 2, KS * 128], BF16)
        sbV = sb("sbV", [128, KS, 256], BF16)
        kv_res = RL("kv", KS)
        x1T_rl = RL("x1T", 4)
        qnT = sb("qnT", [128, 4, GP * 128], BF16)
        qrT = sb("qrT", [64, 4, GP * 128], BF16)
        sbQT = sb("sbQT", [128, 2, GP * 128], BF16)
        cqT = sb("cqT", [128, 3, GP * 128], BF16)
        ckvT = sb("ckvT", [128, 2, 512], BF16)
        q_res = Res("q"); sbq_res = RL("sbq", 4); cqT_res = RL("cqT", 4); ckvT_res = RL("ckvT", 4)
        wuq = sb("wuq", [128, 3, 768], BF16); wrot = sb("wrot", [128, 3, 4, 64], BF16)
        wuk = sb("wuk", [128, 2, 512], BF16); wuv = sb("wuv", [128, 2, 512], BF16)
        wsT = sb("wsT", [128, 4, 128], BF16)
        smallw_res = Res("smallw")
        g_ckv = sb("g_ckv", [128, 256])
        lnb = sb("lnb", [128, 2, 1024]); lnb_res = RL("lnb", 2)
        prm = sb("prm", [128, 47])
        bup = prm[:, 0:32]; bsb = prm[:, 32:36]
        prow_res = None
        identf = sb("identf", [64, 64])
        gains_res = Res("gains")
        NRING = 4
        ring = sb("ring", [128, NRING, 2048], BF16); ring_res = RL("ring", NRING)
        stg = sb("stg", [128, 2, 512]); stg_res = RL("stg", 2)
        tcos = sb("tcos", [128, GP, 32]); tsin = sb("tsin", [128, GP, 32])
        tcosF = sb("tcosF", [64, GP * 128]); tsinF = sb("tsinF", [64, GP * 128])
        tab_res = Res("tab"); tabF_res = Res("tabF")
        ident = sb("ident", [128, 128], BF16); negU = sb("negU", [128, 128], BF16)
        maskSB = sb("maskSB", [128, 128], BF16); ones1 = sb("ones1", [128, 2], BF16)
        const_res = Res("const")
        u_bf = sb("u_bf", [128, GP, 256], BF16); u_res = RL("u", 4)
        f32t = sb("f32t", [128, 3, 512]); f32_res = RL("f32t", 3)
        cqraw = sb("cqraw", [128, GP, 384]); cqraw_res = Res("cqraw")
        kvst = sb("kvst", [128, 2, 832]); kvst_res = RL("kvst", 2)
        bf512 = sb("bf512", [128, 12, 256], BF16)
        bfp_res = RL("bfp", 6)
        bf_res = [bfp_res[i // 2] for i in range(8)]
        kvbf = bf512[:, 8:12, :].rearrange("p a b -> p (a b)")[:, 0:832]
        kvbf_rl = [bfp_res[4], bfp_res[5]]
        wsb = bf512[:, 0:2, :].rearrange("p a (g j) -> p (a g) j", g=2)
        hacc = sb("hacc", [128, 2048], BF16)
        accsb = hacc[:, 0:GP * 512].bitcast(F32).rearrange("p (q h d) -> p q h d", q=GP, h=4); accsb_res = RL("accsb", 4)
        fexp = sb("fexp", [128, 2, 4]); fexp_res = RL("fexp", 2)
        yT = sb("yT", [128, 1, 8, 128], BF16); yT_res = RL("yT", 1)
        xb = sb("xb", [128, 1024], BF16); xb_res = Res("xb")
        hT = hacc[:, :].rearrange("p (f c) -> p f c", f=4); hT_res = RL("hT", 4)
        stat = sb("stat", [128, 64]); stat_res = Res("stat")
        rtmp = sb("rtmp", [128, 4, 32]); rtmp_res = Res("rtmp")
        prow_res = rtmp_res
        prow = rtmp[0:47, :, :].rearrange("p a b -> p (a b)")

        ps = [st.enter_context(nc.psum_tensor(f"ps{i}", [128, 512], F32)) for i in range(8)]
        ps_res = RL("ps", 8)

        P.memset('pool', ident[:], 1.0, writes=[const_res])
        P.op('pool', lambda e: e.affine_select(ident[:], ident[:], pattern=[[-1, 128]], compare_op=ALU.is_equal,
                                               fill=0.0, base=0, channel_multiplier=1), writes=[const_res])
        P.memset('pool', negU[:], -1.0, writes=[const_res])
        P.op('pool', lambda e: e.affine_select(negU[:], negU[:], pattern=[[-1, 128]], compare_op=ALU.is_ge,
                                               fill=0.0, base=0, channel_multiplier=1), writes=[const_res])
        P.memset('pool', maskSB[:], 1.0, writes=[const_res])
        P.op('pool', lambda e: e.affine_select(maskSB[:], maskSB[:], pattern=[[1, 128]], compare_op=ALU.is_gt,
                                               fill=0.0, base=0, channel_multiplier=-1), writes=[const_res])
        P.memset('pool', ones1[:], 1.0, writes=[const_res])
        P.memset('pool', identf[:], 1.0, writes=[const_res])
        P.op('pool', lambda e: e.affine_select(identf[:], identf[:], pattern=[[-1, 64]], compare_op=ALU.is_equal,
                                               fill=0.0, base=0, channel_multiplier=1), writes=[const_res])
        gcqc = prm[:, 36:39]; gmixc = prm[:, 39:47]
        P.memset('pool', vp[:, :, :, 128:130], 1.0, writes=kv_res)

        cnt = {"f32": 0, "bf": 0, "stg": 0, "ring": 0, "mla": 0, "sb": 0, "bfp": 0, "mlap": 0}

        def f32buf():
            if cnt.get("f32fix") is not None:
                i = cnt["f32fix"]
            else:
                i = cnt["f32"] % 3; cnt["f32"] += 1
            return f32t[:, i, :], f32_res[i]

        def bfpair_mla():
            if cnt.get("mla_pool"):
                j = 4 + cnt["mlap"] % 2; cnt["mlap"] += 1
                return bf512[:, 2 * j:2 * j + 2, :], bfp_res[j]
            return bfpair()

        def bfbuf():
            i = cnt["bf"] % 8; cnt["bf"] += 1
            return bf512[:, i, :], bf_res[i]

        class PairRes:
            pass

        def bfpair():
            j = cnt["bfp"] % 4; cnt["bfp"] += 1
            cnt["bf"] = 2 * j + 2
            return bf512[:, 2 * j:2 * j + 2, :], bfp_res[j]

        def quarters(a, b):
            out = []
            if a * b <= 512:
                return [(0, a, 0, b)]
            if b <= 512:
                step = max(1, 512 // b)
                for a0 in range(0, a, step):
                    out.append((a0, min(a, a0 + step), 0, b))
            else:
                for a0 in range(a):
                    for b0 in range(0, b, 512):
                        out.append((a0, a0 + 1, b0, min(b, b0 + 512)))
            return out

        def staged_cast(dst3, src3, a, b, dst_res, scale_cols=None, scale_res=None):
            for (a0, a1, b0, b1) in quarters(a, b):
                si = cnt["stg"] % 2; cnt["stg"] += 1
                na, nb = a1 - a0, b1 - b0
                sview = stg[:, si, 0:na * nb].rearrange("p (a b) -> p a b", a=na)
                P.load(sview, src3[:, a0:a1, b0:b1], f"stg{si}", writes=[stg_res[si]])
                ceng = 'pool' if si == 0 else 'dve'
                if scale_cols is None:
                    P.copy(ceng, dst3[:, a0:a1, b0:b1], sview, reads=[stg_res[si]], writes=[dst_res])
                else:
                    P.tt('pool', dst3[:, a0:a1, b0:b1], sview, scale_cols[:, a0:a1].unsqueeze(2).to_broadcast([128, na, nb]), ALU.mult,
                         reads=[stg_res[si], scale_res], writes=[dst_res])

        def stream_piece(src_ap, shape3, scale_cols=None, scale_res=None):
            a, b = shape3
            ri = cnt["ring"] % NRING; cnt["ring"] += 1
            rview = ring[:, ri, 0:a * b].rearrange("p (a b) -> p a b", a=a)
            staged_cast(rview, src_ap, a, b, ring_res[ri], scale_cols, scale_res)
            return rview, ring_res[ri]

        scr_ids = {}
        scr_res = {}

        class Streamer:
            def __init__(self, specs):
                self.specs = specs
                self.pos_req = 0
                self.pos_get = 0
                self.out = 0
                self.slots = {}

            def request(self, sp):
                key, src, (a, b), scale = sp
                ri = cnt["ring"] % NRING; cnt["ring"] += 1
                rview = ring[:, ri, 0:a * b].rearrange("p (a b) -> p a b", a=a)
                if key in scr_ids:
                    pid = scr_ids[key]
                    P.load(rview, wscr[pid, :, 0:a * b].rearrange("p (a b) -> p a b", a=a), f"ring{ri}",
                           reads=[scr_res[key]], writes=[ring_res[ri]])
                else:
                    pid = len(scr_ids)
                    scr_ids[key] = pid
                    scr_res[key] = Res(f"scr{pid}")
                    if scale:
                        staged_cast(rview, src, a, b, ring_res[ri], gmixc, gains_res)
                    else:
                        staged_cast(rview, src, a, b, ring_res[ri])
                    P.load(wscr[pid, :, 0:a * b].rearrange("p (a b) -> p a b", a=a), rview, f"wst{ri}",
                           reads=[ring_res[ri]], writes=[scr_res[key]])
                return rview, ring_res[ri]

            def top_up(self):
                while self.out < NRING and self.pos_req < len(self.specs):
                    self.slots[self.pos_req] = self.request(self.specs[self.pos_req])
                    self.pos_req += 1
                    self.out += 1

            def get(self, key):
                assert self.specs[self.pos_get][0] == key, (self.specs[self.pos_get][0], key)
                if self.pos_get >= self.pos_req:
                    self.top_up()
                assert self.pos_get < self.pos_req, "ring exhausted (missing release)"
                r = self.slots.pop(self.pos_get)
                self.pos_get += 1
                return r

            def release(self, n=1):
                self.out -= n
                self.top_up()

        def make_specs(NG_):
            sp = []
            for l in range(NL):
                for g in range(NG_):
                    for pi, (c0, ncols) in enumerate(WIN_PIECES):
                        sp.append(((l, 'in', pi), W["w_in"][l][:, c0:c0 + ncols].rearrange("(k p) c -> p k c", p=128), (8, ncols), False))
                    for c in range(4):
                        sp.append(((l, 'out', c), W["w_out"][l][:, c * 256:(c + 1) * 256].rearrange("(k p) c -> p k c", p=128), (8, 256), True))
                for e8 in range(8):
                    for hh in range(2):
                        sp.append(((l, 'up', e8, hh), W["w_up"][l][:, e8 * 512 + hh * 256:e8 * 512 + (hh + 1) * 256].rearrange("(k p) c -> p k c", p=128), (8, 256), False))
                    for hh in range(2):
                        sp.append(((l, 'dn', e8, hh), W["w_down"][l][e8 * 512 + hh * 256:e8 * 512 + (hh + 1) * 256, :].rearrange("(f p) c -> p f c", p=128), (2, 1024), False))
            return sp

        def cast_load(dst_ap, src_ap, shape3, reads_extra=(), dst_res=None):
            a, b = shape3
            staged_cast(dst_ap, src_ap, a, b, dst_res)

        bup_res = Res("bup")

        def load_bup(l):
            P.load(prow[0:32, :], W["b_up"][l].rearrange("(f p) -> f p", p=128), "prow", writes=[prow_res])
            P.tr(ps[7][:, 0:32], prow[0:32, :], identf[0:32, 0:32], reads=[prow_res, const_res], writes=[ps_res[7]])
            P.copy('dve', prm[:, 0:32], ps[7][:, 0:32], reads=[ps_res[7]], writes=[bup_res])

        def load_layer_small(l, nt_s):
            for kc in range(3):
                cast_load(wuq[:, kc:kc + 1, :], W["w_uq"][l, kc * 128:(kc + 1) * 128, :].rearrange("p (a b) -> p a b", a=1),
                          (1, 768), dst_res=smallw_res)
            P.load(prow[32:36, :], W["b_s"][l], "prow", writes=[prow_res])
            P.load(prow[36:39, :], W["g_cq"][l].rearrange("(k p) -> k p", p=128), "prow", writes=[prow_res])
            P.load(prow[39:47, :], W["g_mix"][l].rearrange("(k p) -> k p", p=128), "prow", writes=[prow_res])
            P.tr(ps[7][:, 32:47], prow[32:47, :], identf[32:47, 32:47], reads=[prow_res, const_res], writes=[ps_res[7]])
            P.copy('dve', prm[:, 32:47], ps[7][:, 32:47], reads=[ps_res[7]], writes=[gains_res])
            P.tt('pool', wuq[:], wuq[:], gcqc.unsqueeze(2).to_broadcast([128, 3, 768]), ALU.mult, reads=[smallw_res, gains_res], writes=[smallw_res])
            wq4 = wuq[:].rearrange("p k (h d) -> p k h d", h=4)
            P.ts('pool', wrot[:, :, :, 0:32], wq4[:, :, :, 160:192], -1.0, None, ALU.mult, reads=[smallw_res], writes=[smallw_res])
            P.copy('pool', wrot[:, :, :, 32:64], wq4[:, :, :, 128:160], reads=[smallw_res], writes=[smallw_res])
            for kc in range(2):
                cast_load(wuk[:, kc, :].rearrange("p (h n) -> p h n", h=4),
                          W["w_uk"][l][:, kc * 128:(kc + 1) * 128, :].rearrange("h c n -> c h n"), (4, 128), dst_res=smallw_res)
                cast_load(wuv[:, kc, :].rearrange("p (h n) -> p h n", h=4),
                          W["w_uv"][l][:, kc * 128:(kc + 1) * 128, :].rearrange("h c n -> c h n"), (4, 128), dst_res=smallw_res)
            cast_load(wsb, W["w_s"][l].rearrange("g i j -> i g j"), (4, 128), dst_res=bfp_res[0])
            for g in range(4):
                pst = ps[6][:].bitcast(BF16)
                P.tr(pst[0:nt_s, g * 128:g * 128 + nt_s], wsb[0:nt_s, g, 0:nt_s], ident[0:nt_s, 0:nt_s],
                     reads=[bfp_res[0], const_res], writes=[ps_res[6]])
            P.copy('dve', wsT[0:nt_s, :, 0:nt_s], ps[6][:].bitcast(BF16)[0:nt_s, 0:512].rearrange("p (g i) -> p g i", g=4)[:, :, 0:nt_s],
                   reads=[ps_res[6]], writes=[smallw_res])
            if nt_s == 128:
                P.memset('pool', wsT[64:128, :, 0:64], 0.0, writes=[smallw_res])
            P.load(g_ckv[:], W["g_ckv"][l:l + 1, :].to_broadcast([128, 256]), "gains", writes=[gains_res])

        def load_ln(l, which):
            names = ("ln1_g", "ln1_b") if which == 1 else ("ln2_g", "ln2_b")
            for i, nm in enumerate(names):
                P.load(lnb[:, i, :], W[nm][l:l + 1, :].to_broadcast([128, 1024]), f"lnb{i}", writes=[lnb_res[i]])

        def rstd_from_ss(col_ss, col_out, n, nt):
            P.act(stat[0:nt, col_out:col_out + 1], stat[0:nt, col_ss:col_ss + 1], AF.Ln, scale=1.0 / n, bias=EPS,
                  reads=[stat_res], writes=[stat_res])
            P.act(stat[0:nt, col_out:col_out + 1], stat[0:nt, col_out:col_out + 1], AF.Exp, scale=-0.5,
                  reads=[stat_res], writes=[stat_res])

        def make_xT(t, slot, nt, dst, dst_res, col0):
            P.copy('act', xb[0:nt, :], x_t[0:nt, t, :], reads=[x_res[t]], writes=[xb_res])
            pst = ps[7][:].bitcast(BF16)
            for k in range(8):
                P.tr(pst[:, k * 128:k * 128 + nt], xb[0:nt, k * 128:(k + 1) * 128], ident[0:nt, 0:nt],
                     reads=[xb_res, const_res], writes=[ps_res[7]])
            P.copy('dve', dst[:, :, col0:col0 + nt], pst[:, :].rearrange("p (k c) -> p k c", k=8)[:, :, 0:nt],
                   reads=[ps_res[7]], writes=[dst_res])

        def layer_norm_tile(t, nt, l):
            xv = x_t[0:nt, t, :]
            P.op('dve', lambda e: e.bn_stats(stat[0:nt, 0:6], x_t[0:nt, t, 0:512]), reads=[x_res[t]], writes=[stat_res])
            P.op('dve', lambda e: e.bn_stats(stat[0:nt, 6:12], x_t[0:nt, t, 512:1024]), reads=[x_res[t]], writes=[stat_res])
            P.op('dve', lambda e: e.bn_aggr(stat[0:nt, 12:14], stat[0:nt, 0:12]), reads=[stat_res], writes=[stat_res])
            P.act(stat[0:nt, 14:15], stat[0:nt, 13:14], AF.Ln, bias=EPS, reads=[stat_res], writes=[stat_res])
            P.act(stat[0:nt, 14:15], stat[0:nt, 14:15], AF.Exp, scale=-0.5, reads=[stat_res], writes=[stat_res])
            P.ts('dve', xv, xv, stat[0:nt, 12:13], stat[0:nt, 14:15], ALU.subtract, ALU.mult,
                 reads=[x_res[t], stat_res], writes=[x_res[t]])
            P.tt('pool', xv, xv, lnb[0:nt, 0, :], ALU.mult, reads=[x_res[t], lnb_res[0]], writes=[x_res[t]])
            P.tt('pool', xv, xv, lnb[0:nt, 1, :], ALU.add, reads=[x_res[t], lnb_res[1]], writes=[x_res[t]])

        def ingest_kv(slot, nk, st_i):
            src = kvst[0:nk, st_i, :]
            P.copy('act', kvbf[0:nk, :], src, reads=[kvst_res[st_i]], writes=kvbf_rl)
            c0 = slot * 128
            pst = ps[6][:].bitcast(BF16)
            P.tr(pst[:, 0:nk], kvbf[0:nk, 0:128], ident[0:nk, 0:nk], reads=[*kvbf_rl, const_res], writes=[ps_res[6]])
            P.tr(pst[:, 128:128 + nk], kvbf[0:nk, 128:256], ident[0:nk, 0:nk], reads=kvbf_rl, writes=[ps_res[6]])
            P.tr(pst[:, 256:256 + nk], kvbf[0:nk, 320:448], ident[0:nk, 0:nk], reads=kvbf_rl, writes=[ps_res[6]])
            P.tr(pst[:, 384:384 + nk], kvbf[0:nk, 448:576], ident[0:nk, 0:nk], reads=kvbf_rl, writes=[ps_res[6]])
            P.tr(pst[0:64, 512:512 + nk], kvbf[0:nk, 256:320], ident[0:nk, 0:nk], reads=kvbf_rl, writes=[ps_res[6]])
            gi = slot % 4
            P.copy('dve', ckvT[:, :, gi * 128:gi * 128 + nk], pst[:, 0:256].rearrange("p (k c) -> p k c", k=2)[:, :, 0:nk],
                   reads=[ps_res[6]], writes=[ckvT_res[gi]])
            P.copy('dve', sbKT[:, :, c0:c0 + nk], pst[:, 256:512].rearrange("p (k c) -> p k c", k=2)[:, :, 0:nk],
                   reads=[ps_res[6]], writes=[kv_res[slot]])
            P.copy('act', krT[:, c0:c0 + nk], pst[0:64, 512:512 + nk], reads=[ps_res[6]], writes=[kv_res[slot]])
            P.copy('pool', sbV[0:nk, slot, :], kvbf[0:nk, 576:832], reads=kvbf_rl, writes=[kv_res[slot]])

        def project_keys(slots, nk, kbanks=(0, 1, 2, 3), vbanks=(4, 5)):
            ncol = (len(slots) - 1) * 128 + nk
            g0 = (slots[0] % 4) * 128
            c0 = slots[0] * 128
            rd = [ckvT_res[s % 4] for s in slots] + [smallw_res]
            for h in range(4):
                b = kbanks[h % len(kbanks)]
                for kc in range(2):
                    P.mm(ps[b][:, 0:ncol], wuk[:, kc, h * 128:(h + 1) * 128], ckvT[:, kc, g0:g0 + ncol],
                         start=(kc == 0), stop=(kc == 1), reads=rd, writes=[ps_res[b]])
                P.copy('act' if h % 2 == 0 else 'dve', knT[:, h, c0:c0 + ncol], ps[b][:, 0:ncol], reads=[ps_res[b]],
                       writes=[kv_res[s] for s in slots])
            for i, s in enumerate(slots):
                nkk = 128 if i < len(slots) - 1 else nk
                b = vbanks[i % len(vbanks)]
                for kc in range(2):
                    P.mm(ps[b][0:nkk, :], ckvT[:, kc, (s % 4) * 128:(s % 4) * 128 + nkk], wuv[:, kc, :],
                         start=(kc == 0), stop=(kc == 1), reads=[ckvT_res[s % 4], smallw_res], writes=[ps_res[b]])
                P.copy('dve' if i % 2 == 0 else 'act', vp[0:nkk, s, :, 0:128], ps[b][0:nkk, :].rearrange("p (h d) -> p h d", h=4),
                       reads=[ps_res[b]], writes=[kv_res[s]])

        def run_pass(kind, seq_i):
            prompt = (kind == "p")
            nt = 128 if prompt else NS
            NT = NTP if prompt else 1
            G = GP if prompt else 1
            NG = NT // G
            gq = G * nt
            tcs, tsn, tcF, tsF = tab[kind]
            STR = Streamer(make_specs(NG))
            if prompt:
                for t in range(NT):
                    P.load(x_t[:, t, :], xp[seq_i, t * 128:(t + 1) * 128, :], f"xin{t % 8}", writes=[x_res[t]])
            else:
                P.load(tcos[0:nt, 0, :], tcs, "tab", writes=[tab_res])
                P.load(tsin[0:nt, 0, :], tsn, "tab", writes=[tab_res])
                P.load(x_t[0:nt, 0, :], xs, "xin", writes=[x_res[0]])

            ln2_pending = []
            for l in range(NL):
                P.memset('pool', vp[:, :, :, 128:130], 1.0, writes=kv_res + x1T_rl + accsb_res + hT_res)
                P.mark(f"{kind}{seq_i} L{l} start")
                if l == 0:
                    load_layer_small(0, nt)
                load_bup(l)
                if not ln2_pending:
                    load_ln(l, 1)
                P.mark(f"{kind}{seq_i} L{l} small loaded")
                new_slot0 = 0 if prompt else 8

                for g in range(NG):
                    tiles = [g * G + i for i in range(G)]
                    if prompt:
                        P.load(tcosF[:, 0:gq], tcF[:, g * gq:(g + 1) * gq], "tabF", writes=[tabF_res])
                        P.load(tsinF[:, 0:gq], tsF[:, g * gq:(g + 1) * gq], "tabF", writes=[tabF_res])
                        P.load(tcos[:, 0:G, :], tcs[g * gq:(g + 1) * gq, :].rearrange("(t p) d -> p t d", p=128), "tab", writes=[tab_res])
                        P.load(tsin[:, 0:G, :], tsn[g * gq:(g + 1) * gq, :].rearrange("(t p) d -> p t d", p=128), "tab", writes=[tab_res])
                    else:
                        P.load(tcosF[:, 0:gq], tcF, "tabF", writes=[tabF_res])
                        P.load(tsinF[:, 0:gq], tsF, "tabF", writes=[tabF_res])
                    for i, t in enumerate(tiles):
                        make_xT(t, i, nt, xT, xT_res[i], i * 128)
                    pbank = [0]

                    def proj(pi):
                        c0, ncols = WIN_PIECES[pi]
                        wv, wr = STR.get((l, 'in', pi))
                        outs = []
                        for i, t in enumerate(tiles):
                            b = pbank[0] % 4; pbank[0] += 1
                            for k in range(8):
                                P.mm(ps[b][0:nt, 0:ncols], xT[:, k, i * 128:i * 128 + nt], wv[:, k, :],
                                     start=(k == 0), stop=(k == 7), reads=[xT_res[i], wr], writes=[ps_res[b]])
                            outs.append((ps[b][0:nt, 0:ncols], ps_res[b]))
                        STR.release()
                        return outs

                    def cons(pi, i, t, pv, pr):
                        slot = (new_slot0 + t) if prompt else 8
                        sti = i % 2
                        if pi == 0:
                            P.act(u_bf[0:nt, i, :], pv, AF.Gelu_apprx_tanh, reads=[pr], writes=[u_res[i]])
                            yield "evac"
                        elif pi == 1:
                            gv, gvr = f32buf()
                            yield
                            gv = gv[0:nt, 0:256]
                            yield
                            P.act(gv, pv, AF.Gelu_apprx_tanh, reads=[pr], writes=[gvr])
                            yield
                            gv3 = gv.rearrange("p (g d) -> p g d", g=4)
                            yield
                            sq, sqr = f32buf()
                            yield
                            sq = sq[0:nt, 0:256]
                            yield
                            P.op('dve', lambda e, gv3=gv3: e.tensor_reduce(stat[0:nt, 16:20], gv3, AX.X, ALU.add), reads=[gvr], writes=[stat_res])
                            yield
                            P.act(sq, gv, AF.Square, reads=[gvr], writes=[sqr])
                            yield
                            P.op('dve', lambda e, sq=sq: e.tensor_reduce(stat[0:nt, 20:24], sq.rearrange("p (g d) -> p g d", g=4), AX.X, ALU.add),
                                 reads=[sqr], writes=[stat_res])
                            yield
                            P.ts('dve', stat[0:nt, 16:20], stat[0:nt, 16:20], 1.0 / 64, None, ALU.mult, reads=[stat_res], writes=[stat_res])
                            yield
                            P.tt('dve', stat[0:nt, 24:28], stat[0:nt, 16:20], stat[0:nt, 16:20], ALU.mult, reads=[stat_res], writes=[stat_res])
                            yield
                            P.stt(stat[0:nt, 20:24], stat[0:nt, 20:24], 1.0 / 64, stat[0:nt, 24:28], ALU.mult, ALU.subtract,
                                  reads=[stat_res], writes=[stat_res])
                            yield
                            P.act(stat[0:nt, 20:24], stat[0:nt, 20:24], AF.Ln, bias=EPS, reads=[stat_res], writes=[stat_res])
                            yield
                            P.act(stat[0:nt, 20:24], stat[0:nt, 20:24], AF.Exp, scale=-0.5, reads=[stat_res], writes=[stat_res])
                            yield
                            P.tt('dve', gv3, gv3, stat[0:nt, 16:20].unsqueeze(2).to_broadcast([nt, 4, 64]), ALU.subtract,
                                 reads=[gvr, stat_res], writes=[gvr])
                            yield
                            P.tt('dve', gv3, gv3, stat[0:nt, 20:24].unsqueeze(2).to_broadcast([nt, 4, 64]), ALU.mult,
                                 reads=[gvr, stat_res], writes=[gvr])
                            yield
                            v_bf, vbf_res = bfbuf()
                            yield
                            P.copy('pool', v_bf[0:nt, :], gv, reads=[gvr], writes=[vbf_res])
                            yield
                            if not prompt:
                                P.load(o_gv[l], gv, "ogv", reads=[gvr])
                            yield
                            yield "defer"
                            for gg in range(4):
                                P.mm(ps[5][0:nt, gg * 64:(gg + 1) * 64], wsT[0:nt, gg, 0:nt], v_bf[0:nt, gg * 64:(gg + 1) * 64],
                                     reads=[vbf_res, smallw_res], writes=[ps_res[5]])
                            yield
                            ya, yar = f32buf()
                            yield
                            ya = ya[0:nt, 0:256]
                            yield
                            for gg in range(4):
                                P.stt(ya[:, gg * 64:(gg + 1) * 64], ps[5][0:nt, gg * 64:(gg + 1) * 64], bsb[0:nt, gg:gg + 1],
                                      u_bf[0:nt, i, gg * 64:(gg + 1) * 64], ALU.add, ALU.mult,
                                      reads=[ps_res[5], gains_res, u_res[i]], writes=[yar])
                            yield
                            sq3, sq3r = bfbuf()
                            P.act(sq3[0:nt, 0:256], ya, AF.Square, accum_out=stat[0:nt, 28:29], reads=[yar], writes=[sq3r, stat_res])
                            yield
                            rstd_from_ss(28, 29, 256, nt)
                            yield
                            P.ts('dve', y_t[0:nt, i, 0:256], ya, stat[0:nt, 29:30], None, ALU.mult,
                                 reads=[yar, stat_res], writes=[y_res[i]])
                            yield
                        elif pi == 2:
                            sq, sqr = f32buf()
                            yield
                            P.copy('dve', cqraw[0:nt, i, 0:256], pv, reads=[pr], writes=[cqraw_res])
                            yield "evac"
                            P.act(sq[0:nt, 0:256], cqraw[0:nt, i, 0:256], AF.Square, accum_out=stat[0:nt, 48 + 2 * i:49 + 2 * i], reads=[cqraw_res], writes=[sqr, stat_res])
                            yield
                        elif pi == 3:
                            sq, sqr = f32buf()
                            yield
                            P.copy('dve', cqraw[0:nt, i, 256:384], pv, reads=[pr], writes=[cqraw_res])
                            yield "evac"
                            P.act(sq[0:nt, 0:128], cqraw[0:nt, i, 256:384], AF.Square, accum_out=stat[0:nt, 49 + 2 * i:50 + 2 * i], reads=[cqraw_res], writes=[sqr, stat_res])
                            yield
                            P.tt('dve', stat[0:nt, 32:33], stat[0:nt, 48 + 2 * i:49 + 2 * i], stat[0:nt, 49 + 2 * i:50 + 2 * i], ALU.add, reads=[stat_res], writes=[stat_res])
                            yield
                            rstd_from_ss(32, 33, 384, nt)
                            yield
                            cqa, cqar = bfbuf()
                            cqb, cqbr = bfbuf()
                            P.ts('dve', cqa[0:nt, 0:256], cqraw[0:nt, i, 0:256], stat[0:nt, 33:34], None, ALU.mult,
                                 reads=[cqraw_res, stat_res], writes=[cqar])
                            P.ts('dve', cqb[0:nt, 0:128], cqraw[0:nt, i, 256:384], stat[0:nt, 33:34], None, ALU.mult,
                                 reads=[cqraw_res, stat_res], writes=[cqbr])
                            yield
                            yield "defer"
                            pst = ps[4][:].bitcast(BF16)
                            yield
                            for kc in range(3):
                                src_ = cqa[0:nt, kc * 128:(kc + 1) * 128] if kc < 2 else cqb[0:nt, 0:128]
                                P.tr(pst[:, kc * 128:kc * 128 + nt], src_, ident[0:nt, 0:nt],
                                     reads=[cqar if kc < 2 else cqbr, const_res], writes=[ps_res[4]])
                            yield
                            P.copy('act', cqT[:, :, i * 128:i * 128 + nt], pst[:, 0:384].rearrange("p (k c) -> p k c", k=3)[:, :, 0:nt],
                                   reads=[ps_res[4]], writes=[cqT_res[i]])
                            yield
                        elif pi == 4:
                            sq, sqr = f32buf()
                            yield
                            raw, rawr = f32buf()
                            yield
                            P.copy('dve', raw[0:nt, 0:256], pv, reads=[pr], writes=[rawr])
                            yield
                            P.act(sq[0:nt, 0:256], raw[0:nt, 0:256], AF.Square, accum_out=stat[0:nt, 34:35], reads=[rawr], writes=[sqr, stat_res])
                            yield
                            rstd_from_ss(34, 35, 256, nt)
                            yield
                            P.stt(kvst[0:nt, sti, 0:256], raw[0:nt, 0:256], stat[0:nt, 35:36], g_ckv[0:nt, :], ALU.mult, ALU.mult,
                                  reads=[rawr, stat_res, gains_res], writes=[kvst_res[sti]])
                            yield
                        elif pi == 5:
                            tt_ = i
                            yield
                            cs_ = tcos[0:nt, tt_, :]; sn_ = tsin[0:nt, tt_, :]
                            yield
                            x1 = pv[:, 0:32]; x2 = pv[:, 32:64]
                            yield
                            P.tt('dve', rtmp[0:nt, 0, :], x1, cs_, ALU.mult, reads=[pr, tab_res], writes=[rtmp_res])
                            yield
                            P.tt('dve', rtmp[0:nt, 1, :], x2, sn_, ALU.mult, reads=[pr, tab_res], writes=[rtmp_res])
                            yield
                            P.tt('dve', rtmp[0:nt, 2, :], x2, cs_, ALU.mult, reads=[pr, tab_res], writes=[rtmp_res])
                            yield
                            P.tt('dve', rtmp[0:nt, 3, :], x1, sn_, ALU.mult, reads=[pr, tab_res], writes=[rtmp_res])
                            yield
                            P.tt('pool', kvst[0:nt, sti, 256:288], rtmp[0:nt, 0, :], rtmp[0:nt, 1, :], ALU.subtract,
                                 reads=[rtmp_res], writes=[kvst_res[sti]])
                            yield
                            P.tt('pool', kvst[0:nt, sti, 288:320], rtmp[0:nt, 2, :], rtmp[0:nt, 3, :], ALU.add,
                                 reads=[rtmp_res], writes=[kvst_res[sti]])
                            yield
                        elif pi == 6:
                            sbqb, sbqb_res = bfbuf()
                            yield
                            P.act(sbqb[0:nt, :], pv, AF.Copy, scale=SB_SCALE, reads=[pr], writes=[sbqb_res])
                            yield "evac"
                            yield "defer"
                            pst = ps[4][:].bitcast(BF16)
                            yield
                            for hp in range(2):
                                P.tr(pst[:, 512 + hp * 128:512 + hp * 128 + nt], sbqb[0:nt, hp * 128:(hp + 1) * 128], ident[0:nt, 0:nt],
                                     reads=[sbqb_res, const_res], writes=[ps_res[4]])
                            yield
                            P.copy('dve', sbQT[:, :, i * 128:i * 128 + nt], pst[:, 512:768].rearrange("p (k c) -> p k c", k=2)[:, :, 0:nt],
                                   reads=[ps_res[4]], writes=[sbq_res[i]])
                            yield
                        elif pi == 7:
                            P.copy('act', kvst[0:nt, sti, 320:576], pv, reads=[pr], writes=[kvst_res[sti]])
                            yield "evac"
                        elif pi == 8:
                            P.copy('dve', kvst[0:nt, sti, 576:832], pv, reads=[pr], writes=[kvst_res[sti]])
                            yield "evac"
                            if prompt:
                                rows = slice(t * 128, (t + 1) * 128)
                                outs = [o[l, seq_i, rows, :] for o in o_p]
                            else:
                                outs = [o[l] for o in o_s]
                            yield
                            for oo, (a0, a1) in zip(outs, [(0, 256), (256, 320), (320, 576), (576, 832)]):
                                P.load(oo, kvst[0:nt, sti, a0:a1], f"okv{sti}", reads=[kvst_res[sti]])
                            yield
                            yield "defer"
                            ingest_kv(slot, nt, sti)
                            yield

                    pending = []
                    nxt = proj(0)
                    for pi in range(len(WIN_PIECES)):
                        cur = nxt
                        if pi + 1 < len(WIN_PIECES):
                            nxt = proj(pi + 1)
                        newg = [cons(pi, i, t, cur[i][0], cur[i][1]) for i, t in enumerate(tiles)]
                        if pi in (0, 2, 3, 6, 7, 8):
                            for g_ in newg:
                                for r_ in g_:
                                    if r_ == "evac":
                                        break
                        active = pending + newg
                        pending = []
                        for g_ in active:
                            for r_ in g_:
                                if r_ == "defer":
                                    pending.append(g_)
                                    break
                        for _ in range(2):
                            if ln2_pending:
                                layer_norm_tile(ln2_pending.pop(0), nt, l - 1)
                    for g_ in pending:
                        for r_ in g_:
                            pass
                    if g == 0 and l > 0 and prompt:
                        while ln2_pending:
                            layer_norm_tile(ln2_pending.pop(0), nt, l - 1)
                        load_ln(l, 1)
                    P.mark(f"{kind}{seq_i} L{l} g{g} phaseB")
                    if prompt:
                        project_keys([new_slot0 + t for t in tiles], 128)
                    else:
                        project_keys([8], nt, (0, 1), (0, 1))
                    rdq = cqT_res[0:G] + [smallw_res]
                    for h in range(4):
                        b = h % 2
                        for kc in range(3):
                            P.mm(ps[b][:, 0:gq], wuq[:, kc, h * 192:h * 192 + 128], cqT[:, kc, 0:gq],
                                 start=(kc == 0), stop=(kc == 2), reads=rdq, writes=[ps_res[b]])
                        P.act(qnT[:, h, 0:gq], ps[b][:, 0:gq], AF.Copy, scale=MLA_SCALE, reads=[ps_res[b]], writes=[q_res])
                        for kc in range(3):
                            P.mm(ps[2][0:64, 0:gq], wuq[:, kc, h * 192 + 128:h * 192 + 192], cqT[:, kc, 0:gq],
                                 start=(kc == 0), stop=(kc == 2), reads=rdq, writes=[ps_res[2]])
                        for kc in range(3):
                            P.mm(ps[3][0:64, 0:gq], wrot[:, kc, h, :], cqT[:, kc, 0:gq],
                                 start=(kc == 0), stop=(kc == 2), reads=rdq, writes=[ps_res[3]])
                        t1, t1r = f32buf(); t2, t2r = f32buf()
                        P.tt('dve', t1[0:64, 0:gq], ps[2][0:64, 0:gq], tcosF[:, 0:gq], ALU.mult, reads=[ps_res[2], tabF_res], writes=[t1r])
                        P.tt('dve', t2[0:64, 0:gq], ps[3][0:64, 0:gq], tsinF[:, 0:gq], ALU.mult, reads=[ps_res[3], tabF_res], writes=[t2r])
                        P.tt('pool', qrT[:, h, 0:gq], t1[0:64, 0:gq], t2[0:64, 0:gq], ALU.add, reads=[t1r, t2r], writes=[q_res])

                    P.mark(f"{kind}{seq_i} L{l} g{g} attention")
                    def key_list_prompt():
                        return [(kt, 128, (kt - g * G) if kt >= g * G else -1) for kt in range(g * G + G)]

                    def mla_attend(hp, keys, first, last, accbs, sbanks=(0, 1), stages_only=False):
                        n = len(keys)
                        stt_ = [None] * n

                        def S1(k):
                            slot, nk, di = keys[k]
                            q0 = 0 if di < 0 else di * 128
                            ncol = gq - q0
                            c0 = slot * 128
                            b = sbanks[cnt["mla"] % len(sbanks)]; cnt["mla"] += 1
                            for hh in range(2):
                                h = 2 * hp + hh
                                P.mm(ps[b][0:nk, hh * gq:hh * gq + ncol], knT[:, h, c0:c0 + nk], qnT[:, h, q0:gq], start=True, stop=False,
                                     skip_group_check=True, reads=[kv_res[slot], q_res], writes=[ps_res[b]])
                                P.mm(ps[b][0:nk, hh * gq:hh * gq + ncol], krT[:, c0:c0 + nk], qrT[:, h, q0:gq], start=False, stop=True,
                                     skip_group_check=True, reads=[kv_res[slot], q_res], writes=[ps_res[b]])
                            pT, pTr = bfpair_mla()
                            pin = ps[b][0:nk, 0:2 * gq].rearrange("p (h c) -> p h c", h=2)[:, :, 0:ncol]
                            P.act(pT[0:nk, :, 0:ncol], pin, AF.Exp, reads=[ps_res[b]], writes=[pTr])
                            if di >= 0 and prompt:
                                P.memset('pool', pT[64:128, :, 0:64], 0.0, writes=[pTr])
                            stt_[k] = (pT, pTr, q0)

                        def S2(k):
                            slot, nk, di = keys[k]
                            pT, pTr, q0 = stt_[k]
                            for hh in range(2):
                                h = 2 * hp + hh
                                for qb in range(q0 // 128, G):
                                    ab = accbs[hh] if prompt else accbs[0]
                                    col = (qb % 2) * 129 if prompt else (h % 2) * 129
                                    st_flag = first[0].get(ab, True)
                                    first[0][ab] = False
                                    nq = nt
                                    P.mm(ps[ab][0:nq, col:col + 129], pT[0:nk, hh, qb * 128 - q0:qb * 128 - q0 + nq], vp[0:nk, slot, h, 0:129],
                                         start=st_flag, stop=False, skip_group_check=True, reads=[pTr, kv_res[slot]], writes=[ps_res[ab]])

                        def fin():
                            mla_final(hp, accbs)

                        if stages_only:
                            return n, S1, S2, fin
                        for step in range(n + 1):
                            if step < n:
                                S1(step)
                            if step >= 1:
                                S2(step - 1)
                        if last:
                            fin()

                    def mla_final(hp, accbs):
                        if True:
                            for hh in range(2):
                                h = 2 * hp + hh
                                for qb in range(G):
                                    ab = accbs[hh] if prompt else accbs[0]
                                    col = (qb % 2) * 129 if prompt else (h % 2) * 129
                                    sc = 40 + (cnt["mla"] % 2); cnt["mla"] += 1
                                    P.op('dve', lambda e, ab=ab, col=col, sc=sc: e.reciprocal(stat[0:nt, sc:sc + 1], ps[ab][0:nt, col + 128:col + 129]),
                                         reads=[ps_res[ab]], writes=[stat_res])
                                    yb, ybr = mla_out[qb]
                                    P.act(yb[0:nt, h * 128:(h + 1) * 128], ps[ab][0:nt, col:col + 128], AF.Copy, scale=stat[0:nt, sc:sc + 1],
                                          reads=[ps_res[ab], stat_res], writes=[ybr])

                    def sb_attend(hp, keys, first, banks, stages_only=False):
                        z1b, z2b, pob = banks
                        n = len(keys)
                        stt_ = [None] * n
                        ares = [accsb_res[hp], accsb_res[hp + 2]]

                        def geom(k):
                            slot, nk, di = keys[k]
                            q0 = 0 if di < 0 else di * 128
                            return slot, nk, di, q0, gq - q0, slot * 128

                        def kq(hh, c0, nk, q0):
                            base = 64 * hp
                            return sbKT[base:base + 64, hh, c0:c0 + nk], sbQT[base:base + 64, hh, q0:gq]

                        def S1(k):
                            slot, nk, di, q0, ncol, c0 = geom(k)
                            j = cnt["sb"]; cnt["sb"] += 1
                            b1 = z1b[j % len(z1b)]
                            for hh in range(2):
                                kT, qT = kq(hh, c0, nk, q0)
                                P.mm(ps[b1][0:nk, hh * gq:hh * gq + ncol], kT, qT, skip_group_check=True,
                                     reads=[kv_res[slot]] + sbq_res[0:G], writes=[ps_res[b1]])
                            e_, er = f32buf()
                            ev = e_[:, 0:2 * gq].rearrange("p (h c) -> p h c", h=2)[0:nk, :, 0:ncol]
                            zin = ps[b1][0:nk, 0:2 * gq].rearrange("p (h c) -> p h c", h=2)[:, :, 0:ncol]
                            P.act(ev, zin, AF.Exp, reads=[ps_res[b1]], writes=[er])
                            sp, spr = bfpair()
                            P.act(sp[0:nk, :, 0:ncol], ev, AF.Ln, bias=1.0, reads=[er], writes=[spr])
                            if di >= 0:
                                P.tt('pool', sp[0:nk, :, 0:nt], sp[0:nk, :, 0:nt], maskSB[0:nk, 0:nt].unsqueeze(1).to_broadcast([nk, 2, nt]), ALU.mult,
                                     reads=[spr, const_res], writes=[spr])
                            stt_[k] = dict(sp=sp, spr=spr, j=j)

                        def S2(k):
                            slot, nk, di, q0, ncol, c0 = geom(k)
                            d_ = stt_[k]
                            b2 = z2b[d_["j"] % len(z2b)]
                            for hh in range(2):
                                kT, qT = kq(hh, c0, nk, q0)
                                P.mm(ps[b2][0:nk, hh * gq:hh * gq + ncol], kT, qT, start=True, stop=False, skip_group_check=True,
                                     reads=[kv_res[slot]] + sbq_res[0:G], writes=[ps_res[b2]])
                                P.mm(ps[b2][0:nk, hh * gq:hh * gq + ncol], negU[0:nk, 0:nk], d_["sp"][0:nk, hh, 0:ncol], start=False, stop=True,
                                     skip_group_check=True, reads=[d_["spr"], const_res], writes=[ps_res[b2]])
                            wT, wTr = bfpair()
                            win = ps[b2][0:nk, 0:2 * gq].rearrange("p (h c) -> p h c", h=2)[:, :, 0:ncol]
                            P.act(wT[0:nk, :, 0:ncol], win, AF.Exp, reads=[ps_res[b2]], writes=[wTr])
                            if di >= 0:
                                P.tt('pool', wT[0:nk, :, 0:nt], wT[0:nk, :, 0:nt], maskSB[0:nk, 0:nt].unsqueeze(1).to_broadcast([nk, 2, nt]), ALU.mult,
                                     reads=[wTr, const_res], writes=[wTr])
                            d_["wT"] = wT; d_["wTr"] = wTr

                        def S3(k):
                            slot, nk, di, q0, ncol, c0 = geom(k)
                            d_ = stt_[k]
                            j = d_["j"]
                            b3 = pob[j % len(pob)]
                            sp, spr, wT, wTr = d_["sp"], d_["spr"], d_["wT"], d_["wTr"]
                            qb0 = q0 // 128
                            po = ps[b3][:, 0:G * 2 * 66].rearrange("p (q h c) -> p q h c", q=G, h=2)
                            for hh in range(2):
                                h = hp + 2 * hh
                                for qb in range(qb0, G):
                                    cc = qb * 128 - q0
                                    P.mm(po[0:nt, qb, hh, 0:64], wT[0:nk, hh, cc:cc + nt], sbV[0:nk, slot, h * 64:(h + 1) * 64],
                                         skip_group_check=True, reads=[wTr, kv_res[slot]], writes=[ps_res[b3]])
                                    P.mm(po[0:nt, qb, hh, 64:66], sp[0:nk, hh, cc:cc + nt], ones1[0:nk, 0:2],
                                         skip_group_check=True, reads=[spr, const_res], writes=[ps_res[b3]])
                            acc = accsb[0:nt, qb0:G, hp:4:2, :]
                            if first[0]:
                                P.copy('dve', acc, po[0:nt, qb0:G, :, 0:64], reads=[ps_res[b3]], writes=ares)
                            else:
                                fx = fexp[0:nt, j % 2, :].rearrange("p (q h) -> p q h", h=2)[:, qb0:G, :]
                                P.act(fx, po[0:nt, qb0:G, :, 64], AF.Exp, scale=-1.0, reads=[ps_res[b3]], writes=[fexp_res[j % 2]])
                                P.tt('dve', acc, acc, fx.unsqueeze(3).to_broadcast([nt, G - qb0, 2, 64]), ALU.mult,
                                     reads=ares + [fexp_res[j % 2]], writes=ares)
                                P.tt('dve', acc, acc, po[0:nt, qb0:G, :, 0:64], ALU.add, reads=ares + [ps_res[b3]], writes=ares)
                            first[0] = False
                            stt_[k] = None

                        if stages_only:
                            return n, S1, S2, S3
                        for step in range(n + 2):
                            if 1 <= step <= n:
                                S2(step - 1)
                            if step >= 2:
                                S3(step - 2)
                            if step < n:
                                S1(step)

                    mla_out = []
                    cnt["f32"] = 0
                    for qb in range(G):
                        yb, ybr = f32buf()
                        mla_out.append((yb, ybr))

                    def finish_mla():
                        for qb in range(G):
                            yb, ybr = mla_out[qb]
                            P.act(xb[0:nt, 0:512], yb[0:nt, :], AF.Square, accum_out=stat[0:nt, 42:43], reads=[ybr], writes=[xb_res, stat_res])
                            rstd_from_ss(42, 43, 512, nt)
                            P.ts('dve', y_t[0:nt, qb, 256:768], yb[0:nt, :], stat[0:nt, 43:44], None, ALU.mult,
                                 reads=[ybr, stat_res], writes=[y_res[qb]])

                    def finish_sb():
                        for qb in range(G):
                            yc = accsb[0:nt, qb, :, :]
                            sq2, sq2r = bfbuf()
                            P.act(sq2[0:nt, 0:256].rearrange("p (h d) -> p h d", h=4), yc, AF.Square, accum_out=stat[0:nt, 44:45],
                                  reads=accsb_res, writes=[sq2r, stat_res])
                            rstd_from_ss(44, 45, 256, nt)
                            P.ts('dve', y_t[0:nt, qb, 768:1024].rearrange("p (h d) -> p h d", h=4), yc, stat[0:nt, 45:46], None, ALU.mult,
                                 reads=accsb_res + [stat_res], writes=[y_res[qb]])

                    if prompt:
                        keys = key_list_prompt()
                        cnt["f32fix"] = 2
                        cnt["mla_pool"] = True
                        for hp_ in range(2):
                            n_, M1, M2, Mfin = mla_attend(hp_, keys, [dict()], True, (1, 2), (0,), stages_only=True)
                            n2_, B1, B2, B3 = sb_attend(hp_, keys, [True], ((3,), (5, 6), (4, 7)), stages_only=True)
                            for step in range(n_ + 2):
                                if 1 <= step <= n_:
                                    B2(step - 1)
                                if step >= 2:
                                    B3(step - 2)
                                if 1 <= step <= n_:
                                    M2(step - 1)
                                if step < n_:
                                    B1(step)
                                    M1(step)
                            Mfin()
                        cnt["f32fix"] = None
                        cnt["mla_pool"] = False
                        finish_mla()
                        finish_sb()
                    else:
                        firsts_m = [[dict()]] * 4
                        firsts_s = [[True] for _ in range(4)]
                        npg = PASTL // 512

                        def ingest_group(kg):
                            slots = [(kg % 2) * 4 + i for i in range(4)]
                            for i, s_ in enumerate(slots):
                                r0 = kg * 512 + i * 128
                                sti = i % 2
                                P.load(kvst[:, sti, 0:256], c_ckv[l, r0:r0 + 128, :], f"cin{sti}", writes=[kvst_res[sti]])
                                P.load(kvst[:, sti, 256:320], c_kr[l, r0:r0 + 128, :], f"cin{sti}", writes=[kvst_res[sti]])
                                P.load(kvst[:, sti, 320:576], c_k[l, r0:r0 + 128, :], f"cin{sti}", writes=[kvst_res[sti]])
                                P.load(kvst[:, sti, 576:832], c_v[l, r0:r0 + 128, :], f"cin{sti}", writes=[kvst_res[sti]])
                                ingest_kv(s_, 128, sti)
                            project_keys(slots, 128, (0, 1), (0, 1))
                            return slots

                        nxt_slots = ingest_group(0)
                        for kg in range(npg):
                            slots = nxt_slots
                            if kg + 1 < npg:
                                nxt_slots = ingest_group(kg + 1)
                            keys = [(s_, 128, -1) for s_ in slots]
                            for hp_ in range(2):
                                n_, M1, M2, Mfin = mla_attend(hp_, keys, firsts_m[hp_], False, (4 + hp_,), (2,), stages_only=True)
                                n2_, B1, B2, B3 = sb_attend(hp_, keys, firsts_s[hp_], ((3,), (6,), (7,)), stages_only=True)
                                for step in range(n_ + 2):
                                    if 1 <= step <= n_:
                                        B2(step - 1)
                                    if step >= 2:
                                        B3(step - 2)
                                    if 1 <= step <= n_:
                                        M2(step - 1)
                                    if step < n_:
                                        B1(step)
                                        M1(step)
                        keys = [(8, nt, 0)]
                        for hp_ in range(2):
                            mla_attend(hp_, keys, firsts_m[hp_], True, (4 + hp_,), (2,))
                        finish_mla()
                        for hp_ in range(2):
                            sb_attend(hp_, keys, firsts_s[hp_], ((3,), (6,), (7,)))
                        finish_sb()

                    P.mark(f"{kind}{seq_i} L{l} g{g} phaseD")
                    yTs = [(yT[:, 0, :, :], yT_res[0]), (xb[:, :].rearrange("p (k c) -> p k c", k=8), xb_res)]
                    for i, t in enumerate(tiles):
                        yv, yr = yTs[i % 2]
                        pst = ps[6 + (i % 2)][:].bitcast(BF16)
                        for k in range(8):
                            P.tr(pst[:, k * 128:k * 128 + nt], y_t[0:nt, i, k * 128:(k + 1) * 128], ident[0:nt, 0:nt],
                                 reads=[y_res[i], const_res], writes=[ps_res[6 + (i % 2)]])
                        P.copy('act' if i % 2 == 0 else 'dve', yv[:, :, 0:nt], pst[:, :].rearrange("p (k c) -> p k c", k=8)[:, :, 0:nt],
                               reads=[ps_res[6 + (i % 2)]], writes=[yr])
                    for c in range(4):
                        wv, wr = STR.get((l, 'out', c))
                        for i, t in enumerate(tiles):
                            yv, yr = yTs[i % 2]
                            b = (c * G + i) % 4
                            for k in range(8):
                                P.mm(ps[b][0:nt, 0:256], yv[:, k, 0:nt], wv[:, k, :], start=(k == 0), stop=(k == 7),
                                     reads=[yr, wr], writes=[ps_res[b]])
                            xv = x_t[0:nt, t, c * 256:(c + 1) * 256]
                            P.stt(xv, xv, ALPHA, ps[b][0:nt, 0:256], ALU.mult, ALU.add, reads=[x_res[t], ps_res[b]], writes=[x_res[t]])
                        STR.release()
                    for i, t in enumerate(tiles):
                        layer_norm_tile(t, nt, l)

                P.mark(f"{kind}{seq_i} L{l} MLP")
                P.memset('pool', stat[:, 61:62], 0.0, writes=kv_res + x1T_rl + accsb_res + hT_res)
                P.load(lnb[:, 0, :], W["b_down"][l:l + 1, :].to_broadcast([128, 1024]), "lnb0", writes=[lnb_res[0]])
                x1T_lo = knT
                x1T_hi = vp[:].rearrange("p a b c -> p (a b c)")[:, 0:4 * KS * 128].rearrange("p (k c) -> p k c", k=4)

                class X1:
                    pass

                for t in range(NT):
                    P.copy('act', xb[0:nt, :], x_t[0:nt, t, :], reads=[x_res[t]], writes=[xb_res])
                    pst = ps[7][:].bitcast(BF16)
                    for k in range(8):
                        P.tr(pst[:, k * 128:k * 128 + nt], xb[0:nt, k * 128:(k + 1) * 128], ident[0:nt, 0:nt],
                             reads=[xb_res, const_res], writes=[ps_res[7]])
                    P.copy('dve', x1T_lo[:, :, t * 128:t * 128 + nt], pst[:, 0:512].rearrange("p (k c) -> p k c", k=4)[:, :, 0:nt],
                           reads=[ps_res[7]], writes=[x1T_rl[(t // 4) % 4]])
                    P.copy('dve', x1T_hi[:, :, t * 128:t * 128 + nt], pst[:, 512:1024].rearrange("p (k c) -> p k c", k=4)[:, :, 0:nt],
                           reads=[ps_res[7]], writes=[x1T_rl[(t // 4) % 4]])
                    xv = x_t[0:nt, t, :]
                    P.stt(xv, xv, ALPHA, lnb[0:nt, 0, :], ALU.mult, ALU.add, reads=[x_res[t], lnb_res[0]], writes=[x_res[t]])
                load_ln(l, 2)
                if l + 1 < NL:
                    load_layer_small(l + 1, nt)

                def x1T_ap(k, c0, n):
                    return (x1T_lo if k < 4 else x1T_hi)[:, k % 4, c0:c0 + n]

                for e8 in range(8):
                    wup = [STR.get((l, 'up', e8, hh)) for hh in range(2)]
                    wdn = [STR.get((l, 'dn', e8, hh)) for hh in range(2)]
                    MG = min(4, NT)
                    mq = MG * nt
                    for g in range(NT // MG):
                        c0 = g * MG * 128
                        for fc in range(4):
                            b = fc % 2
                            wv, wr = wup[fc // 2]
                            for k in range(8):
                                P.mm(ps[b][:, 0:mq], wv[:, k, (fc % 2) * 128:(fc % 2) * 128 + 128], x1T_ap(k, c0, mq),
                                     start=(k == 0), stop=(k == 7), reads=[x1T_rl[g % 4], wr], writes=[ps_res[b]])
                            r_, rr = f32buf()
                            f_idx = e8 * 4 + fc
                            P.act(r_[:, 0:mq], ps[b][:, 0:mq], AF.Relu, bias=bup[:, f_idx:f_idx + 1], reads=[ps_res[b], bup_res], writes=[rr])
                            P.act(hT[:, fc, 0:mq], r_[:, 0:mq], AF.Square, reads=[rr], writes=[hT_res[fc]])
                        if g == NT // MG - 1:
                            STR.release(2)
                        for i in range(MG):
                            t = g * MG + i
                            for nh in range(2):
                                b = 2 + (i * 2 + nh) % 4
                                for fc in range(4):
                                    wv, wr = wdn[fc // 2]
                                    P.mm(ps[b][0:nt, :], hT[:, fc, i * 128:i * 128 + nt], wv[:, fc % 2, nh * 512:(nh + 1) * 512],
                                         start=(fc == 0), stop=(fc == 3), reads=[hT_res[fc], wr], writes=[ps_res[b]])
                                xv = x_t[0:nt, t, nh * 512:(nh + 1) * 512]
                                P.tt('dve', xv, xv, ps[b][0:nt, :], ALU.add, reads=[x_res[t], ps_res[b]], writes=[x_res[t]])
                    STR.release(2)
                for t in range(NT):
                    if l < NL - 1 and prompt and t >= G:
                        ln2_pending.append(t)
                        continue
                    layer_norm_tile(t, nt, l)
                    if l == NL - 1:
                        if prompt:
                            P.load(yp[seq_i, t * 128:(t + 1) * 128, :], x_t[:, t, :], "yout", reads=[x_res[t]])
                        else:
                            P.load(ys, x_t[0:nt, 0, :], "yout", reads=[x_res[0]])

        for s_i in range(nseq):
            run_pass("p", s_i)
        if do_sample:
            run_pass("s", 0)
        if os.environ.get("K_MARKS"):
            for m in P.marks:
                print("MARK", m)
            print("TOTAL OPS", P.nrec, {e: len(P.ops[e]) for e in P.ENG})
        P.emit(st)
    return nc


def rope_tables(pos):
    half = 32
    inv = (np.float32(10000.0) ** (-np.arange(half, dtype=np.float32) / np.float32(half))).astype(np.float32)
    ang = pos.astype(np.float32)[:, None] * inv[None, :]
    cos = np.cos(ang).astype(np.float32)
    sin = np.sin(ang).astype(np.float32)
    cosF = np.concatenate([cos, cos], 1).T * np.float32(MLA_SCALE)
    sinF = np.concatenate([sin, sin], 1).T * np.float32(MLA_SCALE)
    return cos, sin, np.ascontiguousarray(cosF.astype(np.float32)), np.ascontiguousarray(sinF.astype(np.float32))


_CACHE = {}


def kernel(**inputs):
    x_prompt = np.asarray(inputs["x_prompt"], np.float32)
    x_sample = np.asarray(inputs["x_sample"], np.float32)
    B, S, _ = x_prompt.shape
    NL = inputs["w_in"].shape[0]
    PASTL = inputs["cache_mla_ckv"].shape[2]
    nseq = B // N_CORES
    key = (S, NL, PASTL, nseq)
    if key not in _CACHE:
        _CACHE[key] = build_program(S, NL, PASTL, nseq)
    nc = _CACHE[key]
    tp = rope_tables(np.arange(S))
    tsm = rope_tables(PASTL + np.arange(NS))
    shared = {}
    for k in WNAMES:
        a = np.ascontiguousarray(np.asarray(inputs[k], np.float32))
        if k == "w_uq":
            a = a.reshape(NL, 384, 768)
        shared[k] = a
    for nm, arr in zip(["tp_cos", "tp_sin", "tp_cosF", "tp_sinF"], tp):
        shared[nm] = arr
    for nm, arr in zip(["ts_cos", "ts_sin", "ts_cosF", "ts_sinF"], tsm):
        shared[nm] = arr
    in_maps = []
    for c in range(N_CORES):
        m = dict(shared)
        m["xp"] = np.ascontiguousarray(x_prompt[c * nseq:(c + 1) * nseq])
        m["xs"] = np.ascontiguousarray(x_sample[c])
        m["c_ckv"] = np.ascontiguousarray(np.asarray(inputs["cache_mla_ckv"], np.float32)[:, c])
        m["c_kr"] = np.ascontiguousarray(np.asarray(inputs["cache_mla_krope"], np.float32)[:, c])
        m["c_k"] = np.ascontiguousarray(np.asarray(inputs["cache_sb_k"], np.float32)[:, c].reshape(NL, PASTL, 256))
        m["c_v"] = np.ascontiguousarray(np.asarray(inputs["cache_sb_v"], np.float32)[:, c].reshape(NL, PASTL, 256))
        in_maps.append(m)
    res = run_bass_kernel_spmd(nc, in_maps, core_ids=list(range(N_CORES)))
    R = res.results
    y_p = np.concatenate([r["yp"] for r in R], 0)
    y_s = np.stack([r["ys"] for r in R], 0)
    ckv_p = np.concatenate([r["o_ckv_p"] for r in R], 1)
    kr_p = np.concatenate([r["o_kr_p"] for r in R], 1)
    k_p = np.concatenate([r["o_k_p"] for r in R], 1).reshape(NL, B, S, 4, 64)
    v_p = np.concatenate([r["o_v_p"] for r in R], 1).reshape(NL, B, S, 4, 64)
    ckv_s = np.stack([r["o_ckv_s"] for r in R], 1)
    kr_s = np.stack([r["o_kr_s"] for r in R], 1)
    k_s = np.stack([r["o_k_s"] for r in R], 1).reshape(NL, len(R), NS, 4, 64)
    v_s = np.stack([r["o_v_s"] for r in R], 1).reshape(NL, len(R), NS, 4, 64)
    gv_s = np.stack([r["o_gv_s"] for r in R], 1).reshape(NL, len(R), NS, 4, 64)
    return (y_p, y_s, ckv_p, kr_p, k_p, v_p, ckv_s, kr_s, k_s, v_s, gv_s)
```
